# Optimizing a Trainium2 kernel written in Bass

```python
import jax
import jax.numpy as jnp
from jax import lax
import numpy as np

D_MODEL = 1024
BATCH = 8
SEQ = 4096
DEPTH = 4

CTX_LEN = 256
GRID_W = 64
EPS = 1e-6

D_CONV = 512
CONV_WIDTH = 31
GLA_HEADS = 4
GLA_DK = 64
GLA_DV = 128
GLA_QK = GLA_HEADS * GLA_DK
GLA_V = GLA_HEADS * GLA_DV
GLA_RANK = 16
GLA_GATE_NORM = 16.0
GLA_CHUNK = 64
ATT_HEADS = 8
ATT_KV_HEADS = 2
ATT_GROUP = ATT_HEADS // ATT_KV_HEADS
ATT_HD = 64
ATT_Q = ATT_HEADS * ATT_HD
ATT_KV = ATT_KV_HEADS * ATT_HD
ATT_BLOCK = 128
ROPE_AXIS_DIM = ATT_HD // 2
ROPE_THETA = 10000.0

IN_SPLITS = (D_CONV, D_CONV, D_CONV,
             GLA_QK, GLA_QK, GLA_V, GLA_V, GLA_RANK, GLA_RANK,
             ATT_Q, ATT_KV, ATT_KV, ATT_Q,
             D_MODEL, D_MODEL, D_MODEL)
N_IN = sum(IN_SPLITS)

kernel_name = 'hybrid_flow_block'


def rms_norm(x, g):
    xf = x.astype(jnp.float32)
    y = xf * lax.rsqrt(jnp.mean(xf * xf, axis=-1, keepdims=True) + EPS)
    return (y * g.astype(jnp.float32)).astype(x.dtype)


def layer_norm(x, g, b):
    xf = x.astype(jnp.float32)
    mu = jnp.mean(xf, axis=-1, keepdims=True)
    var = jnp.mean(jnp.square(xf - mu), axis=-1, keepdims=True)
    y = (xf - mu) * lax.rsqrt(var + EPS)
    return (y * g.astype(jnp.float32) + b.astype(jnp.float32)).astype(x.dtype)


def split_cols(p):
    idx = np.cumsum(np.array(IN_SPLITS))[:-1].tolist()
    return jnp.split(p, idx, axis=-1)


def conv_module(val, glu, gate, dw_w, dw_b, ln_g, ln_b, w_o):
    u = val * jax.nn.sigmoid(glu)
    pad = CONV_WIDTH // 2
    u = lax.conv_general_dilated(u, dw_w[:, None, :], window_strides=(1,),
                                 padding=[(pad, pad)],
                                 dimension_numbers=('NWC', 'WIO', 'NWC'),
                                 feature_group_count=D_CONV) + dw_b
    u = jax.nn.silu(layer_norm(u, ln_g, ln_b)) * jax.nn.silu(gate)
    return u @ w_o


def gla_scan(q, k, v, log_a, s0):
    b_, t_, h_, _ = q.shape
    dv = v.shape[-1]
    n = t_ // GLA_CHUNK

    def chunks(z):
        return z.astype(jnp.float32).reshape(b_, n, GLA_CHUNK, h_, z.shape[-1])

    q, k, v, log_a = chunks(q), chunks(k), chunks(v), chunks(log_a)
    cum = jnp.cumsum(log_a, axis=2)
    last = cum[:, :, -1]
    q_in = q * jnp.exp(cum)
    k_in = k * jnp.exp(-cum)
    mask = jnp.tril(jnp.ones((GLA_CHUNK, GLA_CHUNK), dtype=bool))
    att = jnp.where(mask, jnp.einsum('bnihd,bnjhd->bnhij', q_in, k_in), 0.0)
    o_intra = jnp.einsum('bnhij,bnjhe->bnihe', att, v)
    k_state = k * jnp.exp(last[:, :, None] - cum)
    contrib = jnp.einsum('bnjhd,bnjhe->bnhde', k_state, v)
    decay = jnp.exp(last)

    def step(s, inp):
        dec, con = inp
        return dec[..., None] * s + con, s

    s_fin, s_prev = lax.scan(step, s0, (jnp.swapaxes(decay, 0, 1), jnp.swapaxes(contrib, 0, 1)))
    o_inter = jnp.einsum('bnihd,bnhde->bnihe', q_in, jnp.swapaxes(s_prev, 0, 1))
    return (o_intra + o_inter).reshape(b_, t_, h_, dv), s_fin


def gla_heads(q, k, v):
    b_, t_, _ = q.shape
    return (q.reshape(b_, t_, GLA_HEADS, GLA_DK) * (GLA_DK ** -0.5),
            k.reshape(b_, t_, GLA_HEADS, GLA_DK),
            v.reshape(b_, t_, GLA_HEADS, GLA_DV))


def gla_log_gate(lr, w_up, b_up):
    b_, t_, _ = lr.shape
    z = (lr @ w_up + b_up).astype(jnp.float32)
    return (jax.nn.log_sigmoid(z) / GLA_GATE_NORM).reshape(b_, t_, GLA_HEADS, GLA_DK)


def gla_output(o, gate, norm_g, w_o):
    b_, t_ = o.shape[:2]
    o = rms_norm(o, norm_g).reshape(b_, t_, GLA_V).astype(gate.dtype)
    return (o * jax.nn.silu(gate)) @ w_o


def gla_branch(xp, cp, w_up, b_up, norm_g, w_o, need_ctx):
    xq, xk, xv, xg, xlf, xlb = xp
    cq, ck, cv, cg, clf, clb = cp
    xq, xk, xv = gla_heads(xq, xk, xv)
    cq, ck, cv = gla_heads(cq, ck, cv)
    s0 = jnp.zeros((xq.shape[0], GLA_HEADS, GLA_DK, GLA_DV), jnp.float32)

    def flip(z):
        return jnp.flip(z, axis=1)

    oc_f, sc_f = gla_scan(cq, ck, cv, gla_log_gate(clf, w_up[0], b_up[0]), s0)
    ox_f, _ = gla_scan(xq, xk, xv, gla_log_gate(xlf, w_up[0], b_up[0]), sc_f)
    oc_b, sc_b = gla_scan(flip(cq), flip(ck), flip(cv),
                          flip(gla_log_gate(clb, w_up[1], b_up[1])), s0)
    ox_b, _ = gla_scan(flip(xq), flip(xk), flip(xv),
                       flip(gla_log_gate(xlb, w_up[1], b_up[1])), sc_b)
    y_x = gla_output(ox_f + flip(ox_b), xg, norm_g, w_o)
    y_c = gla_output(oc_f + flip(oc_b), cg, norm_g, w_o) if need_ctx else None
    return y_x, y_c


def rope_tables(t_):
    n_rows = t_ // GRID_W
    row = jnp.repeat(jnp.arange(n_rows, dtype=jnp.float32), GRID_W)
    col = jnp.tile(jnp.arange(GRID_W, dtype=jnp.float32), n_rows)
    n_freq = ROPE_AXIS_DIM // 2
    freqs = ROPE_THETA ** (-jnp.arange(n_freq, dtype=jnp.float32) / n_freq)
    ang_r = row[:, None] * freqs
    ang_c = col[:, None] * freqs
    return jnp.cos(ang_r), jnp.sin(ang_r), jnp.cos(ang_c), jnp.sin(ang_c)


def rotate_half(x, cos, sin):
    x1, x2 = jnp.split(x, 2, axis=-1)
    cos = cos[:, None, :]
    sin = sin[:, None, :]
    return jnp.concatenate([x1 * cos - x2 * sin, x2 * cos + x1 * sin], axis=-1)


def rope_2d(x, tabs):
    cr, sr, cc, sc = tabs
    xr, xcol = jnp.split(x.astype(jnp.float32), 2, axis=-1)
    return jnp.concatenate([rotate_half(xr, cr, sr), rotate_half(xcol, cc, sc)], axis=-1).astype(x.dtype)


def attn_heads(q, k, v, qn_g, kn_g, tabs):
    b_, t_, _ = q.shape
    q = rms_norm(q.reshape(b_, t_, ATT_HEADS, ATT_HD), qn_g)
    k = rms_norm(k.reshape(b_, t_, ATT_KV_HEADS, ATT_HD), kn_g)
    if tabs is not None:
        q, k = rope_2d(q, tabs), rope_2d(k, tabs)
    return (q.reshape(b_, t_, ATT_KV_HEADS, ATT_GROUP, ATT_HD), k,
            v.reshape(b_, t_, ATT_KV_HEADS, ATT_HD))


def gqa_attend(q, k, v):
    s = jnp.einsum('bqkgd,bskd->bkgqs', q, k).astype(jnp.float32) * (ATT_HD ** -0.5)
    p = jax.nn.softmax(s, axis=-1).astype(v.dtype)
    return jnp.einsum('bkgqs,bskd->bqkgd', p, v)


def attn_branch(xp, cp, qn_g, kn_g, w_o, need_ctx):
    xq, xk, xv, xg = xp
    cq, ck, cv, cg = cp
    b_, t_, _ = xq.shape
    xq, xk, xv = attn_heads(xq, xk, xv, qn_g, kn_g, rope_tables(t_))
    cq, ck, cv = attn_heads(cq, ck, cv, qn_g, kn_g, None)
    k_all = jnp.concatenate([ck, xk], axis=1)
    v_all = jnp.concatenate([cv, xv], axis=1)
    n_blk = t_ // ATT_BLOCK
    q_blk = jnp.swapaxes(xq.reshape(b_, n_blk, ATT_BLOCK, ATT_KV_HEADS, ATT_GROUP, ATT_HD), 0, 1)
    o = lax.map(lambda qb: gqa_attend(qb, k_all, v_all), q_blk)
    o = jnp.swapaxes(o, 0, 1).reshape(b_, t_, ATT_Q)
    y_x = (o * jax.nn.silu(xg)) @ w_o
    y_c = None
    if need_ctx:
        oc = gqa_attend(cq, ck, cv).reshape(b_, cq.shape[1], ATT_Q)
        y_c = (oc * jax.nn.silu(cg)) @ w_o
    return y_x, y_c


def hybrid_layer(xs, cs, c, c_ctx, norm_g, w_mod, b_mod, w_in, b_in, dw_w, dw_b, ln_g, ln_b,
                 w_conv_out, gla_w_gate, gla_b_gate, gla_norm_g, w_gla_out, q_norm_g, k_norm_g,
                 w_attn_out, w_out, need_ctx):
    shift_x, scale_x, gate_x = jnp.split(jax.nn.silu(c) @ w_mod + b_mod, 3, axis=-1)
    shift_c, scale_c, gate_c = jnp.split(jax.nn.silu(c_ctx) @ w_mod + b_mod, 3, axis=-1)
    hx = rms_norm(xs, norm_g) * (1 + scale_x[:, None]) + shift_x[:, None]
    hc = rms_norm(cs, norm_g) * (1 + scale_c) + shift_c
    px = split_cols(hx @ w_in + b_in)
    pc = split_cols(hc @ w_in + b_in)

    ya_x = conv_module(*px[0:3], dw_w, dw_b, ln_g, ln_b, w_conv_out)
    yb_x, yb_c = gla_branch(px[3:9], pc[3:9], gla_w_gate, gla_b_gate, gla_norm_g, w_gla_out, need_ctx)
    yc_x, yc_c = attn_branch(px[9:13], pc[9:13], q_norm_g, k_norm_g, w_attn_out, need_ctx)

    def merge(p, ya, yb, yc):
        return (jax.nn.sigmoid(p[13]) * ya + jax.nn.sigmoid(p[14]) * yb
                + jax.nn.sigmoid(p[15]) * yc) @ w_out

    xs = xs + gate_x[:, None] * merge(px, ya_x, yb_x, yc_x)
    if need_ctx:
        ya_c = conv_module(*pc[0:3], dw_w, dw_b, ln_g, ln_b, w_conv_out)
        cs = cs + gate_c * merge(pc, ya_c, yb_c, yc_c)
    return xs, cs


def setup_inputs(seed: int = 0) -> dict:
    key = jax.random.key(seed)
    ks = jax.random.split(key, 22)

    def nrm(k, shape, s):
        return jax.random.normal(k, shape, jnp.float32) * s

    return {
        'x': nrm(ks[0], (BATCH, SEQ, D_MODEL), 1.0),
        'c': nrm(ks[1], (BATCH, D_MODEL), 1.0),
        'ctx': nrm(ks[2], (BATCH, CTX_LEN, D_MODEL), 1.0),
        'c_ctx': nrm(ks[3], (D_MODEL,), 1.0),
        'norm_g': 1.0 + nrm(ks[4], (DEPTH, D_MODEL), 0.02),
        'w_mod': nrm(ks[5], (DEPTH, D_MODEL, 3 * D_MODEL), 0.5 * D_MODEL ** -0.5),
        'b_mod': nrm(ks[6], (DEPTH, 3 * D_MODEL), 0.01),
        'w_in': nrm(ks[7], (DEPTH, D_MODEL, N_IN), D_MODEL ** -0.5),
        'b_in': nrm(ks[8], (DEPTH, N_IN), 0.01),
        'conv_dw_w': nrm(ks[9], (DEPTH, CONV_WIDTH, D_CONV), CONV_WIDTH ** -0.5),
        'conv_dw_b': nrm(ks[10], (DEPTH, D_CONV), 0.01),
        'conv_ln_g': 1.0 + nrm(ks[11], (DEPTH, D_CONV), 0.02),
        'conv_ln_b': nrm(ks[12], (DEPTH, D_CONV), 0.01),
        'w_conv_out': nrm(ks[13], (DEPTH, D_CONV, D_MODEL), D_CONV ** -0.5),
        'gla_w_gate': nrm(ks[14], (DEPTH, 2, GLA_RANK, GLA_QK), GLA_RANK ** -0.5),
        'gla_b_gate': nrm(ks[15], (DEPTH, 2, GLA_QK), 0.1),
        'gla_norm_g': 1.0 + nrm(ks[16], (DEPTH, GLA_DV), 0.02),
        'w_gla_out': nrm(ks[17], (DEPTH, GLA_V, D_MODEL), GLA_V ** -0.5),
        'q_norm_g': 1.0 + nrm(ks[18], (DEPTH, ATT_HD), 0.02),
        'k_norm_g': 1.0 + nrm(ks[19], (DEPTH, ATT_HD), 0.02),
        'w_attn_out': nrm(ks[20], (DEPTH, ATT_Q, D_MODEL), ATT_Q ** -0.5),
        'w_out': nrm(ks[21], (DEPTH, D_MODEL, D_MODEL), D_MODEL ** -0.5),
    }


def reference(x, c, ctx, c_ctx, norm_g, w_mod, b_mod, w_in, b_in, conv_dw_w, conv_dw_b,
              conv_ln_g, conv_ln_b, w_conv_out, gla_w_gate, gla_b_gate, gla_norm_g, w_gla_out,
              q_norm_g, k_norm_g, w_attn_out, w_out):
    xs, cs = x, ctx
    for l in range(DEPTH):
        xs, cs = hybrid_layer(xs, cs, c, c_ctx, norm_g[l], w_mod[l], b_mod[l], w_in[l], b_in[l],
                              conv_dw_w[l], conv_dw_b[l], conv_ln_g[l], conv_ln_b[l], w_conv_out[l],
                              gla_w_gate[l], gla_b_gate[l], gla_norm_g[l], w_gla_out[l],
                              q_norm_g[l], k_norm_g[l], w_attn_out[l], w_out[l],
                              need_ctx=(l < DEPTH - 1))
    return xs
```

```python
import contextlib
import numpy as np
import ml_dtypes
import concourse.bass as bass
import concourse.mybir as mybir
from concourse.bass_utils import run_bass_kernel_spmd

F32 = mybir.dt.float32
BF16 = mybir.dt.bfloat16
AF = mybir.ActivationFunctionType
ALU = mybir.AluOpType
AX = mybir.AxisListType

D = 1024
NIN = 7456
EPS = 1e-6


class Buf:
    __slots__ = ("name", "w", "r", "rd")

    def __init__(self, name=""):
        self.name = name
        self.w = None
        self.r = {}
        self.rd = []


class Op:
    __slots__ = ("eng", "fn", "deps", "needs_inc", "val", "dma", "dsem", "dval", "idx")

    def __init__(self, eng, fn, dma, idx):
        self.eng = eng
        self.fn = fn
        self.dma = dma
        self.deps = []
        self.needs_inc = False
        self.val = None
        self.dsem = None
        self.dval = None
        self.idx = idx


class Prog:
    ENGS = ("pe", "act", "dve", "pool", "sp")
    NDMA = 56

    def __init__(self, nc):
        self.nc = nc
        self.ops = {e: [] for e in self.ENGS}
        self.all = []
        self.dma_since_barrier = []

    def add(self, eng, fn, r=(), w=(), dma=False):
        op = Op(eng, fn, dma, len(self.all))
        edeps = {}
        ddeps = {}

        def dep(x):
            if x is None or x is op:
                return
            if x.dma:
                ddeps[x.idx] = x
            else:
                if x.eng == "pe" and eng == "pe" and not dma:
                    return
                o = edeps.get(x.eng)
                if o is None or o.idx < x.idx:
                    edeps[x.eng] = x

        for b in r:
            dep(b.w)
        for b in w:
            dep(b.w)
            for x in b.r.values():
                dep(x)
            for x in b.rd:
                dep(x)
        for b in w:
            b.w = op
            b.r = {}
            b.rd = []
        for b in r:
            if b.w is not op:
                if dma:
                    b.rd.append(op)
                else:
                    b.r[eng] = op
        op.deps = list(edeps.values()) + list(ddeps.values())
        self.ops[eng].append(op)
        self.all.append(op)
        if dma:
            self.dma_since_barrier.append(op)
        return op

    def dma(self, q, out, in_, r=(), w=(), **kw):
        return self.add(q, lambda e: e.dma_start(out=out, in_=in_, **kw), r=r, w=w, dma=True)

    def barrier(self):
        last = {}
        for e in self.ENGS:
            last[e] = None
            for o in reversed(self.ops[e]):
                if not o.dma and o.fn is not None:
                    last[e] = o
                    break
        dmas = list(self.dma_since_barrier)
        self.dma_since_barrier = []
        for e in self.ENGS:
            op = Op(e, None, False, len(self.all))
            for e2 in self.ENGS:
                if e2 != e and last[e2] is not None:
                    op.deps.append(last[e2])
            op.deps.extend(dmas)
            self.ops[e].append(op)
            self.all.append(op)

    def emit(self, final_wait_ops=()):
        nc = self.nc
        fin = Op("sp", None, False, len(self.all))
        fin.deps = list(final_wait_ops)
        self.ops["sp"].append(fin)
        self.all.append(fin)
        sem_last = [None] * self.NDMA
        sem_cnt = [0] * self.NDMA
        k = 0
        for op in self.all:
            if op.dma:
                s = k % self.NDMA
                k += 1
                if sem_last[s] is not None:
                    op.deps.append(sem_last[s])
                sem_last[s] = op
                sem_cnt[s] += 16
                op.dsem = s
                op.dval = sem_cnt[s]
        for op in self.all:
            for d in op.deps:
                d.needs_inc = True
        cnt = {e: 0 for e in self.ENGS}
        for op in self.all:
            if not op.dma and op.fn is not None:
                if op.needs_inc:
                    cnt[op.eng] += 1
                op.val = cnt[op.eng]
        self.stats = {e: [len(self.ops[e]), cnt[e], 0] for e in self.ENGS}
        with contextlib.ExitStack() as st:
            esem = {e: st.enter_context(nc.semaphore("s_" + e)) for e in self.ENGS}
            dsem = [st.enter_context(nc.semaphore("d%d" % i)) for i in range(self.NDMA)]
            block = st.enter_context(nc.Block())

            def run(ename):
                def body(e):
                    waited = {}
                    nwait = 0
                    for op in self.ops[ename]:
                        for d in op.deps:
                            if d.dma:
                                key, v, sem = ("d", d.dsem), d.dval, dsem[d.dsem]
                            else:
                                if d.fn is None:
                                    continue
                                key, v, sem = ("e", d.eng), d.val, esem[d.eng]
                            if waited.get(key, 0) >= v:
                                continue
                            waited[key] = v
                            e.wait_ge(sem, v)
                            nwait += 1
                        if op.fn is None:
                            continue
                        ins = op.fn(e)
                        if op.dma:
                            ins.then_inc(dsem[op.dsem], 16)
                        elif op.needs_inc:
                            ins.then_inc(esem[ename], 1)
                    self.stats[ename][2] = nwait

                return body

            block.tensor(run("pe"))
            block.scalar(run("act"))
            block.vector(run("dve"))
            block.gpsimd(run("pool"))
            block.sync(run("sp"))


def host_consts(T):
    c = {}
    c["ident_bf"] = np.eye(128, dtype=np.float32).astype(ml_dtypes.bfloat16)
    c["ident_f"] = np.eye(128, dtype=np.float32)
    tp = np.arange(128)[:, None]
    t = np.arange(128)[None, :]
    mle = (tp <= t).astype(np.float32)
    mge = (tp >= t).astype(np.float32)
    s = -1.0 / 16.0
    masks = np.stack([s * mle, s * (1 - mle), s * mge, s * (1 - mge), mle, mge], axis=1)
    c["masks"] = np.ascontiguousarray(masks.astype(np.float32))
    c["ones_f"] = np.ones((128, 128), np.float32)
    c["neg16"] = np.full((128, 2), s, np.float32)
    n_rows = T // 64
    row = np.repeat(np.arange(n_rows, dtype=np.float32), 64)
    col = np.tile(np.arange(64, dtype=np.float32), n_rows)
    freqs = (np.float32(10000.0) ** (-np.arange(16, dtype=np.float32) / np.float32(16))).astype(np.float32)
    ar = (row[:, None] * freqs).astype(np.float32)
    ac = (col[:, None] * freqs).astype(np.float32)
    cr, sr, cc, sc = np.cos(ar), np.sin(ar), np.cos(ac), np.sin(ac)
    c["rope_cos"] = np.concatenate([cr, cr, cc, cc], axis=1).astype(np.float32)
    c["rope_sin"] = np.concatenate([-sr, sr, -sc, sc], axis=1).astype(np.float32)
    return c


CONST_SPECS = [("ident_bf", [128, 128], BF16), ("ident_f", [128, 128], F32), ("masks", [128, 6, 128], F32),
               ("ones_f", [128, 128], F32), ("neg16", [128, 2], F32)]

WEIGHT_SPECS = [("norm_g", [D]), ("w_mod", [D, 3 * D]), ("b_mod", [3 * D]), ("w_in", [D, NIN]), ("b_in", [NIN]),
                ("conv_dw_w", [31, 512]), ("conv_dw_b", [512]), ("conv_ln_g", [512]), ("conv_ln_b", [512]),
                ("w_conv_out", [512, D]), ("gla_w_gate", [2, 16, 256]), ("gla_b_gate", [2, 256]),
                ("gla_norm_g", [128]), ("w_gla_out", [512, D]), ("q_norm_g", [64]), ("k_norm_g", [64]),
                ("w_attn_out", [512, D]), ("w_out", [D, D])]


class _Stop(Exception):
    pass


def build(T, TC, DEPTH, debug=False, stop=None):
    try:
        return _build(T, TC, DEPTH, debug, stop)
    except _Stop as e:
        nc, P = e.args
        P.emit([])
        return nc, P


def _build(T, TC, DEPTH, debug=False, stop=None):
    NT = TC + T
    NTILE = NT // 128
    TCT = TC // 128
    chunks = [(0, TC)] + [(TC + 512 * i, 512) for i in range(T // 512)]

    nc = bass.Bass("TRN2", target_bir_lowering=False)
    P = Prog(nc)

    def din(name, shape, dt=F32):
        return nc.dram_tensor(name, shape, dt, kind="ExternalInput").ap()

    x_d = din("x", [T, D])
    ctx_d = din("ctx", [TC, D])
    c_d = din("c", [1, D])
    cctx_d = din("c_ctx", [1, D])
    Wd = {n: din(n, [DEPTH] + s) for n, s in WEIGHT_SPECS}
    Cd = {n: din(n, s, dt) for n, s, dt in CONST_SPECS}
    cos_d = din("rope_cos", [T, 64])
    sin_d = din("rope_sin", [T, 64])
    out_d = nc.dram_tensor("out", [T, D], F32, kind="ExternalOutput").ap()

    skind = "ExternalOutput" if debug else "Internal"

    def dscr(name, shape, dt):
        return nc.dram_tensor(name, shape, dt, kind=skind).ap()

    XS = dscr("XS", [NT, D], F32)
    MOD = dscr("MOD", [DEPTH, 2, 3 * D], F32)
    NPX = 5120
    PXF = dscr("PXF", [NPX, NT], BF16)
    QT = dscr("QT", [512, NT], BF16)
    GQK = dscr("GQK", [NT, 512], F32)
    GV = dscr("GV", [NT, 512], BF16)
    GZ = dscr("GZ", [NT, 512], F32)
    OBF = [dscr("OB", [512, NT], F32), dscr("OF", [512, NT], F32)]
    U_, CG_, GG_, AG_, M13_ = 0, 512, 1024, 1536, 2048
    bXS, bMOD, bPXF, bQT, bGQK, bGV, bGZ = [Buf(n) for n in ("XS", "MOD", "PXF", "QT", "GQK", "GV", "GZ")]
    bOBF = [Buf("OB"), Buf("OF")]
    bOUT = Buf("OUT")

    SB_LO, SB_HI = 16512, 229344
    state = {"persist": SB_LO, "top": SB_LO, "n": 0}

    def alloc(name, shape, dt):
        esz = 2 if dt == BF16 else 4
        nbytes = esz
        for s_ in shape[1:]:
            nbytes *= s_
        nbytes = (nbytes + 63) // 64 * 64
        off = state["top"]
        state["top"] += nbytes
        assert state["top"] <= SB_HI, ("SBUF overflow", name, state["top"])
        state["n"] += 1
        return nc.alloc_sbuf_tensor_at("%s_%d" % (name, state["n"]), shape, dt, offset=off)

    def arena_reset():
        state["top"] = state["persist"]

    PS = [nc.alloc_psum_tensor("ps%d" % i, [128, 512], F32) for i in range(8)]
    bPS = [Buf("ps%d" % i) for i in range(8)]

    def mm(out, lhsT, rhs, start, stop, r, w):
        return P.add("pe", lambda e: e.matmul(out, lhsT=lhsT, rhs=rhs, start=start, stop=stop), r=r, w=w)

    def tr(out, in_, ident, r, w):
        return P.add("pe", lambda e: e.transpose(out=out, in_=in_, identity=ident), r=r, w=w)

    def act(out, in_, func, r, w, bias=None, scale=None, accum=None):
        kw = {}
        if bias is not None:
            kw["bias"] = bias
        if scale is not None:
            kw["scale"] = scale
        if accum is not None:
            kw["accum_out"] = accum
        return P.add("act", lambda e: e.activation(out=out, in_=in_, func=func, **kw), r=r, w=w)

    def tt(eng, out, in0, in1, op, r, w):
        return P.add(eng, lambda e: e.tensor_tensor(out=out, in0=in0, in1=in1, op=op), r=r, w=w)

    def ts(eng, out, in0, s1, s2, op0, op1, r, w):
        return P.add(eng, lambda e: e.tensor_scalar(out=out, in0=in0, scalar1=s1, scalar2=s2, op0=op0, op1=op1), r=r, w=w)

    def stt(out, in0, scalar, in1, op0, op1, r, w):
        return P.add("dve", lambda e: e.scalar_tensor_tensor(out=out, in0=in0, scalar=scalar, in1=in1, op0=op0, op1=op1), r=r, w=w)

    def cp(eng, out, in_, r, w):
        if eng == "act":
            return P.add("act", lambda e: e.copy(out=out, in_=in_), r=r, w=w)
        return P.add(eng, lambda e: e.tensor_copy(out=out, in_=in_), r=r, w=w)

    def memset(eng, ap, v, w):
        return P.add(eng, lambda e: e.memset(ap, v), w=w)

    def recip(out, in_, r, w):
        return P.add("dve", lambda e: e.reciprocal(out=out, in_=in_), r=r, w=w)

    def kview(w2d):
        return w2d.rearrange("(kb p) n -> p kb n", p=128)

    ident_bf = alloc("ident_bf", [128, 128], BF16)
    ident_f = alloc("ident_f", [128, 128], F32)
    masks = alloc("masks", [128, 6, 128], F32)
    ones_f = alloc("ones_f", [128, 128], F32)
    neg16 = alloc("neg16", [128, 2], F32)
    eps_t = alloc("eps", [128, 1], F32)
    bCONST = Buf("const")
    for n, t_ in (("ident_bf", ident_bf), ("ident_f", ident_f), ("masks", masks), ("ones_f", ones_f), ("neg16", neg16)):
        P.dma("sp", t_[:], Cd[n], w=[bCONST])
    memset("dve", eps_t[:], EPS, [bCONST])
    KT = alloc("KT", [128, NT], BF16)
    Vext = alloc("Vext", [128, NTILE, 2, 65], BF16)
    bKT = [Buf("KT%d" % i) for i in range(NTILE)]
    bV = [Buf("V%d" % i) for i in range(NTILE)]
    memset("pool", Vext[:, :, :, 64:65], 1.0, bV)
    state["persist"] = state["top"]

    arena_reset()
    cc = alloc("cc", [128, 2, 8], F32)
    sc = alloc("sc", [128, 2, 8], F32)
    bm = alloc("bm", [2, 3 * D], F32)
    ng = alloc("ng", [2, D], F32)
    modsb = alloc("modsb", [2, 3 * D], F32)
    wm = [alloc("wm%d" % i, [128, 8, 512], F32) for i in range(2)]
    bcc, bsc, bbm, bng, bmodsb = Buf(), Buf(), Buf(), Buf(), Buf()
    bwm = [Buf(), Buf()]
    P.dma("sp", cc[:, 0, :], c_d.rearrange("o (kb p) -> p (o kb)", p=128), w=[bcc], allow_slow_non_contiguous=True)
    P.dma("sp", cc[:, 1, :], cctx_d.rearrange("o (kb p) -> p (o kb)", p=128), w=[bcc], allow_slow_non_contiguous=True)
    act(sc[:], cc[:], AF.Silu, [bcc], [bsc])
    k = 0
    for l in range(DEPTH):
        P.dma("sp", bm[:], Wd["b_mod"][l:l + 1, :].partition_broadcast(2), w=[bbm])
        P.dma("sp", ng[:], Wd["norm_g"][l:l + 1, :].partition_broadcast(2), w=[bng])
        for ch in range(6):
            wb_ = wm[k % 2]
            bw_ = bwm[k % 2]
            pb = k % 2
            k += 1
            P.dma("sp", wb_[:], kview(Wd["w_mod"][l][:, ch * 512:(ch + 1) * 512]), w=[bw_])
            for kb in range(8):
                mm(PS[pb][0:2, :], sc[:, :, kb], wb_[:, kb, :], kb == 0, kb == 7, [bsc, bw_], [bPS[pb]])
            tt("dve", modsb[:, ch * 512:(ch + 1) * 512], PS[pb][0:2, :], bm[:, ch * 512:(ch + 1) * 512], ALU.add,
               [bPS[pb], bbm], [bmodsb])
        stt(modsb[:, D:2 * D], modsb[:, D:2 * D], 1.0, ng[:], ALU.add, ALU.mult, [bmodsb, bng], [bmodsb])
        P.dma("sp", MOD[l], modsb[:], r=[bmodsb], w=[bMOD])

    if stop == '0':
        raise _Stop(nc, P)
    final_ops = []
    for l in range(DEPTH):
        need_ctx = l < DEPTH - 1
        last = l == DEPTH - 1
        w_in = Wd["w_in"][l]
        b_in = Wd["b_in"]

        def src_tile(t):
            if l == 0:
                return ctx_d[t * 128:(t + 1) * 128, :] if t < TCT else x_d[(t - TCT) * 128:(t - TCT + 1) * 128, :]
            return XS[t * 128:(t + 1) * 128, :]

        P.barrier()
        arena_reset()
        hxT = alloc("hxT", [128, 8, NT], BF16)
        mark_hx = state["top"]
        bhx = [Buf("hx%d" % i) for i in range(NTILE)]
        modA = alloc("modA", [128, 2, 2, D], F32)
        bmodA = Buf()
        for s_ in range(2):
            P.dma("sp", modA[:, s_, 0, :], MOD[l, s_:s_ + 1, D:2 * D].partition_broadcast(128), r=[bMOD], w=[bmodA])
            P.dma("sp", modA[:, s_, 1, :], MOD[l, s_:s_ + 1, 0:D].partition_broadcast(128), r=[bMOD], w=[bmodA])
        wlr = alloc("wlr", [128, 8, 32], BF16)
        blr = alloc("blr", [32, 1], F32)
        lrT = alloc("lrT", [33, NT], BF16)
        w_tm = alloc("w_tm", [128, 8, 1792], BF16)
        bias_tm = alloc("bias_tm", [128, 1792], F32)
        Gq = alloc("Gq", [128, 64], F32)
        Gk = alloc("Gk", [128, 64], F32)
        wupf = alloc("wupf", [33, 512], F32)
        wup = alloc("wup", [33, 512], BF16)
        bwlr, bblr, blrT, bwtm, bbtm, bG, bwupf, bwup = [Buf() for _ in range(8)]
        P.dma("pool", wlr[:], kview(w_in[:, 3072:3104]), w=[bwlr])
        P.dma("sp", blr[:], b_in[l:l + 1, 3072:3104].rearrange("o n -> n o"), w=[bblr], allow_slow_non_contiguous=True)
        memset("pool", lrT[32:33, :], 1.0, [blrT])
        tm_cols = [(1536, 2048, 0), (2048, 2560, 512), (3104, 3616, 1024), (3616, 3872, 1536)]
        for a, b, o in tm_cols:
            P.dma("pool", w_tm[:, :, o:o + (b - a)], kview(w_in[:, a:b]), w=[bwtm])
            P.dma("sp", bias_tm[:, o:o + (b - a)], b_in[l:l + 1, a:b].partition_broadcast(128), w=[bbtm])
        P.dma("sp", Gq[:], Wd["q_norm_g"][l:l + 1, :].partition_broadcast(128), w=[bG])
        P.dma("sp", Gk[:], Wd["k_norm_g"][l:l + 1, :].partition_broadcast(128), w=[bG])
        memset("dve", wupf[:], 0.0, [bwupf])
        P.dma("sp", wupf[0:16, 0:256], Wd["gla_w_gate"][l, 0], w=[bwupf])
        P.dma("sp", wupf[16:32, 256:512], Wd["gla_w_gate"][l, 1], w=[bwupf])
        P.dma("sp", wupf[32:33, 0:256], Wd["gla_b_gate"][l, 0:1, :], w=[bwupf])
        P.dma("sp", wupf[32:33, 256:512], Wd["gla_b_gate"][l, 1:2, :], w=[bwupf])
        cp("dve", wup[:], wupf[:], [bwupf], [bwup])

        xt = [alloc("xt%d" % i, [128, D], F32) for i in range(2)]
        bxt = [Buf(), Buf()]
        junk = alloc("junk", [128, D], F32)
        hb = [alloc("hb%d" % i, [128, D], BF16) for i in range(2)]
        bhb = [Buf(), Buf()]
        ssA = alloc("ssA", [128, 2], F32)
        bjunk, bss = Buf(), [Buf(), Buf()]
        PT = PS[7][:].bitcast(BF16)
        PTv = PT.rearrange("p (a b) -> p a b", a=8)
        for t in range(NTILE):
            i = t % 2
            s_ = 0 if t >= TCT else 1
            P.dma("sp", xt[i][:], src_tile(t), r=[bXS], w=[bxt[i]])
            act(junk[:], xt[i][:], AF.Square, [bxt[i]], [bjunk, bss[i]], accum=ssA[:, i:i + 1])
            act(ssA[:, i:i + 1], ssA[:, i:i + 1], AF.Sqrt, [bss[i], bCONST], [bss[i]], bias=eps_t[:], scale=1.0 / D)
            recip(ssA[:, i:i + 1], ssA[:, i:i + 1], [bss[i]], [bss[i]])
            stt(junk[:], xt[i][:], ssA[:, i:i + 1], modA[:, s_, 0, :], ALU.mult, ALU.mult, [bxt[i], bss[i], bmodA], [bjunk])
            tt("dve", hb[i][:], junk[:], modA[:, s_, 1, :], ALU.add, [bjunk, bmodA], [bhb[i]])
            for kb in range(8):
                tr(PTv[:, kb, :], hb[i][:, kb * 128:(kb + 1) * 128], ident_bf[:], [bhb[i], bCONST], [bPS[7]])
            cp("act", hxT[:, :, t * 128:(t + 1) * 128], PTv[:], [bPS[7]], [bhx[t]])

        if stop == 'A':
            raise _Stop(nc, P)
        for ci, (c0, cn) in enumerate(chunks):
            pb = ci % 2
            tl = list(range(c0 // 128, (c0 + cn) // 128))
            for kb in range(8):
                mm(PS[pb][0:32, :cn], wlr[:, kb, :], hxT[:, kb, c0:c0 + cn], kb == 0, kb == 7,
                   [bwlr] + [bhx[t] for t in tl], [bPS[pb]])
            act(lrT[0:32, c0:c0 + cn], PS[pb][0:32, :cn], AF.Identity, [bPS[pb], bblr], [blrT], bias=blr[:])

        if stop == 'B0':
            raise _Stop(nc, P)
        qk_sb = [alloc("qk_sb%d" % i, [128, 512], F32) for i in range(2)]
        v_sb = [alloc("v_sb%d" % i, [128, 512], BF16) for i in range(2)]
        la_e = alloc("la_e", [128, 512], F32)
        la_sb = [alloc("la_sb%d" % i, [128, 512], F32) for i in range(2)]
        qT_sb = [alloc("qT_sb%d" % i, [128, 4, 128], BF16) for i in range(2)]
        bqk_sb, bv_sb, bla_sb, bqT_sb = [[Buf(), Buf()] for _ in range(4)]
        bla_e = Buf()
        nq_b = alloc("nq_b", [128, 512], F32)
        nq_sq = alloc("nq_sq", [128, 512], F32)
        nq_g = alloc("nq_g", [128, 512], F32)
        nq_n = alloc("nq_n", [128, 512], F32)
        nq_a = alloc("nq_a", [128, 512], F32)
        nq_r = alloc("nq_r", [128, 512], F32)
        nq_ss = alloc("nq_ss", [128, 8], F32)
        nq_o = [alloc("nq_o%d" % i, [128, 512], BF16) for i in range(2)]
        nk_o = alloc("nk_o", [128, 128], BF16)
        bnq = [Buf() for _ in range(7)]
        bnq_o = [Buf(), Buf()]
        bnk_o = Buf()
        cs_t = [alloc("cs_t%d" % i, [128, 2, 64], F32) for i in range(2)]
        bcs = [Buf(), Buf()]

        def norm_rope(src, bsrc, nh, G, boff, rope_i, out, bout, perm=False):
            n = nh * 64
            b_, sq_, g_, n_, a_, r_, ss_ = [x[:, :n] for x in (nq_b, nq_sq, nq_g, nq_n, nq_a, nq_r)] + [nq_ss[:, :nh]]
            Bb, Bsq, Bg, Bn, Ba, Br, Bss = bnq

            def v3(ap):
                return ap.rearrange("p (h d) -> p h d", h=nh)

            if perm:
                def vin(ap):
                    return ap.rearrange("p (a j d) -> p a j d", a=2, j=4)
                vout = out.rearrange("p (j a d) -> p a j d", a=2, j=4)
                rsb = ss_.rearrange("p (a j) -> p a j", a=2).unsqueeze(3).broadcast_to([128, 2, 4, 64])
            else:
                vin = v3
                vout = v3(out)
                rsb = ss_.unsqueeze(2).broadcast_to([128, nh, 64])

            tt("dve", b_, src, bias_tm[:, boff:boff + n], ALU.add, [bsrc, bbtm], [Bb])
            act(sq_, b_, AF.Square, [Bb], [Bsq])
            P.add("dve", lambda e: e.tensor_reduce(out=ss_, in_=v3(sq_), axis=AX.X, op=ALU.add), r=[Bsq], w=[Bss])
            act(ss_, ss_, AF.Sqrt, [Bss, bCONST], [Bss], bias=eps_t[:], scale=1.0 / 64)
            recip(ss_, ss_, [Bss], [Bss])
            tt("dve", v3(g_), v3(b_), G[:].unsqueeze(1).broadcast_to([128, nh, 64]), ALU.mult, [Bb, bG], [Bg])
            if rope_i is None:
                tt("dve", vout, vin(g_), rsb, ALU.mult, [Bg, Bss], [bout])
                return
            tt("dve", v3(n_), v3(g_), ss_.unsqueeze(2).broadcast_to([128, nh, 64]), ALU.mult, [Bg, Bss], [Bn])
            cs = cs_t[rope_i]
            tt("dve", v3(a_), v3(n_), cs[:, 0, :].unsqueeze(1).broadcast_to([128, nh, 64]), ALU.mult, [Bn, bcs[rope_i]], [Ba])

            def v5(ap):
                return ap.rearrange("p (h a b c) -> p h a b c", h=nh, a=2, b=2)

            sn5 = cs[:, 1, :].rearrange("p (a b c) -> p a b c", a=2, b=2)
            for hf in range(2):
                tt("dve", v5(r_)[:, :, :, hf, :], v5(n_)[:, :, :, 1 - hf, :],
                   sn5[:, :, hf, :].unsqueeze(1).broadcast_to([128, nh, 2, 16]), ALU.mult, [Bn, bcs[rope_i]], [Br])
            tt("dve", vout, vin(a_), vin(r_), ALU.add, [Ba, Br], [bout])

        for t in range(NTILE):
            i = t % 2
            tc_ = slice(t * 128, (t + 1) * 128)
            is_x = t >= TCT
            if is_x:
                xr = (t - TCT) * 128
                P.dma("sp", cs_t[i][:, 0, :], cos_d[xr:xr + 128, :], w=[bcs[i]])
                P.dma("sp", cs_t[i][:, 1, :], sin_d[xr:xr + 128, :], w=[bcs[i]])
            for kb in range(8):
                mm(PS[0][:], hxT[:, kb, tc_], w_tm[:, kb, 0:512], kb == 0, kb == 7, [bhx[t], bwtm], [bPS[0]])
            tt("dve", qk_sb[i][:], PS[0][:], bias_tm[:, 0:512], ALU.add, [bPS[0], bbtm], [bqk_sb[i]])
            P.dma("sp", GQK[tc_, :], qk_sb[i][:], r=[bqk_sb[i]], w=[bGQK])
            for kb in range(8):
                mm(PS[1][:], hxT[:, kb, tc_], w_tm[:, kb, 512:1024], kb == 0, kb == 7, [bhx[t], bwtm], [bPS[1]])
            tt("dve", v_sb[i][:], PS[1][:], bias_tm[:, 512:1024], ALU.add, [bPS[1], bbtm], [bv_sb[i]])
            P.dma("sp", GV[tc_, :], v_sb[i][:], r=[bv_sb[i]], w=[bGV])
            if is_x or need_ctx:
                for kb in range(8):
                    mm(PS[2][:], hxT[:, kb, tc_], w_tm[:, kb, 1024:1536], kb == 0, kb == 7, [bhx[t], bwtm], [bPS[2]])
                norm_rope(PS[2][:], bPS[2], 8, Gq, 1024, i if is_x else None, nq_o[i][:], bnq_o[i], perm=True)
                for j in range(4):
                    tr(PTv[:, j, :], nq_o[i][:, j * 128:(j + 1) * 128], ident_bf[:], [bnq_o[i], bCONST], [bPS[7]])
                cp("act", qT_sb[i][:], PTv[:, 0:4, :], [bPS[7]], [bqT_sb[i]])
                P.dma("sp", QT.rearrange("(j p) n -> p j n", p=128)[:, :, tc_], qT_sb[i][:], r=[bqT_sb[i]], w=[bQT])
            for kb in range(8):
                mm(PS[3][:, 0:256], hxT[:, kb, tc_], w_tm[:, kb, 1536:1792], kb == 0, kb == 7, [bhx[t], bwtm], [bPS[3]])
            norm_rope(PS[3][:, 0:128], bPS[3], 2, Gk, 1536, i if is_x else None, nk_o[:], bnk_o)
            PT6 = PS[6][:].bitcast(BF16)
            tr(PT6[:, 0:128], nk_o[:], ident_bf[:], [bnk_o, bCONST], [bPS[6]])
            cp("act", KT[:, tc_], PT6[:, 0:128], [bPS[6]], [bKT[t]])
            tt("dve", Vext[:, t, :, 0:64], PS[3][:, 128:256].rearrange("p (h d) -> p h d", h=2),
               bias_tm[:, 1664:1792].rearrange("p (h d) -> p h d", h=2), ALU.add, [bPS[3], bbtm], [bV[t]])
            mm(PS[4][:], lrT[0:33, tc_], wup[0:33, :], True, True, [blrT, bwup], [bPS[4]])
            act(la_e[:], PS[4][:], AF.Exp, [bPS[4]], [bla_e], scale=-1.0)
            act(la_sb[i][:], la_e[:], AF.Ln, [bla_e], [bla_sb[i]], bias=1.0)
            P.dma("sp", GZ[tc_, :], la_sb[i][:], r=[bla_sb[i]], w=[bGZ])

        if stop == 'B1':
            raise _Stop(nc, P)
        P.barrier()
        state["top"] = mark_hx
        NG = 11
        grp_cols = [0, 512, 1024, 2560, 3872] + [4384 + 512 * g for g in range(6)]
        grp_rows = [None, U_, CG_, GG_, AG_] + [M13_ + 512 * g for g in range(6)]
        grp_func = [AF.Identity, AF.Sigmoid, AF.Silu, AF.Silu, AF.Silu] + [AF.Sigmoid] * 6
        bcol = alloc("bcol", [128, NG, 4], F32)
        bbcol = Buf()
        for g in range(NG):
            P.dma("sp", bcol[:, g, :], b_in[l:l + 1, grp_cols[g]:grp_cols[g] + 512].rearrange("o (nb p) -> p (o nb)", p=128),
                  w=[bbcol], allow_slow_non_contiguous=True)
        wst = [alloc("wst%d" % i, [128, 8, 512], BF16) for i in range(3)]
        bwst = [Buf() for _ in range(3)]
        stg = [alloc("stg%d" % i, [128, 512], BF16) for i in range(3)]
        bstg = [Buf() for _ in range(3)]
        tv = [alloc("tv%d" % i, [128, 512], F32) for i in range(2)]
        tg = [alloc("tg%d" % i, [128, 512], F32) for i in range(2)]
        btv, btg = [Buf(), Buf()], [Buf(), Buf()]
        kk = [0, 0]

        def proj(wbuf, bw, mb, c0, cn, pb):
            tl = [bhx[t] for t in range(c0 // 128, (c0 + cn) // 128)]
            for kb in range(8):
                mm(PS[pb][:, :cn], wbuf[:, kb, mb * 128:(mb + 1) * 128], hxT[:, kb, c0:c0 + cn], kb == 0, kb == 7,
                   [bw] + tl, [bPS[pb]])

        P.dma("pool", wst[0][:], kview(w_in[:, 0:512]), w=[bwst[0]])
        P.dma("pool", wst[1][:], kview(w_in[:, 512:1024]), w=[bwst[1]])
        for mb in range(4):
            for (c0, cn) in chunks:
                pa, pg = kk[0] % 4, (kk[0] + 1) % 4
                kk[0] += 2
                j = kk[1] % 2
                s3 = kk[1] % 3
                kk[1] += 1
                proj(wst[0], bwst[0], mb, c0, cn, pa)
                proj(wst[1], bwst[1], mb, c0, cn, pg)
                act(tv[j][:, :cn], PS[pa][:, :cn], AF.Identity, [bPS[pa], bbcol], [btv[j]], bias=bcol[:, 0, mb:mb + 1])
                act(tg[j][:, :cn], PS[pg][:, :cn], AF.Sigmoid, [bPS[pg], bbcol], [btg[j]], bias=bcol[:, 1, mb:mb + 1])
                tt("dve", stg[s3][:, :cn], tv[j][:, :cn], tg[j][:, :cn], ALU.mult, [btv[j], btg[j]], [bstg[s3]])
                P.dma("sp", PXF[U_ + mb * 128:U_ + (mb + 1) * 128, c0:c0 + cn], stg[s3][:, :cn], r=[bstg[s3]], w=[bPXF])
        for g in range(2, NG):
            wi = g % 3
            P.dma("pool", wst[wi][:], kview(w_in[:, grp_cols[g]:grp_cols[g] + 512]), w=[bwst[wi]])
            for mb in range(4):
                for (c0, cn) in chunks:
                    pa = kk[0] % 4
                    kk[0] += 1
                    s3 = kk[1] % 3
                    kk[1] += 1
                    proj(wst[wi], bwst[wi], mb, c0, cn, pa)
                    act(stg[s3][:, :cn], PS[pa][:, :cn], grp_func[g], [bPS[pa], bbcol], [bstg[s3]], bias=bcol[:, g, mb:mb + 1])
                    r0 = grp_rows[g] + mb * 128
                    P.dma("sp", PXF[r0:r0 + 128, c0:c0 + cn], stg[s3][:, :cn], r=[bstg[s3]], w=[bPXF])

        if stop == 'B2':
            raise _Stop(nc, P)
        P.barrier()
        arena_reset()
        Sst = alloc("Sst", [128, 2, 128], F32)
        Sbf = alloc("Sbf", [128, 2, 128], BF16)
        bS, bSbf = Buf(), Buf()
        g_la = [alloc("g_la%d" % i, [128, 256], F32) for i in range(2)]
        g_qk = [alloc("g_qk%d" % i, [128, 512], F32) for i in range(2)]
        g_v = [alloc("g_v%d" % i, [128, 512], BF16) for i in range(2)]
        g_e13 = [alloc("g_e13%d" % i, [128, 512], F32) for i in range(2)]
        g_e2 = [alloc("g_e2%d" % i, [128, 256], F32) for i in range(2)]
        g_qin = [alloc("g_qin%d" % i, [128, 256], BF16) for i in range(2)]
        g_kin = [alloc("g_kin%d" % i, [128, 256], BF16) for i in range(2)]
        g_kst = [alloc("g_kst%d" % i, [128, 256], BF16) for i in range(2)]
        g_qpad = [alloc("g_qpad%d" % i, [128, 2, 2, 128], BF16) for i in range(2)]
        g_kT = [alloc("g_kT%d" % i, [128, 2, 128], BF16) for i in range(2)]
        g_dec = [alloc("g_dec%d" % i, [128, 4], F32) for i in range(2)]
        g_att = [alloc("g_att%d" % i, [128, 4, 128], BF16) for i in range(2)]
        g_o = [alloc("g_o%d" % i, [128, 4, 128], F32) for i in range(2)]
        (bg_la, bg_qk, bg_v, bg_e13, bg_e2, bg_qin, bg_kin, bg_kst, bg_qpad, bg_kT, bg_dec, bg_att, bg_o) = [[Buf(), Buf()] for _ in range(13)]
        for i in range(2):
            memset("pool", g_qpad[i][:], 0.0, [bg_qpad[i]])
        PT7 = PS[7][:].bitcast(BF16).rearrange("p (a b) -> p a b", a=8)
        n_it = 0
        for dr in (1, 0):
            order = (list(range(TCT)) + list(range(TCT, NTILE))) if dr == 0 else (list(range(TCT - 1, -1, -1)) + list(range(NTILE - 1, TCT - 1, -1)))
            memset("dve", Sst[:], 0.0, [bS])
            memset("pool", Sbf[:], 0.0, [bSbf])
            mcum, mrest, matt = masks[:, 2 * dr, :], masks[:, 2 * dr + 1, :], masks[:, 4 + dr, :]
            for t in order:
                i = n_it % 2
                n_it += 1
                tc_ = slice(t * 128, (t + 1) * 128)
                P.dma("sp", g_la[i][:], GZ[tc_, dr * 256:(dr + 1) * 256], r=[bGZ], w=[bg_la[i]])
                P.dma("sp", g_qk[i][:], GQK[tc_, :], r=[bGQK], w=[bg_qk[i]])
                P.dma("sp", g_v[i][:], GV[tc_, :], r=[bGV], w=[bg_v[i]])
                pc = i
                mm(PS[pc][:, 0:256], mcum, g_la[i][:], True, True, [bCONST, bg_la[i]], [bPS[pc]])
                mm(PS[pc][:, 256:512], mrest, g_la[i][:], True, True, [bCONST, bg_la[i]], [bPS[pc]])
                act(g_e13[i][:], PS[pc][:], AF.Exp, [bPS[pc]], [bg_e13[i]])
                if stop == 'g1':
                    raise _Stop(nc, P)
                act(g_e2[i][:], PS[pc][:, 0:256], AF.Exp, [bPS[pc]], [bg_e2[i]], scale=-1.0)
                stt(g_qin[i][:], g_qk[i][:, 0:256], 0.125, g_e13[i][:, 0:256], ALU.mult, ALU.mult, [bg_qk[i], bg_e13[i]], [bg_qin[i]])
                tt("dve", g_kin[i][:], g_qk[i][:, 256:512], g_e2[i][:], ALU.mult, [bg_qk[i], bg_e2[i]], [bg_kin[i]])
                tt("dve", g_kst[i][:], g_qk[i][:, 256:512], g_e13[i][:, 256:512], ALU.mult, [bg_qk[i], bg_e13[i]], [bg_kst[i]])
                for b2 in range(2):
                    tr(PT7[:, b2, :], g_qin[i][:, b2 * 128:(b2 + 1) * 128], ident_bf[:], [bg_qin[i], bCONST], [bPS[7]])
                for b2 in range(2):
                    tr(PT7[:, 2 + b2, :], g_kin[i][:, b2 * 128:(b2 + 1) * 128], ident_bf[:], [bg_kin[i], bCONST], [bPS[7]])
                cp("act", g_qpad[i][0:64, :, 0, :], PT7[0:64, 0:2, :], [bPS[7]], [bg_qpad[i]])
                cp("act", g_qpad[i][64:128, :, 1, :], PT7[64:128, 0:2, :], [bPS[7]], [bg_qpad[i]])
                cp("act", g_kT[i][:], PT7[:, 2:4, :], [bPS[7]], [bg_kT[i]])
                if stop == 'g2':
                    raise _Stop(nc, P)
                for b2 in range(2):
                    mm(PS[7][:, 256 + 2 * b2:256 + 2 * b2 + 2], g_la[i][:, b2 * 128:(b2 + 1) * 128], neg16[:], True, True, [bg_la[i], bCONST], [bPS[7]])
                act(g_dec[i][:], PS[7][:, 256:260], AF.Exp, [bPS[7]], [bg_dec[i]])
                if stop == 'g3':
                    raise _Stop(nc, P)
                pa = 2 + i
                po_ = 4 + i
                for h in range(4):
                    b2, p0 = h // 2, 64 * (h % 2)
                    mm(PS[pa][:, h * 128:(h + 1) * 128], g_kT[i][:, b2, :], g_qpad[i][:, b2, h % 2, :], True, True,
                       [bg_kT[i], bg_qpad[i]], [bPS[pa]])
                if stop == 'g3b':
                    raise _Stop(nc, P)
                tt("dve", g_att[i][:], PS[pa][:].rearrange("p (h t) -> p h t", h=4), matt.unsqueeze(1).broadcast_to([128, 4, 128]),
                   ALU.mult, [bPS[pa], bCONST], [bg_att[i]])
                if stop == 'g4':
                    raise _Stop(nc, P)
                for h in range(4):
                    b2, p0 = h // 2, 64 * (h % 2)
                    mm(PS[po_][:, h * 128:(h + 1) * 128], g_v[i][:, h * 128:(h + 1) * 128], g_att[i][:, h, :], True, False,
                       [bg_v[i], bg_att[i]], [bPS[po_]])
                    mm(PS[po_][:, h * 128:(h + 1) * 128], Sbf[:, b2, :], g_qpad[i][:, b2, h % 2, :], False, True,
                       [bSbf, bg_qpad[i]], [bPS[po_]])
                cp("act", g_o[i][:], PS[po_][:].rearrange("p (h t) -> p h t", h=4), [bPS[po_]], [bg_o[i]])
                if stop == 'g5':
                    raise _Stop(nc, P)
                P.dma("sp", OBF[1 - dr].rearrange("(h p) n -> p h n", p=128)[:, :, tc_], g_o[i][:], r=[bg_o[i]], w=[bOBF[1 - dr]])
                for h in range(4):
                    b2, p0 = h // 2, 64 * (h % 2)
                    mm(PS[6][:, h * 128:(h + 1) * 128], g_kst[i][:, b2 * 128:(b2 + 1) * 128],
                       g_v[i][:, h * 128:(h + 1) * 128], True, True, [bg_kst[i], bg_v[i]], [bPS[6]])
                for h in range(4):
                    b2, p0 = h // 2, 64 * (h % 2)
                    stt(Sst[p0:p0 + 64, b2, :], Sst[p0:p0 + 64, b2, :], g_dec[i][p0:p0 + 64, 2 * b2:2 * b2 + 1],
                        PS[6][p0:p0 + 64, h * 128:(h + 1) * 128], ALU.mult, ALU.add, [bS, bg_dec[i], bPS[6]], [bS])
                cp("pool", Sbf[:], Sst[:], [bS], [bSbf])
                if stop == 'g6':
                    raise _Stop(nc, P)

        if stop == 'G':
            raise _Stop(nc, P)
        P.barrier()
        arena_reset()
        wco = alloc("wco", [128, 4, D], BF16)
        wgo = alloc("wgo", [128, 4, D], BF16)
        wao = alloc("wao", [64, 8, D], BF16)
        wo = alloc("wo", [128, 8, D], BF16)
        bwc = Buf()
        P.dma("pool", wco[:], kview(Wd["w_conv_out"][l]), w=[bwc])
        P.dma("pool", wgo[:], kview(Wd["w_gla_out"][l]), w=[bwc])
        P.dma("pool", wao[:], Wd["w_attn_out"][l].rearrange("(h p) n -> p h n", p=64), w=[bwc])
        P.dma("pool", wo[:], kview(Wd["w_out"][l]), w=[bwc])
        gateC = alloc("gateC", [128, D], F32)
        bgateC = Buf()
        dwT = alloc("dwT", [32, 512], F32)
        dww = alloc("dww", [128, 4, 32], F32)
        cpar = alloc("cpar", [128, 3, 4], F32)
        gng = alloc("gng", [128, 1], F32)
        diag = alloc("diag", [128, 4, 31, 128], BF16)
        bdwT, bdww, bcpar, bdiag = Buf(), Buf(), Buf(), Buf()
        memset("dve", dwT[:], 0.0, [bdwT])
        P.dma("sp", dwT[0:31, :], Wd["conv_dw_w"][l], w=[bdwT])
        for cb in range(4):
            tr(PS[0][:, cb * 32:cb * 32 + 32], dwT[0:32, cb * 128:(cb + 1) * 128], ident_f[0:32, 0:32], [bdwT, bCONST], [bPS[0]])
        cp("dve", dww[:, :, 0:31], PS[0][:, 0:128].rearrange("p (c j) -> p c j", c=4)[:, :, 0:31], [bPS[0]], [bdww])
        for pi, nm in enumerate(("conv_dw_b", "conv_ln_g", "conv_ln_b")):
            P.dma("sp", cpar[:, pi, :], Wd[nm][l:l + 1, :].rearrange("o (cb p) -> p (o cb)", p=128), w=[bcpar], allow_slow_non_contiguous=True)
        P.dma("sp", gng[:], Wd["gla_norm_g"][l:l + 1, :].rearrange("o n -> n o"), w=[bcpar], allow_slow_non_contiguous=True)
        for cb in range(4):
            for j in range(31):
                ts("pool", diag[:, cb, j, :], ident_f[:], dww[:, cb, j:j + 1], 0.0, ALU.mult, ALU.add, [bCONST, bdww], [bdiag])

        qT = [alloc("c_qT%d" % i, [128, 512], BF16) for i in range(2)]
        bqT = [Buf(), Buf()]
        Ee = [alloc("c_E%d" % i, [128, 512], BF16) for i in range(4)]
        bE = [Buf() for _ in range(4)]
        attg = [alloc("c_attg%d" % i, [64, 2, 512], BF16) for i in range(2)]
        OAT = alloc("c_OAT", [64, 8, 512], BF16)
        rinv = alloc("c_rinv", [65, 512], F32)
        og = alloc("c_og", [64, 512], F32)
        battg, bOAT, brinv, bog = [Buf(), Buf()], Buf(), Buf(), Buf()
        uh = alloc("c_uh", [128, 4, 512 + 30], BF16)
        cgt = alloc("c_cg", [128, 4, 512], BF16)
        cv = alloc("c_cv", [128, 4, 512], F32)
        csq = alloc("c_csq", [128, 512], F32)
        mean = alloc("c_mean", [128, 512], F32)
        msq = alloc("c_msq", [128, 512], F32)
        rstd = alloc("c_rstd", [128, 512], F32)
        yt = alloc("c_yt", [128, 512], F32)
        yt2 = alloc("c_yt2", [128, 512], F32)
        uo = alloc("c_uo", [128, 4, 512], BF16)
        buh, bcgt, bcv, bcsq, bmean, bmsq, brstd, byt, byt2, buo = [Buf() for _ in range(10)]
        ob = [alloc("c_ob0", [128, 512], F32)] * 2
        of_ = [alloc("c_of0", [128, 512], F32)] * 2
        bob, bof = [Buf()] * 2, [Buf()] * 2
        gsum = mean
        ggt = [alloc("c_gg%d" % i, [128, 512], BF16) for i in range(2)]
        go = alloc("c_go", [128, 4, 512], BF16)
        bgsum, bggt, bgo = bmean, [Buf(), Buf()], Buf()
        mg = [alloc("c_mg0", [128, 3, 512], BF16)] * 2
        bmg = [Buf()] * 2
        mt1, mt2, mt3 = yt, yt2, csq
        bmt1, bmt2, bmt3 = byt, byt2, bcsq
        mT = alloc("c_mT", [128, 8, 512], BF16)
        bmT = Buf()
        xr_ = alloc("c_xr", [128, D], F32)
        xo_ = alloc("c_xo", [128, D], F32)
        bxr, bxo = Buf(), Buf()

        for ci, (c0, cn) in enumerate(chunks):
            is_ctx = ci == 0
            if is_ctx and not need_ctx:
                continue
            cs_ = slice(c0, c0 + cn)
            ktiles = list(range(TCT)) if is_ctx else list(range(NTILE))
            if ci <= 1:
                s_ = 1 if is_ctx else 0
                P.dma("sp", gateC[:], MOD[l, s_:s_ + 1, 2 * D:3 * D].partition_broadcast(128), r=[bMOD], w=[bgateC])
            ei = 0
            for j in range(4):
                qi = j % 2
                P.dma("sp", qT[qi][:, :cn], QT[j * 128:(j + 1) * 128, cs_], r=[bQT], w=[bqT[qi]])
                for hh in range(2):
                    r0 = AG_ + (j + 4 * hh) * 64
                    P.dma("sp", attg[qi][:, hh, :cn], PXF[r0:r0 + 64, cs_], r=[bPXF], w=[battg[qi]])
                for kt_i, kt in enumerate(ktiles):
                    for hh in range(2):
                        p0 = 64 * hh
                        sb_ = ei % 4
                        ei += 1
                        mm(PS[sb_][:, :cn], KT[p0:p0 + 64, kt * 128:(kt + 1) * 128], qT[qi][p0:p0 + 64, :cn], True, True,
                           [bKT[kt], bqT[qi]], [bPS[sb_]])
                        act(Ee[sb_][:, :cn], PS[sb_][:, :cn], AF.Exp, [bPS[sb_]], [bE[sb_]], scale=0.125)
                        mm(PS[4 + hh][0:65, :cn], Vext[:, kt, hh, :], Ee[sb_][:, :cn], kt_i == 0, kt_i == len(ktiles) - 1,
                           [bV[kt], bE[sb_]], [bPS[4 + hh]])
                for hh in range(2):
                    h8 = j + 4 * hh
                    recip(rinv[64:65, :cn], PS[4 + hh][64:65, :cn], [bPS[4 + hh]], [brinv])
                    mm(PS[6][0:64, :cn], ones_f[64:65, 0:64], rinv[64:65, :cn], True, True, [bCONST, brinv], [bPS[6]])
                    tt("dve", og[:, :cn], PS[4 + hh][0:64, :cn], attg[qi][:, hh, :cn], ALU.mult, [bPS[4 + hh], battg[qi]], [bog])
                    tt("dve", OAT[:, h8, :cn], og[:, :cn], PS[6][0:64, :cn], ALU.mult, [bog, bPS[6]], [bOAT])
            seq0, seq1 = (0, TC) if is_ctx else (TC, NT)
            lo, hi = max(c0 - 15, seq0), min(c0 + cn + 15, seq1)
            if lo > c0 - 15:
                memset("pool", uh[:, :, 0:15], 0.0, [buh])
            if hi < c0 + cn + 15:
                memset("pool", uh[:, :, cn + 15:cn + 30], 0.0, [buh])
            P.dma("sp", uh[:, :, lo - (c0 - 15):hi - (c0 - 15)], PXF[U_:U_ + 512, lo:hi].rearrange("(cb p) n -> p cb n", p=128),
                  r=[bPXF], w=[buh])
            P.dma("sp", cgt[:, :, :cn], PXF[CG_:CG_ + 512, cs_].rearrange("(cb p) n -> p cb n", p=128), r=[bPXF], w=[bcgt])
            for cb in range(4):
                pb = cb % 2
                for j in range(31):
                    mm(PS[pb][:, :cn], diag[:, cb, j, :], uh[:, cb, j:j + cn], j == 0, j == 30, [bdiag, buh], [bPS[pb]])
                act(cv[:, cb, :cn], PS[pb][:, :cn], AF.Identity, [bPS[pb], bcpar], [bcv], bias=cpar[:, 0, cb:cb + 1])
            for cb in range(4):
                mm(PS[2][:, :cn], ones_f[:], cv[:, cb, :cn], cb == 0, cb == 3, [bCONST, bcv], [bPS[2]])
            for cb in range(4):
                act(csq[:, :cn], cv[:, cb, :cn], AF.Square, [bcv], [bcsq])
                mm(PS[3][:, :cn], ones_f[:], csq[:, :cn], cb == 0, cb == 3, [bCONST, bcsq], [bPS[3]])
            act(mean[:, :cn], PS[2][:, :cn], AF.Copy, [bPS[2]], [bmean], scale=1.0 / 512)
            tt("dve", msq[:, :cn], mean[:, :cn], mean[:, :cn], ALU.mult, [bmean], [bmsq])
            stt(rstd[:, :cn], PS[3][:, :cn], 1.0 / 512, msq[:, :cn], ALU.mult, ALU.subtract, [bPS[3], bmsq], [brstd])
            act(rstd[:, :cn], rstd[:, :cn], AF.Sqrt, [brstd, bCONST], [brstd], bias=eps_t[:])
            recip(rstd[:, :cn], rstd[:, :cn], [brstd], [brstd])
            for cb in range(4):
                tt("dve", yt[:, :cn], cv[:, cb, :cn], mean[:, :cn], ALU.subtract, [bcv, bmean], [byt])
                tt("dve", yt2[:, :cn], yt[:, :cn], rstd[:, :cn], ALU.mult, [byt, brstd], [byt2])
                act(yt[:, :cn], yt2[:, :cn], AF.Silu, [byt2, bcpar], [byt], bias=cpar[:, 2, cb:cb + 1], scale=cpar[:, 1, cb:cb + 1])
                tt("dve", uo[:, cb, :cn], yt[:, :cn], cgt[:, cb, :cn], ALU.mult, [byt, bcgt], [buo])
            for h in range(4):
                i = h % 2
                P.dma("sp", ggt[i][:, :cn], PXF[GG_ + h * 128:GG_ + (h + 1) * 128, cs_], r=[bPXF], w=[bggt[i]])
                P.dma("sp", ob[i][:, :cn], OBF[0][h * 128:(h + 1) * 128, cs_], r=[bOBF[0]], w=[bob[i]])
                P.dma("sp", of_[i][:, :cn], OBF[1][h * 128:(h + 1) * 128, cs_], r=[bOBF[1]], w=[bof[i]])
                tt("pool", gsum[:, :cn], ob[i][:, :cn], of_[i][:, :cn], ALU.add, [bob[i], bof[i]], [bgsum])
                act(csq[:, :cn], gsum[:, :cn], AF.Square, [bgsum], [bcsq])
                mm(PS[2][:, :cn], ones_f[:], csq[:, :cn], True, True, [bCONST, bcsq], [bPS[2]])
                act(msq[:, :cn], PS[2][:, :cn], AF.Sqrt, [bPS[2], bCONST], [bmsq], bias=eps_t[:], scale=1.0 / 128)
                recip(msq[:, :cn], msq[:, :cn], [bmsq], [bmsq])
                tt("dve", yt[:, :cn], gsum[:, :cn], msq[:, :cn], ALU.mult, [bgsum, bmsq], [byt])
                stt(go[:, h, :cn], yt[:, :cn], gng[:], ggt[i][:, :cn], ALU.mult, ALU.mult, [byt, bcpar, bggt[i]], [bgo])
            for fb in range(8):
                i = fb % 2
                fs = slice(fb * 128, (fb + 1) * 128)
                P.dma("sp", mg[i][:, :, :cn], PXF[M13_:M13_ + 3 * D, cs_].rearrange("(g f p) n -> p g f n", g=3, f=8, p=128)[:, :, fb, :],
                      r=[bPXF], w=[bmg[i]])
                for cb in range(4):
                    mm(PS[0][:, :cn], wco[:, cb, fs], uo[:, cb, :cn], cb == 0, cb == 3, [bwc, buo], [bPS[0]])
                for h in range(4):
                    mm(PS[1][:, :cn], wgo[:, h, fs], go[:, h, :cn], h == 0, h == 3, [bwc, bgo], [bPS[1]])
                for h8 in range(8):
                    mm(PS[2][:, :cn], wao[:, h8, fs], OAT[:, h8, :cn], h8 == 0, h8 == 7, [bwc, bOAT], [bPS[2]])
                tt("dve", mt1[:, :cn], PS[0][:, :cn], mg[i][:, 0, :cn], ALU.mult, [bPS[0], bmg[i]], [bmt1])
                tt("dve", mt2[:, :cn], PS[1][:, :cn], mg[i][:, 1, :cn], ALU.mult, [bPS[1], bmg[i]], [bmt2])
                tt("dve", mt3[:, :cn], PS[2][:, :cn], mg[i][:, 2, :cn], ALU.mult, [bPS[2], bmg[i]], [bmt3])
                tt("pool", mt1[:, :cn], mt1[:, :cn], mt2[:, :cn], ALU.add, [bmt1, bmt2], [bmt1])
                tt("pool", mT[:, fb, :cn], mt1[:, :cn], mt3[:, :cn], ALU.add, [bmt1, bmt3], [bmT])
            for s4 in range(cn // 128):
                t = c0 // 128 + s4
                P.dma("sp", xr_[:], src_tile(t), r=[bXS], w=[bxr])
                for hf in range(2):
                    pb = 3 + hf
                    for fb in range(8):
                        mm(PS[pb][:], mT[:, fb, s4 * 128:(s4 + 1) * 128], wo[:, fb, hf * 512:(hf + 1) * 512], fb == 0, fb == 7,
                           [bmT, bwc], [bPS[pb]])
                    tt("dve", xo_[:, hf * 512:(hf + 1) * 512], PS[pb][:], gateC[:, hf * 512:(hf + 1) * 512],
                       ALU.mult, [bPS[pb], bgateC], [bxo])
                tt("pool", xo_[:], xo_[:], xr_[:], ALU.add, [bxo, bxr], [bxo])
                if last:
                    xrow = (t - TCT) * 128
                    final_ops.append(P.dma("sp", out_d[xrow:xrow + 128, :], xo_[:], r=[bxo], w=[bOUT]))
                else:
                    P.dma("sp", XS[t * 128:(t + 1) * 128, :], xo_[:], r=[bxo], w=[Buf()])

    P.emit(final_ops)
    return nc, P


_CACHE = {}


def kernel(**inputs):
    T, TC, DEPTH, B = 4096, 256, 4, 8
    x = np.asarray(inputs["x"], np.float32)
    B, T = x.shape[0], x.shape[1]
    TC = inputs["ctx"].shape[1]
    DEPTH = inputs["w_in"].shape[0]
    key = (T, TC, DEPTH)
    if key not in _CACHE:
        _CACHE[key] = build(T, TC, DEPTH)[0]
    nc = _CACHE[key]
    consts = host_consts(T)
    shared = {n: np.ascontiguousarray(np.asarray(inputs[n], np.float32)) for n, _ in WEIGHT_SPECS}
    shared["c_ctx"] = np.ascontiguousarray(np.asarray(inputs["c_ctx"], np.float32).reshape(1, D))
    shared.update(consts)
    in_maps = []
    for b in range(B):
        m = dict(shared)
        m["x"] = np.ascontiguousarray(x[b])
        m["ctx"] = np.ascontiguousarray(np.asarray(inputs["ctx"], np.float32)[b])
        m["c"] = np.ascontiguousarray(np.asarray(inputs["c"], np.float32)[b:b + 1])
        in_maps.append(m)
    res = run_bass_kernel_spmd(nc, in_maps, core_ids=list(range(B)))
    return np.stack([np.asarray(r["out"], np.float32) for r in res.results], axis=0)
```

```python
import contextlib
import numpy as np
import ml_dtypes
import concourse.bass as bass
import concourse.mybir as mybir
from concourse.bass_utils import run_bass_kernel_spmd

F32 = mybir.dt.float32
BF16 = mybir.dt.bfloat16
AF = mybir.ActivationFunctionType
ALU = mybir.AluOpType
AX = mybir.AxisListType

D = 1024
NIN = 7456
EPS = 1e-6


class Buf:
    __slots__ = ("name", "w", "r", "rd")

    def __init__(self, name=""):
        self.name = name
        self.w = None
        self.r = {}
        self.rd = []


class Op:
    __slots__ = ("eng", "fn", "deps", "needs_inc", "val", "dma", "dsem", "dval", "idx", "tag")

    def __init__(self, eng, fn, dma, idx):
        self.eng = eng
        self.fn = fn
        self.dma = dma
        self.deps = []
        self.needs_inc = False
        self.val = None
        self.dsem = None
        self.dval = None
        self.idx = idx


class Prog:
    ENGS = ("pe", "act", "dve", "pool", "sp")
    NDMA = 56

    def __init__(self, nc):
        self.nc = nc
        self.ops = {e: [] for e in self.ENGS}
        self.all = []
        self.dma_since_barrier = []
        self.tag = ""
        self.annotate = False

    def add(self, eng, fn, r=(), w=(), dma=False):
        op = Op(eng, fn, dma, len(self.all))
        op.tag = self.tag
        edeps = {}
        ddeps = {}

        def dep(x):
            if x is None or x is op:
                return
            if x.dma:
                ddeps[x.idx] = x
            else:
                if x.eng == "pe" and eng == "pe" and not dma:
                    return
                o = edeps.get(x.eng)
                if o is None or o.idx < x.idx:
                    edeps[x.eng] = x

        for b in r:
            dep(b.w)
        for b in w:
            dep(b.w)
            for x in b.r.values():
                dep(x)
            for x in b.rd:
                dep(x)
        for b in w:
            b.w = op
            b.r = {}
            b.rd = []
        for b in r:
            if b.w is not op:
                if dma:
                    b.rd.append(op)
                else:
                    b.r[eng] = op
        op.deps = list(edeps.values()) + list(ddeps.values())
        self.ops[eng].append(op)
        self.all.append(op)
        if dma:
            self.dma_since_barrier.append(op)
        return op

    def dma(self, q, out, in_, r=(), w=(), **kw):
        return self.add(q, lambda e: e.dma_start(out=out, in_=in_, **kw), r=r, w=w, dma=True)

    def barrier(self):
        last = {}
        for e in self.ENGS:
            last[e] = None
            for o in reversed(self.ops[e]):
                if not o.dma and o.fn is not None:
                    last[e] = o
                    break
        dmas = list(self.dma_since_barrier)
        self.dma_since_barrier = []
        for e in self.ENGS:
            op = Op(e, None, False, len(self.all))
            for e2 in self.ENGS:
                if e2 != e and last[e2] is not None:
                    op.deps.append(last[e2])
            op.deps.extend(dmas)
            self.ops[e].append(op)
            self.all.append(op)

    def emit(self, final_wait_ops=()):
        nc = self.nc
        fin = Op("sp", None, False, len(self.all))
        fin.deps = list(final_wait_ops)
        self.ops["sp"].append(fin)
        self.all.append(fin)
        sem_last = [None] * self.NDMA
        sem_cnt = [0] * self.NDMA
        k = 0
        for op in self.all:
            if op.dma:
                s = k % self.NDMA
                k += 1
                if sem_last[s] is not None:
                    op.deps.append(sem_last[s])
                sem_last[s] = op
                sem_cnt[s] += 16
                op.dsem = s
                op.dval = sem_cnt[s]
        for op in self.all:
            for d in op.deps:
                d.needs_inc = True
        cnt = {e: 0 for e in self.ENGS}
        for op in self.all:
            if not op.dma and op.fn is not None:
                if op.needs_inc:
                    cnt[op.eng] += 1
                op.val = cnt[op.eng]
        self.stats = {e: [len(self.ops[e]), cnt[e], 0] for e in self.ENGS}
        with contextlib.ExitStack() as st:
            esem = {e: st.enter_context(nc.semaphore("s_" + e)) for e in self.ENGS}
            dsem = [st.enter_context(nc.semaphore("d%d" % i)) for i in range(self.NDMA)]
            block = st.enter_context(nc.Block())

            def run(ename):
                def body(e):
                    waited = {}
                    nwait = 0
                    for op in self.ops[ename]:
                        for d in op.deps:
                            if d.dma:
                                key, v, sem = ("d", d.dsem), d.dval, dsem[d.dsem]
                            else:
                                if d.fn is None:
                                    continue
                                key, v, sem = ("e", d.eng), d.val, esem[d.eng]
                            if waited.get(key, 0) >= v:
                                continue
                            waited[key] = v
                            e.wait_ge(sem, v)
                            nwait += 1
                        if op.fn is None:
                            continue
                        ins = op.fn(e)
                        if self.annotate:
                            ins.annotate(op.tag)
                        if op.dma:
                            ins.then_inc(dsem[op.dsem], 16)
                        elif op.needs_inc:
                            ins.then_inc(esem[ename], 1)
                    self.stats[ename][2] = nwait

                return body

            block.tensor(run("pe"))
            block.scalar(run("act"))
            block.vector(run("dve"))
            block.gpsimd(run("pool"))
            block.sync(run("sp"))


def host_consts(T):
    c = {}
    c["ident_bf"] = np.eye(128, dtype=np.float32).astype(ml_dtypes.bfloat16)
    c["ident_f"] = np.eye(128, dtype=np.float32)
    tp = np.arange(128)[:, None]
    t = np.arange(128)[None, :]
    mle = (tp <= t).astype(np.float32)
    mge = (tp >= t).astype(np.float32)
    s = -1.0 / 16.0
    masks = np.stack([s * mle, s * (1 - mle), s * mge, s * (1 - mge), mle, mge], axis=1)
    c["masks"] = np.ascontiguousarray(masks.astype(np.float32))
    c["ones_f"] = np.ones((128, 128), np.float32)
    c["neg16"] = np.full((128, 2), s, np.float32)
    n_rows = T // 64
    row = np.repeat(np.arange(n_rows, dtype=np.float32), 64)
    col = np.tile(np.arange(64, dtype=np.float32), n_rows)
    freqs = (np.float32(10000.0) ** (-np.arange(16, dtype=np.float32) / np.float32(16))).astype(np.float32)
    ar = (row[:, None] * freqs).astype(np.float32)
    ac = (col[:, None] * freqs).astype(np.float32)
    cr, sr, cc, sc = np.cos(ar), np.sin(ar), np.cos(ac), np.sin(ac)
    c["rope_cos"] = np.concatenate([cr, cr, cc, cc], axis=1).astype(np.float32)
    c["rope_sin"] = np.concatenate([-sr, sr, -sc, sc], axis=1).astype(np.float32)
    return c


CONST_SPECS = [("ident_bf", [128, 128], BF16), ("ident_f", [128, 128], F32), ("masks", [128, 6, 128], F32),
               ("ones_f", [128, 128], F32), ("neg16", [128, 2], F32)]

WEIGHT_SPECS = [("norm_g", [D]), ("w_mod", [D, 3 * D]), ("b_mod", [3 * D]), ("w_in", [D, NIN]), ("b_in", [NIN]),
                ("conv_dw_w", [31, 512]), ("conv_dw_b", [512]), ("conv_ln_g", [512]), ("conv_ln_b", [512]),
                ("w_conv_out", [512, D]), ("gla_w_gate", [2, 16, 256]), ("gla_b_gate", [2, 256]),
                ("gla_norm_g", [128]), ("w_gla_out", [512, D]), ("q_norm_g", [64]), ("k_norm_g", [64]),
                ("w_attn_out", [512, D]), ("w_out", [D, D])]


class _Stop(Exception):
    pass


def build(T, TC, DEPTH, debug=False, stop=None, annotate=False):
    try:
        return _build(T, TC, DEPTH, debug, stop, annotate)
    except _Stop as e:
        nc, P = e.args
        P.emit([])
        return nc, P


def _build(T, TC, DEPTH, debug=False, stop=None, annotate=False):
    NT = TC + T
    NTILE = NT // 128
    TCT = TC // 128
    chunks = [(0, TC)] + [(TC + 512 * i, 512) for i in range(T // 512)]

    nc = bass.Bass("TRN2", target_bir_lowering=False)
    P = Prog(nc)
    P.annotate = annotate

    def din(name, shape, dt=F32):
        return nc.dram_tensor(name, shape, dt, kind="ExternalInput").ap()

    x_d = din("x", [T, D])
    ctx_d = din("ctx", [TC, D])
    c_d = din("c", [1, D])
    cctx_d = din("c_ctx", [1, D])
    Wd = {n: din(n, [DEPTH] + s) for n, s in WEIGHT_SPECS}
    Cd = {n: din(n, s, dt) for n, s, dt in CONST_SPECS}
    cos_d = din("rope_cos", [T, 64])
    sin_d = din("rope_sin", [T, 64])
    out_d = nc.dram_tensor("out", [T, D], F32, kind="ExternalOutput").ap()

    skind = "ExternalOutput" if debug else "Internal"

    def dscr(name, shape, dt):
        return nc.dram_tensor(name, shape, dt, kind=skind).ap()

    XS = dscr("XS", [NT, D], F32)
    MOD = dscr("MOD", [DEPTH, 2, 3 * D], F32)
    NPX = 5120
    PXF = dscr("PXF", [NPX, NT], BF16)
    QT = dscr("QT", [512, NT], BF16)
    GQK = dscr("GQK", [NT, 512], F32)
    GV = dscr("GV", [NT, 512], BF16)
    GZ = dscr("GZ", [NT, 512], F32)
    OBF = [dscr("OB", [512, NT], F32), dscr("OF", [512, NT], F32)]
    U_, CG_, GG_, AG_, M13_ = 0, 512, 1024, 1536, 2048
    bXS, bMOD, bPXF, bQT, bGQK, bGV, bGZ = [Buf(n) for n in ("XS", "MOD", "PXF", "QT", "GQK", "GV", "GZ")]
    bOBF = [Buf("OB"), Buf("OF")]
    bOUT = Buf("OUT")

    SB_LO, SB_HI = 16512, 229344
    state = {"persist": SB_LO, "top": SB_LO, "n": 0}

    def alloc(name, shape, dt):
        esz = 2 if dt == BF16 else 4
        nbytes = esz
        for s_ in shape[1:]:
            nbytes *= s_
        nbytes = (nbytes + 63) // 64 * 64
        off = state["top"]
        state["top"] += nbytes
        assert state["top"] <= SB_HI, ("SBUF overflow", name, state["top"])
        state["n"] += 1
        return nc.alloc_sbuf_tensor_at("%s_%d" % (name, state["n"]), shape, dt, offset=off)

    def arena_reset():
        state["top"] = state["persist"]

    PS = [nc.alloc_psum_tensor("ps%d" % i, [128, 512], F32) for i in range(8)]
    bPS = [Buf("ps%d" % i) for i in range(8)]

    def mm(out, lhsT, rhs, start, stop, r, w):
        return P.add("pe", lambda e: e.matmul(out, lhsT=lhsT, rhs=rhs, start=start, stop=stop), r=r, w=w)

    def tr(out, in_, ident, r, w):
        return P.add("pe", lambda e: e.transpose(out=out, in_=in_, identity=ident), r=r, w=w)

    def act(out, in_, func, r, w, bias=None, scale=None, accum=None):
        kw = {}
        if bias is not None:
            kw["bias"] = bias
        if scale is not None:
            kw["scale"] = scale
        if accum is not None:
            kw["accum_out"] = accum
        return P.add("act", lambda e: e.activation(out=out, in_=in_, func=func, **kw), r=r, w=w)

    def tt(eng, out, in0, in1, op, r, w):
        return P.add(eng, lambda e: e.tensor_tensor(out=out, in0=in0, in1=in1, op=op), r=r, w=w)

    def ts(eng, out, in0, s1, s2, op0, op1, r, w):
        return P.add(eng, lambda e: e.tensor_scalar(out=out, in0=in0, scalar1=s1, scalar2=s2, op0=op0, op1=op1), r=r, w=w)

    def stt(out, in0, scalar, in1, op0, op1, r, w):
        return P.add("dve", lambda e: e.scalar_tensor_tensor(out=out, in0=in0, scalar=scalar, in1=in1, op0=op0, op1=op1), r=r, w=w)

    def cp(eng, out, in_, r, w):
        if eng == "act":
            return P.add("act", lambda e: e.copy(out=out, in_=in_), r=r, w=w)
        return P.add(eng, lambda e: e.tensor_copy(out=out, in_=in_), r=r, w=w)

    def memset(eng, ap, v, w):
        return P.add(eng, lambda e: e.memset(ap, v), w=w)

    def recip(out, in_, r, w):
        return P.add("dve", lambda e: e.reciprocal(out=out, in_=in_), r=r, w=w)

    def kview(w2d):
        return w2d.rearrange("(kb p) n -> p kb n", p=128)

    ident_bf = alloc("ident_bf", [128, 128], BF16)
    ident_f = alloc("ident_f", [128, 128], F32)
    masks = alloc("masks", [128, 6, 128], F32)
    ones_f = alloc("ones_f", [128, 128], F32)
    neg16 = alloc("neg16", [128, 2], F32)
    eps_t = alloc("eps", [128, 1], F32)
    bCONST = Buf("const")
    for n, t_ in (("ident_bf", ident_bf), ("ident_f", ident_f), ("masks", masks), ("ones_f", ones_f), ("neg16", neg16)):
        P.dma("sp", t_[:], Cd[n], w=[bCONST])
    memset("dve", eps_t[:], EPS, [bCONST])
    KT = alloc("KT", [128, NT], BF16)
    Vext = alloc("Vext", [128, NTILE, 2, 65], BF16)
    bKT = [Buf("KT%d" % i) for i in range(NTILE)]
    bV = [Buf("V%d" % i) for i in range(NTILE)]
    memset("pool", Vext[:, :, :, 64:65], 1.0, bV)
    state["persist"] = state["top"]

    P.tag = "0"
    arena_reset()
    cc = alloc("cc", [128, 2, 8], F32)
    sc = alloc("sc", [128, 2, 8], F32)
    bm = alloc("bm", [2, 3 * D], F32)
    ng = alloc("ng", [2, D], F32)
    modsb = alloc("modsb", [2, 3 * D], F32)
    wm = [alloc("wm%d" % i, [128, 8, 512], F32) for i in range(2)]
    bcc, bsc, bbm, bng, bmodsb = Buf(), Buf(), Buf(), Buf(), Buf()
    bwm = [Buf(), Buf()]
    P.dma("sp", cc[:, 0, :], c_d.rearrange("o (kb p) -> p (o kb)", p=128), w=[bcc], allow_slow_non_contiguous=True)
    P.dma("sp", cc[:, 1, :], cctx_d.rearrange("o (kb p) -> p (o kb)", p=128), w=[bcc], allow_slow_non_contiguous=True)
    act(sc[:], cc[:], AF.Silu, [bcc], [bsc])
    k = 0
    for l in range(DEPTH):
        P.dma("sp", bm[:], Wd["b_mod"][l:l + 1, :].partition_broadcast(2), w=[bbm])
        P.dma("sp", ng[:], Wd["norm_g"][l:l + 1, :].partition_broadcast(2), w=[bng])
        for ch in range(6):
            wb_ = wm[k % 2]
            bw_ = bwm[k % 2]
            pb = k % 2
            k += 1
            P.dma("sp", wb_[:], kview(Wd["w_mod"][l][:, ch * 512:(ch + 1) * 512]), w=[bw_])
            for kb in range(8):
                mm(PS[pb][0:2, :], sc[:, :, kb], wb_[:, kb, :], kb == 0, kb == 7, [bsc, bw_], [bPS[pb]])
            tt("dve", modsb[:, ch * 512:(ch + 1) * 512], PS[pb][0:2, :], bm[:, ch * 512:(ch + 1) * 512], ALU.add,
               [bPS[pb], bbm], [bmodsb])
        stt(modsb[:, D:2 * D], modsb[:, D:2 * D], 1.0, ng[:], ALU.add, ALU.mult, [bmodsb, bng], [bmodsb])
        P.dma("sp", MOD[l], modsb[:], r=[bmodsb], w=[bMOD])

    if stop == '0':
        raise _Stop(nc, P)
    final_ops = []
    for l in range(DEPTH):
        need_ctx = l < DEPTH - 1
        last = l == DEPTH - 1
        w_in = Wd["w_in"][l]
        b_in = Wd["b_in"]

        def src_tile(t):
            if l == 0:
                return ctx_d[t * 128:(t + 1) * 128, :] if t < TCT else x_d[(t - TCT) * 128:(t - TCT + 1) * 128, :]
            return XS[t * 128:(t + 1) * 128, :]

        P.tag = "L%d.A" % l
        P.barrier()
        arena_reset()
        hxT = alloc("hxT", [128, 8, NT], BF16)
        mark_hx = state["top"]
        bhx = [Buf("hx%d" % i) for i in range(NTILE)]
        modA = alloc("modA", [128, 2, 2, D], F32)
        bmodA = Buf()
        for s_ in range(2):
            P.dma("sp", modA[:, s_, 0, :], MOD[l, s_:s_ + 1, D:2 * D].partition_broadcast(128), r=[bMOD], w=[bmodA])
            P.dma("sp", modA[:, s_, 1, :], MOD[l, s_:s_ + 1, 0:D].partition_broadcast(128), r=[bMOD], w=[bmodA])
        wlr = alloc("wlr", [128, 8, 32], BF16)
        blr = alloc("blr", [32, 1], F32)
        lrT = alloc("lrT", [33, NT], BF16)
        w_tm = alloc("w_tm", [128, 8, 1792], BF16)
        bias_tm = alloc("bias_tm", [128, 1792], F32)
        Gq = alloc("Gq", [128, 64], F32)
        Gk = alloc("Gk", [128, 64], F32)
        wupf = alloc("wupf", [33, 512], F32)
        wup = alloc("wup", [33, 512], BF16)
        bwlr, bblr, blrT, bwtm, bbtm, bG, bwupf, bwup = [Buf() for _ in range(8)]
        P.dma("pool", wlr[:], kview(w_in[:, 3072:3104]), w=[bwlr])
        P.dma("sp", blr[:], b_in[l:l + 1, 3072:3104].rearrange("o n -> n o"), w=[bblr], allow_slow_non_contiguous=True)
        memset("pool", lrT[32:33, :], 1.0, [blrT])
        tm_cols = [(1536, 2048, 0), (2048, 2560, 512), (3104, 3616, 1024), (3616, 3872, 1536)]
        for a, b, o in tm_cols:
            P.dma("pool", w_tm[:, :, o:o + (b - a)], kview(w_in[:, a:b]), w=[bwtm])
            P.dma("sp", bias_tm[:, o:o + (b - a)], b_in[l:l + 1, a:b].partition_broadcast(128), w=[bbtm])
        P.dma("sp", Gq[:], Wd["q_norm_g"][l:l + 1, :].partition_broadcast(128), w=[bG])
        P.dma("sp", Gk[:], Wd["k_norm_g"][l:l + 1, :].partition_broadcast(128), w=[bG])
        memset("dve", wupf[:], 0.0, [bwupf])
        P.dma("sp", wupf[0:16, 0:256], Wd["gla_w_gate"][l, 0], w=[bwupf])
        P.dma("sp", wupf[16:32, 256:512], Wd["gla_w_gate"][l, 1], w=[bwupf])
        P.dma("sp", wupf[32:33, 0:256], Wd["gla_b_gate"][l, 0:1, :], w=[bwupf])
        P.dma("sp", wupf[32:33, 256:512], Wd["gla_b_gate"][l, 1:2, :], w=[bwupf])
        cp("dve", wup[:], wupf[:], [bwupf], [bwup])

        mark_A = state["top"]
        xt = [alloc("xt%d" % i, [128, D], F32) for i in range(3)]
        bxt = [Buf(), Buf(), Buf()]
        sqj = alloc("sqj", [128, D], BF16)
        junk = [alloc("junk%d" % i, [128, D], F32) for i in range(2)]
        hb = [alloc("hb%d" % i, [128, D], BF16) for i in range(2)]
        bhb = [Buf(), Buf()]
        ssA = alloc("ssA", [128, 2], F32)
        bsqj, bjunk, bss = Buf(), [Buf(), Buf()], [Buf(), Buf()]
        PTvs = [PS[6 + i][:].bitcast(BF16).rearrange("p (a b) -> p a b", a=8) for i in range(2)]
        PTv = PTvs[1]
        for t in range(NTILE):
            i = t % 2
            i3 = t % 3
            s_ = 0 if t >= TCT else 1
            P.dma("sp", xt[i3][:], src_tile(t), r=[bXS], w=[bxt[i3]])
            act(sqj[:], xt[i3][:], AF.Square, [bxt[i3]], [bsqj, bss[i]], accum=ssA[:, i:i + 1])
            act(ssA[:, i:i + 1], ssA[:, i:i + 1], AF.Sqrt, [bss[i], bCONST], [bss[i]], bias=eps_t[:], scale=1.0 / D)
            recip(ssA[:, i:i + 1], ssA[:, i:i + 1], [bss[i]], [bss[i]])
            stt(junk[i][:], xt[i3][:], ssA[:, i:i + 1], modA[:, s_, 0, :], ALU.mult, ALU.mult, [bxt[i3], bss[i], bmodA], [bjunk[i]])
            tt("dve", hb[i][:], junk[i][:], modA[:, s_, 1, :], ALU.add, [bjunk[i], bmodA], [bhb[i]])
            for kb in range(8):
                tr(PTvs[i][:, kb, :], hb[i][:, kb * 128:(kb + 1) * 128], ident_bf[:], [bhb[i], bCONST], [bPS[6 + i]])
            cp("act", hxT[:, :, t * 128:(t + 1) * 128], PTvs[i][:], [bPS[6 + i]], [bhx[t]])
        P.barrier()
        state["top"] = mark_A

        if stop == 'A':
            raise _Stop(nc, P)
        P.tag = "L%d.B0" % l
        for ci, (c0, cn) in enumerate(chunks):
            pb = ci % 2
            tl = list(range(c0 // 128, (c0 + cn) // 128))
            for kb in range(8):
                mm(PS[pb][0:32, :cn], wlr[:, kb, :], hxT[:, kb, c0:c0 + cn], kb == 0, kb == 7,
                   [bwlr] + [bhx[t] for t in tl], [bPS[pb]])
            act(lrT[0:32, c0:c0 + cn], PS[pb][0:32, :cn], AF.Identity, [bPS[pb], bblr], [blrT], bias=blr[:])

        if stop == 'B0':
            raise _Stop(nc, P)
        P.tag = "L%d.B1" % l
        qk_sb = [alloc("qk_sb%d" % i, [128, 512], F32) for i in range(2)]
        v_sb = [alloc("v_sb%d" % i, [128, 512], BF16) for i in range(2)]
        la_e = alloc("la_e", [128, 512], F32)
        la_sb = [alloc("la_sb%d" % i, [128, 512], F32) for i in range(2)]
        qT_sb = [alloc("qT_sb%d" % i, [128, 4, 128], BF16) for i in range(2)]
        bqk_sb, bv_sb, bla_sb, bqT_sb = [[Buf(), Buf()] for _ in range(4)]
        bla_e = Buf()
        nscr = []
        for si, wdt in enumerate((512, 512, 128)):
            nscr.append(([alloc("nq%d_%d" % (si, z), [128, wdt], F32) for z in range(6)] + [alloc("nqss%d" % si, [128, 8], F32)],
                         [Buf() for _ in range(7)]))
        nq_o = [alloc("nq_o%d" % i, [128, 512], BF16) for i in range(2)]
        nk_o = alloc("nk_o", [128, 128], BF16)
        bnq_o = [Buf(), Buf()]
        bnk_o = Buf()
        cs_t = [alloc("cs_t%d" % i, [128, 2, 64], F32) for i in range(2)]
        bcs = [Buf(), Buf()]

        def norm_rope(src, bsrc, nh, G, boff, rope_i, out, bout, perm=False, scr=0):
            n = nh * 64
            tl_, bl_ = nscr[scr]
            b_, sq_, g_, n_, a_, r_, ss_ = [x[:, :n] for x in tl_[:6]] + [tl_[6][:, :nh]]
            Bb, Bsq, Bg, Bn, Ba, Br, Bss = bl_

            def v3(ap):
                return ap.rearrange("p (h d) -> p h d", h=nh)

            if perm:
                def vin(ap):
                    return ap.rearrange("p (a j d) -> p a j d", a=2, j=4)
                vout = out.rearrange("p (j a d) -> p a j d", a=2, j=4)
                rsb = ss_.rearrange("p (a j) -> p a j", a=2).unsqueeze(3).broadcast_to([128, 2, 4, 64])
            else:
                vin = v3
                vout = v3(out)
                rsb = ss_.unsqueeze(2).broadcast_to([128, nh, 64])

            tt("dve", b_, src, bias_tm[:, boff:boff + n], ALU.add, [bsrc, bbtm], [Bb])
            act(sq_, b_, AF.Square, [Bb], [Bsq])
            P.add("dve", lambda e: e.tensor_reduce(out=ss_, in_=v3(sq_), axis=AX.X, op=ALU.add), r=[Bsq], w=[Bss])
            act(ss_, ss_, AF.Sqrt, [Bss, bCONST], [Bss], bias=eps_t[:], scale=1.0 / 64)
            recip(ss_, ss_, [Bss], [Bss])
            tt("dve", v3(g_), v3(b_), G[:].unsqueeze(1).broadcast_to([128, nh, 64]), ALU.mult, [Bb, bG], [Bg])
            if rope_i is None:
                tt("dve", vout, vin(g_), rsb, ALU.mult, [Bg, Bss], [bout])
                return
            tt("dve", v3(n_), v3(g_), ss_.unsqueeze(2).broadcast_to([128, nh, 64]), ALU.mult, [Bg, Bss], [Bn])
            cs = cs_t[rope_i]
            tt("dve", v3(a_), v3(n_), cs[:, 0, :].unsqueeze(1).broadcast_to([128, nh, 64]), ALU.mult, [Bn, bcs[rope_i]], [Ba])

            def v5(ap):
                return ap.rearrange("p (h a b c) -> p h a b c", h=nh, a=2, b=2)

            sn5 = cs[:, 1, :].rearrange("p (a b c) -> p a b c", a=2, b=2)
            for hf in range(2):
                tt("dve", v5(r_)[:, :, :, hf, :], v5(n_)[:, :, :, 1 - hf, :],
                   sn5[:, :, hf, :].unsqueeze(1).broadcast_to([128, nh, 2, 16]), ALU.mult, [Bn, bcs[rope_i]], [Br])
            tt("dve", vout, vin(a_), vin(r_), ALU.add, [Ba, Br], [bout])

        for t in range(NTILE):
            i = t % 2
            tc_ = slice(t * 128, (t + 1) * 128)
            is_x = t >= TCT
            if is_x:
                xr = (t - TCT) * 128
                P.dma("sp", cs_t[i][:, 0, :], cos_d[xr:xr + 128, :], w=[bcs[i]])
                P.dma("sp", cs_t[i][:, 1, :], sin_d[xr:xr + 128, :], w=[bcs[i]])
            for kb in range(8):
                mm(PS[0][:], hxT[:, kb, tc_], w_tm[:, kb, 0:512], kb == 0, kb == 7, [bhx[t], bwtm], [bPS[0]])
            tt("dve", qk_sb[i][:], PS[0][:], bias_tm[:, 0:512], ALU.add, [bPS[0], bbtm], [bqk_sb[i]])
            P.dma("sp", GQK[tc_, :], qk_sb[i][:], r=[bqk_sb[i]], w=[bGQK])
            for kb in range(8):
                mm(PS[1][:], hxT[:, kb, tc_], w_tm[:, kb, 512:1024], kb == 0, kb == 7, [bhx[t], bwtm], [bPS[1]])
            tt("dve", v_sb[i][:], PS[1][:], bias_tm[:, 512:1024], ALU.add, [bPS[1], bbtm], [bv_sb[i]])
            P.dma("sp", GV[tc_, :], v_sb[i][:], r=[bv_sb[i]], w=[bGV])
            if is_x or need_ctx:
                for kb in range(8):
                    mm(PS[2][:], hxT[:, kb, tc_], w_tm[:, kb, 1024:1536], kb == 0, kb == 7, [bhx[t], bwtm], [bPS[2]])
                norm_rope(PS[2][:], bPS[2], 8, Gq, 1024, i if is_x else None, nq_o[i][:], bnq_o[i], perm=True, scr=i)
                for j in range(4):
                    tr(PTv[:, j, :], nq_o[i][:, j * 128:(j + 1) * 128], ident_bf[:], [bnq_o[i], bCONST], [bPS[7]])
                cp("act", qT_sb[i][:], PTv[:, 0:4, :], [bPS[7]], [bqT_sb[i]])
                P.dma("sp", QT.rearrange("(j p) n -> p j n", p=128)[:, :, tc_], qT_sb[i][:], r=[bqT_sb[i]], w=[bQT])
            for kb in range(8):
                mm(PS[3][:, 0:256], hxT[:, kb, tc_], w_tm[:, kb, 1536:1792], kb == 0, kb == 7, [bhx[t], bwtm], [bPS[3]])
            norm_rope(PS[3][:, 0:128], bPS[3], 2, Gk, 1536, i if is_x else None, nk_o[:], bnk_o, scr=2)
            PT6 = PS[6][:].bitcast(BF16)
            tr(PT6[:, 0:128], nk_o[:], ident_bf[:], [bnk_o, bCONST], [bPS[6]])
            cp("act", KT[:, tc_], PT6[:, 0:128], [bPS[6]], [bKT[t]])
            tt("dve", Vext[:, t, :, 0:64], PS[3][:, 128:256].rearrange("p (h d) -> p h d", h=2),
               bias_tm[:, 1664:1792].rearrange("p (h d) -> p h d", h=2), ALU.add, [bPS[3], bbtm], [bV[t]])
            mm(PS[4][:], lrT[0:33, tc_], wup[0:33, :], True, True, [blrT, bwup], [bPS[4]])
            act(la_e[:], PS[4][:], AF.Exp, [bPS[4]], [bla_e], scale=-1.0)
            act(la_sb[i][:], la_e[:], AF.Ln, [bla_e], [bla_sb[i]], bias=1.0)
            P.dma("sp", GZ[tc_, :], la_sb[i][:], r=[bla_sb[i]], w=[bGZ])

        if stop == 'B1':
            raise _Stop(nc, P)
        P.tag = "L%d.B2" % l
        P.barrier()
        state["top"] = mark_hx
        NG = 11
        grp_cols = [0, 512, 1024, 2560, 3872] + [4384 + 512 * g for g in range(6)]
        grp_rows = [None, U_, CG_, GG_, AG_] + [M13_ + 512 * g for g in range(6)]
        grp_func = [AF.Identity, AF.Sigmoid, AF.Silu, AF.Silu, AF.Silu] + [AF.Sigmoid] * 6
        bcol = alloc("bcol", [128, NG, 4], F32)
        bbcol = Buf()
        for g in range(NG):
            P.dma("sp", bcol[:, g, :], b_in[l:l + 1, grp_cols[g]:grp_cols[g] + 512].rearrange("o (nb p) -> p (o nb)", p=128),
                  w=[bbcol], allow_slow_non_contiguous=True)
        wst = [alloc("wst%d" % i, [128, 8, 512], BF16) for i in range(3)]
        bwst = [Buf() for _ in range(3)]
        stg = [alloc("stg%d" % i, [128, 512], BF16) for i in range(3)]
        bstg = [Buf() for _ in range(3)]
        tv = [alloc("tv%d" % i, [128, 512], F32) for i in range(2)]
        tg = [alloc("tg%d" % i, [128, 512], F32) for i in range(2)]
        btv, btg = [Buf(), Buf()], [Buf(), Buf()]
        kk = [0, 0]

        def proj(wbuf, bw, mb, c0, cn, pb):
            tl = [bhx[t] for t in range(c0 // 128, (c0 + cn) // 128)]
            for kb in range(8):
                mm(PS[pb][:, :cn], wbuf[:, kb, mb * 128:(mb + 1) * 128], hxT[:, kb, c0:c0 + cn], kb == 0, kb == 7,
                   [bw] + tl, [bPS[pb]])

        P.dma("pool", wst[0][:], kview(w_in[:, 0:512]), w=[bwst[0]])
        P.dma("pool", wst[1][:], kview(w_in[:, 512:1024]), w=[bwst[1]])
        for mb in range(4):
            for (c0, cn) in chunks:
                pa, pg = kk[0] % 4, (kk[0] + 1) % 4
                kk[0] += 2
                j = kk[1] % 2
                s3 = kk[1] % 3
                kk[1] += 1
                proj(wst[0], bwst[0], mb, c0, cn, pa)
                proj(wst[1], bwst[1], mb, c0, cn, pg)
                act(tv[j][:, :cn], PS[pa][:, :cn], AF.Identity, [bPS[pa], bbcol], [btv[j]], bias=bcol[:, 0, mb:mb + 1])
                act(tg[j][:, :cn], PS[pg][:, :cn], AF.Sigmoid, [bPS[pg], bbcol], [btg[j]], bias=bcol[:, 1, mb:mb + 1])
                tt("dve", stg[s3][:, :cn], tv[j][:, :cn], tg[j][:, :cn], ALU.mult, [btv[j], btg[j]], [bstg[s3]])
                P.dma("sp", PXF[U_ + mb * 128:U_ + (mb + 1) * 128, c0:c0 + cn], stg[s3][:, :cn], r=[bstg[s3]], w=[bPXF])
        for g in range(2, NG):
            wi = g % 3
            P.dma("pool", wst[wi][:], kview(w_in[:, grp_cols[g]:grp_cols[g] + 512]), w=[bwst[wi]])
            for mb in range(4):
                for (c0, cn) in chunks:
                    pa = kk[0] % 4
                    kk[0] += 1
                    s3 = kk[1] % 3
                    kk[1] += 1
                    proj(wst[wi], bwst[wi], mb, c0, cn, pa)
                    act(stg[s3][:, :cn], PS[pa][:, :cn], grp_func[g], [bPS[pa], bbcol], [bstg[s3]], bias=bcol[:, g, mb:mb + 1])
                    r0 = grp_rows[g] + mb * 128
                    P.dma("sp", PXF[r0:r0 + 128, c0:c0 + cn], stg[s3][:, :cn], r=[bstg[s3]], w=[bPXF])

        if stop == 'B2':
            raise _Stop(nc, P)
        P.tag = "L%d.G" % l
        P.barrier()
        arena_reset()
        Sst = alloc("Sst", [128, 2, 128], F32)
        Sbf = alloc("Sbf", [128, 2, 128], BF16)
        bS, bSbf = Buf(), Buf()
        g_la4 = [alloc("g_la%d" % i, [128, 256], F32) for i in range(4)]
        g_qk4 = [alloc("g_qk%d" % i, [128, 512], F32) for i in range(4)]
        g_v4 = [alloc("g_v%d" % i, [128, 512], BF16) for i in range(4)]
        bg_la4, bg_qk4, bg_v4 = [[Buf() for _ in range(4)] for _ in range(3)]
        g_e13 = [alloc("g_e13%d" % i, [128, 512], F32) for i in range(2)]
        g_e2 = [alloc("g_e2%d" % i, [128, 256], F32) for i in range(2)]
        g_qin = [alloc("g_qin%d" % i, [128, 256], BF16) for i in range(2)]
        g_kin = [alloc("g_kin%d" % i, [128, 256], BF16) for i in range(2)]
        g_kst = [alloc("g_kst%d" % i, [128, 256], BF16) for i in range(2)]
        g_qpad = [alloc("g_qpad%d" % i, [128, 2, 2, 128], BF16) for i in range(2)]
        g_kT = [alloc("g_kT%d" % i, [128, 2, 128], BF16) for i in range(2)]
        g_dec = [alloc("g_dec%d" % i, [128, 4], F32) for i in range(2)]
        g_att = [alloc("g_att%d" % i, [128, 4, 128], BF16) for i in range(2)]
        g_o = [alloc("g_o%d" % i, [128, 4, 128], F32) for i in range(2)]
        (bg_e13, bg_e2, bg_qin, bg_kin, bg_kst, bg_qpad, bg_kT, bg_dec, bg_att, bg_o) = [[Buf(), Buf()] for _ in range(10)]
        for i in range(2):
            memset("pool", g_qpad[i][:], 0.0, [bg_qpad[i]])
        PT7 = PS[7][:].bitcast(BF16).rearrange("p (a b) -> p a b", a=8)
        n_it = 0
        for dr in (1, 0):
            order = (list(range(TCT)) + list(range(TCT, NTILE))) if dr == 0 else (list(range(TCT - 1, -1, -1)) + list(range(NTILE - 1, TCT - 1, -1)))
            memset("dve", Sst[:], 0.0, [bS])
            memset("pool", Sbf[:], 0.0, [bSbf])
            mcum, mrest, matt = masks[:, 2 * dr, :], masks[:, 2 * dr + 1, :], masks[:, 4 + dr, :]
            for t in order:
                i = n_it % 2
                i4 = n_it % 4
                g_la, g_qk, g_v = {i: g_la4[i4]}, {i: g_qk4[i4]}, {i: g_v4[i4]}
                bg_la, bg_qk, bg_v = {i: bg_la4[i4]}, {i: bg_qk4[i4]}, {i: bg_v4[i4]}
                n_it += 1
                tc_ = slice(t * 128, (t + 1) * 128)
                P.dma("sp", g_la[i][:], GZ[tc_, dr * 256:(dr + 1) * 256], r=[bGZ], w=[bg_la[i]])
                P.dma("sp", g_qk[i][:], GQK[tc_, :], r=[bGQK], w=[bg_qk[i]])
                P.dma("sp", g_v[i][:], GV[tc_, :], r=[bGV], w=[bg_v[i]])
                pc = i
                mm(PS[pc][:, 0:256], mcum, g_la[i][:], True, True, [bCONST, bg_la[i]], [bPS[pc]])
                mm(PS[pc][:, 256:512], mrest, g_la[i][:], True, True, [bCONST, bg_la[i]], [bPS[pc]])
                act(g_e13[i][:], PS[pc][:], AF.Exp, [bPS[pc]], [bg_e13[i]])
                if stop == 'g1':
                    raise _Stop(nc, P)
                act(g_e2[i][:], PS[pc][:, 0:256], AF.Exp, [bPS[pc]], [bg_e2[i]], scale=-1.0)
                stt(g_qin[i][:], g_qk[i][:, 0:256], 0.125, g_e13[i][:, 0:256], ALU.mult, ALU.mult, [bg_qk[i], bg_e13[i]], [bg_qin[i]])
                tt("dve", g_kin[i][:], g_qk[i][:, 256:512], g_e2[i][:], ALU.mult, [bg_qk[i], bg_e2[i]], [bg_kin[i]])
                tt("dve", g_kst[i][:], g_qk[i][:, 256:512], g_e13[i][:, 256:512], ALU.mult, [bg_qk[i], bg_e13[i]], [bg_kst[i]])
                for b2 in range(2):
                    tr(PT7[:, b2, :], g_qin[i][:, b2 * 128:(b2 + 1) * 128], ident_bf[:], [bg_qin[i], bCONST], [bPS[7]])
                for b2 in range(2):
                    tr(PT7[:, 2 + b2, :], g_kin[i][:, b2 * 128:(b2 + 1) * 128], ident_bf[:], [bg_kin[i], bCONST], [bPS[7]])
                cp("act", g_qpad[i][0:64, :, 0, :], PT7[0:64, 0:2, :], [bPS[7]], [bg_qpad[i]])
                cp("act", g_qpad[i][64:128, :, 1, :], PT7[64:128, 0:2, :], [bPS[7]], [bg_qpad[i]])
                cp("act", g_kT[i][:], PT7[:, 2:4, :], [bPS[7]], [bg_kT[i]])
                if stop == 'g2':
                    raise _Stop(nc, P)
                for b2 in range(2):
                    mm(PS[7][:, 256 + 2 * b2:256 + 2 * b2 + 2], g_la[i][:, b2 * 128:(b2 + 1) * 128], neg16[:], True, True, [bg_la[i], bCONST], [bPS[7]])
                act(g_dec[i][:], PS[7][:, 256:260], AF.Exp, [bPS[7]], [bg_dec[i]])
                if stop == 'g3':
                    raise _Stop(nc, P)
                pa = 2 + i
                po_ = 4 + i
                for h in range(4):
                    b2, p0 = h // 2, 64 * (h % 2)
                    mm(PS[pa][:, h * 128:(h + 1) * 128], g_kT[i][:, b2, :], g_qpad[i][:, b2, h % 2, :], True, True,
                       [bg_kT[i], bg_qpad[i]], [bPS[pa]])
                if stop == 'g3b':
                    raise _Stop(nc, P)
                tt("dve", g_att[i][:], PS[pa][:].rearrange("p (h t) -> p h t", h=4), matt.unsqueeze(1).broadcast_to([128, 4, 128]),
                   ALU.mult, [bPS[pa], bCONST], [bg_att[i]])
                if stop == 'g4':
                    raise _Stop(nc, P)
                for h in range(4):
                    b2, p0 = h // 2, 64 * (h % 2)
                    mm(PS[po_][:, h * 128:(h + 1) * 128], g_v[i][:, h * 128:(h + 1) * 128], g_att[i][:, h, :], True, False,
                       [bg_v[i], bg_att[i]], [bPS[po_]])
                    mm(PS[po_][:, h * 128:(h + 1) * 128], Sbf[:, b2, :], g_qpad[i][:, b2, h % 2, :], False, True,
                       [bSbf, bg_qpad[i]], [bPS[po_]])
                cp("act", g_o[i][:], PS[po_][:].rearrange("p (h t) -> p h t", h=4), [bPS[po_]], [bg_o[i]])
                if stop == 'g5':
                    raise _Stop(nc, P)
                P.dma("sp", OBF[1 - dr].rearrange("(h p) n -> p h n", p=128)[:, :, tc_], g_o[i][:], r=[bg_o[i]], w=[bOBF[1 - dr]])
                for h in range(4):
                    b2, p0 = h // 2, 64 * (h % 2)
                    mm(PS[6][:, h * 128:(h + 1) * 128], g_kst[i][:, b2 * 128:(b2 + 1) * 128],
                       g_v[i][:, h * 128:(h + 1) * 128], True, True, [bg_kst[i], bg_v[i]], [bPS[6]])
                for h in range(4):
                    b2, p0 = h // 2, 64 * (h % 2)
                    stt(Sst[p0:p0 + 64, b2, :], Sst[p0:p0 + 64, b2, :], g_dec[i][p0:p0 + 64, 2 * b2:2 * b2 + 1],
                        PS[6][p0:p0 + 64, h * 128:(h + 1) * 128], ALU.mult, ALU.add, [bS, bg_dec[i], bPS[6]], [bS])
                cp("pool", Sbf[:], Sst[:], [bS], [bSbf])
                if stop == 'g6':
                    raise _Stop(nc, P)

        if stop == 'G':
            raise _Stop(nc, P)
        P.tag = "L%d.Cs" % l
        P.barrier()
        arena_reset()
        wco = alloc("wco", [128, 4, D], BF16)
        wgo = alloc("wgo", [128, 4, D], BF16)
        wao = alloc("wao", [64, 8, D], BF16)
        wo = alloc("wo", [128, 8, D], BF16)
        bwc = Buf()
        P.dma("pool", wco[:], kview(Wd["w_conv_out"][l]), w=[bwc])
        P.dma("pool", wgo[:], kview(Wd["w_gla_out"][l]), w=[bwc])
        P.dma("pool", wao[:], Wd["w_attn_out"][l].rearrange("(h p) n -> p h n", p=64), w=[bwc])
        P.dma("pool", wo[:], kview(Wd["w_out"][l]), w=[bwc])
        gateC = alloc("gateC", [128, D], F32)
        bgateC = Buf()
        dwT = alloc("dwT", [32, 512], F32)
        dww = alloc("dww", [128, 4, 32], F32)
        cpar = alloc("cpar", [128, 3, 4], F32)
        gng = alloc("gng", [128, 1], F32)
        diag = alloc("diag", [128, 4, 31, 128], BF16)
        bdwT, bdww, bcpar, bdiag = Buf(), Buf(), Buf(), Buf()
        memset("dve", dwT[:], 0.0, [bdwT])
        P.dma("sp", dwT[0:31, :], Wd["conv_dw_w"][l], w=[bdwT])
        for cb in range(4):
            tr(PS[0][:, cb * 32:cb * 32 + 32], dwT[0:32, cb * 128:(cb + 1) * 128], ident_f[0:32, 0:32], [bdwT, bCONST], [bPS[0]])
        cp("dve", dww[:, :, 0:31], PS[0][:, 0:128].rearrange("p (c j) -> p c j", c=4)[:, :, 0:31], [bPS[0]], [bdww])
        for pi, nm in enumerate(("conv_dw_b", "conv_ln_g", "conv_ln_b")):
            P.dma("sp", cpar[:, pi, :], Wd[nm][l:l + 1, :].rearrange("o (cb p) -> p (o cb)", p=128), w=[bcpar], allow_slow_non_contiguous=True)
        P.dma("sp", gng[:], Wd["gla_norm_g"][l:l + 1, :].rearrange("o n -> n o"), w=[bcpar], allow_slow_non_contiguous=True)
        for cb in range(4):
            tt("dve", diag[:, cb, :, :], ident_f[:].unsqueeze(1).broadcast_to([128, 31, 128]),
               dww[:, cb, 0:31].unsqueeze(2).broadcast_to([128, 31, 128]), ALU.mult, [bCONST, bdww], [bdiag])

        qT = [alloc("c_qT%d" % i, [128, 512], BF16) for i in range(2)]
        bqT = [Buf(), Buf()]
        Ee = [alloc("c_E%d" % i, [128, 512], BF16) for i in range(4)]
        bE = [Buf() for _ in range(4)]
        attg = [alloc("c_attg%d" % i, [64, 2, 512], BF16) for i in range(3)]
        OAT = alloc("c_OAT", [64, 8, 512], BF16)
        rinv = [alloc("c_rinv%d" % i, [65, 512], F32) for i in range(2)]
        og = alloc("c_og", [64, 512], F32)
        battg, bOAT, brinv, bog = [Buf(), Buf(), Buf()], Buf(), [Buf(), Buf()], Buf()
        uh = alloc("c_uh", [128, 4, 512 + 30], BF16)
        cgt = alloc("c_cg", [128, 4, 512], BF16)
        cv = alloc("c_cv", [128, 4, 512], F32)
        csq = alloc("c_csq", [128, 512], F32)
        mean = alloc("c_mean", [128, 512], F32)
        msq = alloc("c_msq", [128, 512], F32)
        rstd = alloc("c_rstd", [128, 512], F32)
        yt = alloc("c_yt", [128, 512], F32)
        yt2 = alloc("c_yt2", [128, 512], F32)
        uo = alloc("c_uo", [128, 4, 512], BF16)
        buh, bcgt, bcv, bcsq, bmean, bmsq, brstd, byt, byt2, buo = [Buf() for _ in range(10)]
        ob = [alloc("c_ob0", [128, 512], F32)] * 2
        of_ = [alloc("c_of0", [128, 512], F32)] * 2
        bob, bof = [Buf()] * 2, [Buf()] * 2
        gsum = mean
        ggt = [alloc("c_gg%d" % i, [128, 512], BF16) for i in range(2)]
        go = alloc("c_go", [128, 4, 512], BF16)
        bgsum, bggt, bgo = bmean, [Buf(), Buf()], Buf()
        mg = [alloc("c_mg0", [128, 3, 512], BF16)] * 2
        bmg = [Buf()] * 2
        mt1, mt2, mt3 = yt, yt2, csq
        bmt1, bmt2, bmt3 = byt, byt2, bcsq
        mT = alloc("c_mT", [128, 8, 512], BF16)
        bmT = Buf()
        xr_ = alloc("c_xr", [128, D], F32)
        xo_ = alloc("c_xo", [128, D], F32)
        bxr, bxo = Buf(), Buf()

        for ci, (c0, cn) in enumerate(chunks):
            is_ctx = ci == 0
            if is_ctx and not need_ctx:
                continue
            cs_ = slice(c0, c0 + cn)
            ktiles = list(range(TCT)) if is_ctx else list(range(NTILE))
            if ci <= 1:
                s_ = 1 if is_ctx else 0
                P.dma("sp", gateC[:], MOD[l, s_:s_ + 1, 2 * D:3 * D].partition_broadcast(128), r=[bMOD], w=[bgateC])
            seq0, seq1 = (0, TC) if is_ctx else (TC, NT)
            nkt = len(ktiles)

            def att_gen():
                P_tag = "L%d.Catt" % l
                its = [(j, kt_i, kt, hh) for j in range(4) for kt_i, kt in enumerate(ktiles) for hh in range(2)]
                n = len(its)
                LA = 1
                pend = []

                def loads(j):
                    qi = j % 2
                    P.dma("sp", qT[qi][:, :cn], QT[j * 128:(j + 1) * 128, cs_], r=[bQT], w=[bqT[qi]])
                    for hh in range(2):
                        r0 = AG_ + (j + 4 * hh) * 64
                        P.dma("sp", attg[j % 3][:, hh, :cn], PXF[r0:r0 + 64, cs_], r=[bPXF], w=[battg[j % 3]])

                for idx in range(n + LA):
                    P.tag = P_tag
                    if idx < n:
                        j, kt_i, kt, hh = its[idx]
                        qi, p0, sb_, eb_ = j % 2, 64 * hh, idx % 2, idx % 4
                        if kt_i == 0 and hh == 0:
                            if j == 0:
                                loads(0)
                            if j + 1 < 4:
                                loads(j + 1)
                        mm(PS[sb_][:, :cn], KT[p0:p0 + 64, kt * 128:(kt + 1) * 128], qT[qi][p0:p0 + 64, :cn], True, True,
                           [bKT[kt], bqT[qi]], [bPS[sb_]])
                        act(Ee[eb_][:, :cn], PS[sb_][:, :cn], AF.Exp, [bPS[sb_]], [bE[eb_]], scale=0.125)
                    if idx >= LA:
                        j, kt_i, kt, hh = its[idx - LA]
                        qi, eb_ = j % 2, (idx - LA) % 4
                        pbk = 4 + 2 * (j % 2) + hh
                        mm(PS[pbk][0:65, :cn], Vext[:, kt, hh, :], Ee[eb_][:, :cn], kt_i == 0, kt_i == nkt - 1,
                           [bV[kt], bE[eb_]], [bPS[pbk]])
                        if kt_i == nkt - 1:
                            rv = rinv[hh]
                            recip(rv[64:65, :cn], PS[pbk][64:65, :cn], [bPS[pbk]], [brinv[hh]])
                            pend.append((idx + 3, j, hh, pbk))
                    while pend and (pend[0][0] <= idx or idx == n + LA - 1):
                        _, j, hh, pbk = pend.pop(0)
                        qi, h8 = j % 2, j + 4 * hh
                        mm(PS[2][0:64, :cn], ones_f[64:65, 0:64], rinv[hh][64:65, :cn], True, True, [bCONST, brinv[hh]], [bPS[2]])
                        tt("dve", og[:, :cn], PS[pbk][0:64, :cn], attg[j % 3][:, hh, :cn], ALU.mult, [bPS[pbk], battg[j % 3]], [bog])
                        tt("dve", OAT[:, h8, :cn], og[:, :cn], PS[2][0:64, :cn], ALU.mult, [bog, bPS[2]], [bOAT])
                    yield

            def side_gen():
                P.tag = "L%d.Cconv" % l
                lo, hi = max(c0 - 15, seq0), min(c0 + cn + 15, seq1)
                if lo > c0 - 15:
                    memset("pool", uh[:, :, 0:15], 0.0, [buh])
                if hi < c0 + cn + 15:
                    memset("pool", uh[:, :, cn + 15:cn + 30], 0.0, [buh])
                P.dma("sp", uh[:, :, lo - (c0 - 15):hi - (c0 - 15)], PXF[U_:U_ + 512, lo:hi].rearrange("(cb p) n -> p cb n", p=128),
                      r=[bPXF], w=[buh])
                P.dma("sp", cgt[:, :, :cn], PXF[CG_:CG_ + 512, cs_].rearrange("(cb p) n -> p cb n", p=128), r=[bPXF], w=[bcgt])
                yield
                for cb in range(4):
                    P.tag = "L%d.Cconv" % l
                    for j in range(31):
                        mm(PS[3][:, :cn], diag[:, cb, j, :], uh[:, cb, j:j + cn], j == 0, j == 30, [bdiag, buh], [bPS[3]])
                    yield
                    P.tag = "L%d.Cconv" % l
                    act(cv[:, cb, :cn], PS[3][:, :cn], AF.Identity, [bPS[3], bcpar], [bcv], bias=cpar[:, 0, cb:cb + 1])
                    yield
                P.tag = "L%d.Cconv" % l
                for cb in range(4):
                    mm(PS[3][:, :cn], ones_f[:], cv[:, cb, :cn], cb == 0, cb == 3, [bCONST, bcv], [bPS[3]])
                yield
                P.tag = "L%d.Cconv" % l
                act(mean[:, :cn], PS[3][:, :cn], AF.Copy, [bPS[3]], [bmean], scale=1.0 / 512)
                tt("dve", msq[:, :cn], mean[:, :cn], mean[:, :cn], ALU.mult, [bmean], [bmsq])
                yield
                for cb in range(4):
                    P.tag = "L%d.Cconv" % l
                    act(csq[:, :cn], cv[:, cb, :cn], AF.Square, [bcv], [bcsq])
                    yield
                    P.tag = "L%d.Cconv" % l
                    mm(PS[3][:, :cn], ones_f[:], csq[:, :cn], cb == 0, cb == 3, [bCONST, bcsq], [bPS[3]])
                    yield
                P.tag = "L%d.Cconv" % l
                stt(rstd[:, :cn], PS[3][:, :cn], 1.0 / 512, msq[:, :cn], ALU.mult, ALU.subtract, [bPS[3], bmsq], [brstd])
                yield
                P.tag = "L%d.Cconv" % l
                act(rstd[:, :cn], rstd[:, :cn], AF.Sqrt, [brstd, bCONST], [brstd], bias=eps_t[:])
                yield
                P.tag = "L%d.Cconv" % l
                recip(rstd[:, :cn], rstd[:, :cn], [brstd], [brstd])
                yield
                for cb in range(4):
                    P.tag = "L%d.Cconv" % l
                    tt("dve", yt[:, :cn], cv[:, cb, :cn], mean[:, :cn], ALU.subtract, [bcv, bmean], [byt])
                    yield
                    P.tag = "L%d.Cconv" % l
                    tt("dve", yt2[:, :cn], yt[:, :cn], rstd[:, :cn], ALU.mult, [byt, brstd], [byt2])
                    yield
                    P.tag = "L%d.Cconv" % l
                    act(yt[:, :cn], yt2[:, :cn], AF.Silu, [byt2, bcpar], [byt], bias=cpar[:, 2, cb:cb + 1], scale=cpar[:, 1, cb:cb + 1])
                    yield
                    P.tag = "L%d.Cconv" % l
                    tt("dve", uo[:, cb, :cn], yt[:, :cn], cgt[:, cb, :cn], ALU.mult, [byt, bcgt], [buo])
                    yield
                for h in range(4):
                    i = h % 2
                    P.tag = "L%d.Cgla" % l
                    P.dma("sp", ggt[i][:, :cn], PXF[GG_ + h * 128:GG_ + (h + 1) * 128, cs_], r=[bPXF], w=[bggt[i]])
                    P.dma("sp", ob[i][:, :cn], OBF[0][h * 128:(h + 1) * 128, cs_], r=[bOBF[0]], w=[bob[i]])
                    P.dma("sp", of_[i][:, :cn], OBF[1][h * 128:(h + 1) * 128, cs_], r=[bOBF[1]], w=[bof[i]])
                    yield
                    P.tag = "L%d.Cgla" % l
                    tt("pool", gsum[:, :cn], ob[i][:, :cn], of_[i][:, :cn], ALU.add, [bob[i], bof[i]], [bgsum])
                    yield
                    P.tag = "L%d.Cgla" % l
                    act(csq[:, :cn], gsum[:, :cn], AF.Square, [bgsum], [bcsq])
                    yield
                    P.tag = "L%d.Cgla" % l
                    mm(PS[3][:, :cn], ones_f[:], csq[:, :cn], True, True, [bCONST, bcsq], [bPS[3]])
                    yield
                    P.tag = "L%d.Cgla" % l
                    act(msq[:, :cn], PS[3][:, :cn], AF.Sqrt, [bPS[3], bCONST], [bmsq], bias=eps_t[:], scale=1.0 / 128)
                    yield
                    P.tag = "L%d.Cgla" % l
                    recip(msq[:, :cn], msq[:, :cn], [bmsq], [bmsq])
                    yield
                    P.tag = "L%d.Cgla" % l
                    tt("dve", yt[:, :cn], gsum[:, :cn], msq[:, :cn], ALU.mult, [bgsum, bmsq], [byt])
                    yield
                    P.tag = "L%d.Cgla" % l
                    stt(go[:, h, :cn], yt[:, :cn], gng[:], ggt[i][:, :cn], ALU.mult, ALU.mult, [byt, bcpar, bggt[i]], [bgo])
                    yield

            gA, gB = att_gen(), side_gen()
            nA = 8 * nkt + 1
            nB = 80
            acc = 0.0
            doneB = False
            for _ in gA:
                acc += float(nB) / nA
                while acc >= 1.0 and not doneB:
                    acc -= 1.0
                    try:
                        next(gB)
                    except StopIteration:
                        doneB = True
            if not doneB:
                for _ in gB:
                    pass
            P.tag = "L%d.Cmrg" % l
            for fb in range(8):
                i = fb % 2
                fs = slice(fb * 128, (fb + 1) * 128)
                P.dma("sp", mg[i][:, :, :cn], PXF[M13_:M13_ + 3 * D, cs_].rearrange("(g f p) n -> p g f n", g=3, f=8, p=128)[:, :, fb, :],
                      r=[bPXF], w=[bmg[i]])
                for cb in range(4):
                    mm(PS[0][:, :cn], wco[:, cb, fs], uo[:, cb, :cn], cb == 0, cb == 3, [bwc, buo], [bPS[0]])
                for h in range(4):
                    mm(PS[1][:, :cn], wgo[:, h, fs], go[:, h, :cn], h == 0, h == 3, [bwc, bgo], [bPS[1]])
                for h8 in range(8):
                    mm(PS[2][:, :cn], wao[:, h8, fs], OAT[:, h8, :cn], h8 == 0, h8 == 7, [bwc, bOAT], [bPS[2]])
                tt("dve", mt1[:, :cn], PS[0][:, :cn], mg[i][:, 0, :cn], ALU.mult, [bPS[0], bmg[i]], [bmt1])
                tt("dve", mt2[:, :cn], PS[1][:, :cn], mg[i][:, 1, :cn], ALU.mult, [bPS[1], bmg[i]], [bmt2])
                tt("dve", mt3[:, :cn], PS[2][:, :cn], mg[i][:, 2, :cn], ALU.mult, [bPS[2], bmg[i]], [bmt3])
                tt("pool", mt1[:, :cn], mt1[:, :cn], mt2[:, :cn], ALU.add, [bmt1, bmt2], [bmt1])
                tt("pool", mT[:, fb, :cn], mt1[:, :cn], mt3[:, :cn], ALU.add, [bmt1, bmt3], [bmT])
            P.tag = "L%d.Cfin" % l
            for s4 in range(cn // 128):
                t = c0 // 128 + s4
                P.dma("sp", xr_[:], src_tile(t), r=[bXS], w=[bxr])
                for hf in range(2):
                    pb = 3 + hf
                    for fb in range(8):
                        mm(PS[pb][:], mT[:, fb, s4 * 128:(s4 + 1) * 128], wo[:, fb, hf * 512:(hf + 1) * 512], fb == 0, fb == 7,
                           [bmT, bwc], [bPS[pb]])
                    tt("dve", xo_[:, hf * 512:(hf + 1) * 512], PS[pb][:], gateC[:, hf * 512:(hf + 1) * 512],
                       ALU.mult, [bPS[pb], bgateC], [bxo])
                tt("pool", xo_[:], xo_[:], xr_[:], ALU.add, [bxo, bxr], [bxo])
                if last:
                    xrow = (t - TCT) * 128
                    final_ops.append(P.dma("sp", out_d[xrow:xrow + 128, :], xo_[:], r=[bxo], w=[bOUT]))
                else:
                    P.dma("sp", XS[t * 128:(t + 1) * 128, :], xo_[:], r=[bxo], w=[Buf()])

    P.emit(final_ops)
    return nc, P


_CACHE = {}


def kernel(**inputs):
    T, TC, DEPTH, B = 4096, 256, 4, 8
    x = np.asarray(inputs["x"], np.float32)
    B, T = x.shape[0], x.shape[1]
    TC = inputs["ctx"].shape[1]
    DEPTH = inputs["w_in"].shape[0]
    key = (T, TC, DEPTH)
    if key not in _CACHE:
        _CACHE[key] = build(T, TC, DEPTH)[0]
    nc = _CACHE[key]
    consts = host_consts(T)
    shared = {n: np.ascontiguousarray(np.asarray(inputs[n], np.float32)) for n, _ in WEIGHT_SPECS}
    shared["c_ctx"] = np.ascontiguousarray(np.asarray(inputs["c_ctx"], np.float32).reshape(1, D))
    shared.update(consts)
    in_maps = []
    for b in range(B):
        m = dict(shared)
        m["x"] = np.ascontiguousarray(x[b])
        m["ctx"] = np.ascontiguousarray(np.asarray(inputs["ctx"], np.float32)[b])
        m["c"] = np.ascontiguousarray(np.asarray(inputs["c"], np.float32)[b:b + 1])
        in_maps.append(m)
    res = run_bass_kernel_spmd(nc, in_maps, core_ids=list(range(B)))
    return np.stack([np.asarray(r["out"], np.float32) for r in res.results], axis=0)
```

```python
import contextlib
import numpy as np
import ml_dtypes
import concourse.bass as bass
import concourse.mybir as mybir
from concourse.bass_utils import run_bass_kernel_spmd

F32 = mybir.dt.float32
BF16 = mybir.dt.bfloat16
AF = mybir.ActivationFunctionType
ALU = mybir.AluOpType
AX = mybir.AxisListType

D = 1024
NIN = 7456
EPS = 1e-6


class Buf:
    __slots__ = ("name", "w", "r", "rd")

    def __init__(self, name=""):
        self.name = name
        self.w = None
        self.r = {}
        self.rd = []


class Op:
    __slots__ = ("eng", "fn", "deps", "needs_inc", "val", "dma", "dsem", "dval", "idx", "tag")

    def __init__(self, eng, fn, dma, idx):
        self.eng = eng
        self.fn = fn
        self.dma = dma
        self.deps = []
        self.needs_inc = False
        self.val = None
        self.dsem = None
        self.dval = None
        self.idx = idx


class Prog:
    ENGS = ("pe", "act", "dve", "pool", "sp")
    NDMA = 56

    def __init__(self, nc):
        self.nc = nc
        self.ops = {e: [] for e in self.ENGS}
        self.all = []
        self.dma_since_barrier = []
        self.tag = ""
        self.annotate = False

    def add(self, eng, fn, r=(), w=(), dma=False):
        op = Op(eng, fn, dma, len(self.all))
        op.tag = self.tag
        edeps = {}
        ddeps = {}

        def dep(x):
            if x is None or x is op:
                return
            if x.dma:
                ddeps[x.idx] = x
            else:
                if x.eng == "pe" and eng == "pe" and not dma:
                    return
                o = edeps.get(x.eng)
                if o is None or o.idx < x.idx:
                    edeps[x.eng] = x

        for b in r:
            dep(b.w)
        for b in w:
            dep(b.w)
            for x in b.r.values():
                dep(x)
            for x in b.rd:
                dep(x)
        for b in w:
            b.w = op
            b.r = {}
            b.rd = []
        for b in r:
            if b.w is not op:
                if dma:
                    b.rd.append(op)
                else:
                    b.r[eng] = op
        op.deps = list(edeps.values()) + list(ddeps.values())
        self.ops[eng].append(op)
        self.all.append(op)
        if dma:
            self.dma_since_barrier.append(op)
        return op

    def dma(self, q, out, in_, r=(), w=(), **kw):
        return self.add(q, lambda e: e.dma_start(out=out, in_=in_, **kw), r=r, w=w, dma=True)

    def barrier(self):
        last = {}
        for e in self.ENGS:
            last[e] = None
            for o in reversed(self.ops[e]):
                if not o.dma and o.fn is not None:
                    last[e] = o
                    break
        dmas = list(self.dma_since_barrier)
        self.dma_since_barrier = []
        for e in self.ENGS:
            op = Op(e, None, False, len(self.all))
            for e2 in self.ENGS:
                if e2 != e and last[e2] is not None:
                    op.deps.append(last[e2])
            op.deps.extend(dmas)
            self.ops[e].append(op)
            self.all.append(op)

    def emit(self, final_wait_ops=()):
        nc = self.nc
        fin = Op("sp", None, False, len(self.all))
        fin.deps = list(final_wait_ops)
        self.ops["sp"].append(fin)
        self.all.append(fin)
        sem_last = [None] * self.NDMA
        sem_cnt = [0] * self.NDMA
        k = 0
        for op in self.all:
            if op.dma:
                s = k % self.NDMA
                k += 1
                if sem_last[s] is not None:
                    op.deps.append(sem_last[s])
                sem_last[s] = op
                sem_cnt[s] += 16
                op.dsem = s
                op.dval = sem_cnt[s]
        for op in self.all:
            for d in op.deps:
                d.needs_inc = True
        cnt = {e: 0 for e in self.ENGS}
        for op in self.all:
            if not op.dma and op.fn is not None:
                if op.needs_inc:
                    cnt[op.eng] += 1
                op.val = cnt[op.eng]
        self.stats = {e: [len(self.ops[e]), cnt[e], 0] for e in self.ENGS}
        with contextlib.ExitStack() as st:
            esem = {e: st.enter_context(nc.semaphore("s_" + e)) for e in self.ENGS}
            dsem = [st.enter_context(nc.semaphore("d%d" % i)) for i in range(self.NDMA)]
            block = st.enter_context(nc.Block())

            def run(ename):
                def body(e):
                    waited = {}
                    nwait = 0
                    for op in self.ops[ename]:
                        for d in op.deps:
                            if d.dma:
                                key, v, sem = ("d", d.dsem), d.dval, dsem[d.dsem]
                            else:
                                if d.fn is None:
                                    continue
                                key, v, sem = ("e", d.eng), d.val, esem[d.eng]
                            if waited.get(key, 0) >= v:
                                continue
                            waited[key] = v
                            e.wait_ge(sem, v)
                            nwait += 1
                        if op.fn is None:
                            continue
                        ins = op.fn(e)
                        if self.annotate:
                            ins.annotate(op.tag)
                        if op.dma:
                            ins.then_inc(dsem[op.dsem], 16)
                        elif op.needs_inc:
                            ins.then_inc(esem[ename], 1)
                    self.stats[ename][2] = nwait

                return body

            block.tensor(run("pe"))
            block.scalar(run("act"))
            block.vector(run("dve"))
            block.gpsimd(run("pool"))
            block.sync(run("sp"))


def run_rr(n_items, make_gen, width):
    free = list(range(width))
    active = []
    nxt = 0
    while nxt < n_items or active:
        while nxt < n_items and free:
            slot = free.pop(0)
            active.append((make_gen(nxt, slot), slot))
            nxt += 1
        for item in list(active):
            try:
                next(item[0])
            except StopIteration:
                active.remove(item)
                free.append(item[1])


def host_consts(T):
    c = {}
    c["ident_bf"] = np.eye(128, dtype=np.float32).astype(ml_dtypes.bfloat16)
    c["ident_f"] = np.eye(128, dtype=np.float32)
    tp = np.arange(128)[:, None]
    t = np.arange(128)[None, :]
    mle = (tp <= t).astype(np.float32)
    mge = (tp >= t).astype(np.float32)
    s = -1.0 / 16.0
    masks = np.stack([s * mle, s * (1 - mle), s * mge, s * (1 - mge), mle, mge], axis=1)
    c["masks"] = np.ascontiguousarray(masks.astype(np.float32))
    c["ones_f"] = np.ones((128, 128), np.float32)
    c["neg16"] = np.full((128, 2), s, np.float32)
    n_rows = T // 64
    row = np.repeat(np.arange(n_rows, dtype=np.float32), 64)
    col = np.tile(np.arange(64, dtype=np.float32), n_rows)
    freqs = (np.float32(10000.0) ** (-np.arange(16, dtype=np.float32) / np.float32(16))).astype(np.float32)
    ar = (row[:, None] * freqs).astype(np.float32)
    ac = (col[:, None] * freqs).astype(np.float32)
    cr, sr, cc, sc = np.cos(ar), np.sin(ar), np.cos(ac), np.sin(ac)
    c["rope_cos"] = np.concatenate([cr, cr, cc, cc], axis=1).astype(np.float32)
    c["rope_sin"] = np.concatenate([-sr, sr, -sc, sc], axis=1).astype(np.float32)
    return c


CONST_SPECS = [("ident_bf", [128, 128], BF16), ("ident_f", [128, 128], F32), ("masks", [128, 6, 128], F32),
               ("ones_f", [128, 128], F32), ("neg16", [128, 2], F32)]

WEIGHT_SPECS = [("norm_g", [D]), ("w_mod", [D, 3 * D]), ("b_mod", [3 * D]), ("w_in", [D, NIN]), ("b_in", [NIN]),
                ("conv_dw_w", [31, 512]), ("conv_dw_b", [512]), ("conv_ln_g", [512]), ("conv_ln_b", [512]),
                ("w_conv_out", [512, D]), ("gla_w_gate", [2, 16, 256]), ("gla_b_gate", [2, 256]),
                ("gla_norm_g", [128]), ("w_gla_out", [512, D]), ("q_norm_g", [64]), ("k_norm_g", [64]),
                ("w_attn_out", [512, D]), ("w_out", [D, D])]


class _Stop(Exception):
    pass


def build(T, TC, DEPTH, debug=False, stop=None, annotate=False):
    try:
        return _build(T, TC, DEPTH, debug, stop, annotate)
    except _Stop as e:
        nc, P = e.args
        P.emit([])
        return nc, P


def _build(T, TC, DEPTH, debug=False, stop=None, annotate=False):
    NT = TC + T
    NTILE = NT // 128
    TCT = TC // 128
    chunks = [(0, TC)] + [(TC + 512 * i, 512) for i in range(T // 512)]

    nc = bass.Bass("TRN2", target_bir_lowering=False)
    P = Prog(nc)
    P.annotate = annotate

    def din(name, shape, dt=F32):
        return nc.dram_tensor(name, shape, dt, kind="ExternalInput").ap()

    x_d = din("x", [T, D])
    ctx_d = din("ctx", [TC, D])
    c_d = din("c", [1, D])
    cctx_d = din("c_ctx", [1, D])
    Wd = {n: din(n, [DEPTH] + s) for n, s in WEIGHT_SPECS}
    Cd = {n: din(n, s, dt) for n, s, dt in CONST_SPECS}
    cos_d = din("rope_cos", [T, 64])
    sin_d = din("rope_sin", [T, 64])
    out_d = nc.dram_tensor("out", [T, D], F32, kind="ExternalOutput").ap()

    skind = "ExternalOutput" if debug else "Internal"

    def dscr(name, shape, dt):
        return nc.dram_tensor(name, shape, dt, kind=skind).ap()

    XS = dscr("XS", [NT, D], F32)
    MOD = dscr("MOD", [DEPTH, 2, 3 * D], F32)
    NPX = 5120
    PXF = dscr("PXF", [NPX, NT], BF16)
    QT = dscr("QT", [512, NT], BF16)
    GQK = dscr("GQK", [NT, 512], F32)
    GV = dscr("GV", [NT, 512], BF16)
    GZ = dscr("GZ", [NT, 512], F32)
    OBF = [dscr("OB", [512, NT], F32), dscr("OF", [512, NT], F32)]
    U_, CG_, GG_, AG_, M13_ = 0, 512, 1024, 1536, 2048
    bXS, bMOD, bPXF, bQT, bGQK, bGV, bGZ = [Buf(n) for n in ("XS", "MOD", "PXF", "QT", "GQK", "GV", "GZ")]
    bOBF = [Buf("OB"), Buf("OF")]
    bOUT = Buf("OUT")

    SB_LO, SB_HI = 16512, 229344
    state = {"persist": SB_LO, "top": SB_LO, "n": 0}

    def alloc(name, shape, dt):
        esz = 2 if dt == BF16 else 4
        nbytes = esz
        for s_ in shape[1:]:
            nbytes *= s_
        nbytes = (nbytes + 63) // 64 * 64
        off = state["top"]
        state["top"] += nbytes
        assert state["top"] <= SB_HI, ("SBUF overflow", name, state["top"])
        state["n"] += 1
        return nc.alloc_sbuf_tensor_at("%s_%d" % (name, state["n"]), shape, dt, offset=off)

    def arena_reset():
        state["top"] = state["persist"]

    PSALL = nc.alloc_psum_tensor("psall", [128, 4096], F32)
    PS = [PSALL[:, i * 512:(i + 1) * 512] for i in range(8)]
    bPS = [Buf("ps%d" % i) for i in range(8)]

    def mm(out, lhsT, rhs, start, stop, r, w):
        return P.add("pe", lambda e: e.matmul(out, lhsT=lhsT, rhs=rhs, start=start, stop=stop), r=r, w=w)

    def tr(out, in_, ident, r, w):
        return P.add("pe", lambda e: e.transpose(out=out, in_=in_, identity=ident), r=r, w=w)

    def act(out, in_, func, r, w, bias=None, scale=None, accum=None):
        kw = {}
        if bias is not None:
            kw["bias"] = bias
        if scale is not None:
            kw["scale"] = scale
        if accum is not None:
            kw["accum_out"] = accum
        return P.add("act", lambda e: e.activation(out=out, in_=in_, func=func, **kw), r=r, w=w)

    def tt(eng, out, in0, in1, op, r, w):
        return P.add(eng, lambda e: e.tensor_tensor(out=out, in0=in0, in1=in1, op=op), r=r, w=w)

    def ts(eng, out, in0, s1, s2, op0, op1, r, w):
        return P.add(eng, lambda e: e.tensor_scalar(out=out, in0=in0, scalar1=s1, scalar2=s2, op0=op0, op1=op1), r=r, w=w)

    def stt(out, in0, scalar, in1, op0, op1, r, w):
        return P.add("dve", lambda e: e.scalar_tensor_tensor(out=out, in0=in0, scalar=scalar, in1=in1, op0=op0, op1=op1), r=r, w=w)

    def cp(eng, out, in_, r, w):
        if eng == "act":
            return P.add("act", lambda e: e.copy(out=out, in_=in_), r=r, w=w)
        return P.add(eng, lambda e: e.tensor_copy(out=out, in_=in_), r=r, w=w)

    def memset(eng, ap, v, w):
        return P.add(eng, lambda e: e.memset(ap, v), w=w)

    def recip(out, in_, r, w):
        return P.add("dve", lambda e: e.reciprocal(out=out, in_=in_), r=r, w=w)

    def kview(w2d):
        return w2d.rearrange("(kb p) n -> p kb n", p=128)

    ident_bf = alloc("ident_bf", [128, 128], BF16)
    ident_f = alloc("ident_f", [128, 128], F32)
    masks = alloc("masks", [128, 6, 128], F32)
    ones_f = alloc("ones_f", [128, 128], F32)
    neg16 = alloc("neg16", [128, 2], F32)
    eps_t = alloc("eps", [128, 1], F32)
    nhalf = alloc("nhalf", [128, 512], F32)
    bCONST = Buf("const")
    for n, t_ in (("ident_bf", ident_bf), ("ident_f", ident_f), ("masks", masks), ("ones_f", ones_f), ("neg16", neg16)):
        P.dma("sp", t_[:], Cd[n], w=[bCONST])
    memset("dve", eps_t[:], EPS, [bCONST])
    memset("dve", nhalf[:], -0.5, [bCONST])
    KT = alloc("KT", [128, NT], BF16)
    Vext = alloc("Vext", [128, NTILE, 2, 65], BF16)
    bKT = [Buf("KT%d" % i) for i in range(NTILE)]
    bV = [Buf("V%d" % i) for i in range(NTILE)]
    memset("pool", Vext[:, :, :, 64:65], 1.0, bV)
    state["persist"] = state["top"]

    P.tag = "0"
    arena_reset()
    cc = alloc("cc", [128, 2, 8], F32)
    sc = alloc("sc", [128, 2, 8], F32)
    bm = alloc("bm", [2, 3 * D], F32)
    ng = alloc("ng", [2, D], F32)
    modsb = alloc("modsb", [2, 3 * D], F32)
    wm = [alloc("wm%d" % i, [128, 8, 512], F32) for i in range(2)]
    bcc, bsc, bbm, bng, bmodsb = Buf(), Buf(), Buf(), Buf(), Buf()
    bwm = [Buf(), Buf()]
    P.dma("sp", cc[:, 0, :], c_d.rearrange("o (kb p) -> p (o kb)", p=128), w=[bcc], allow_slow_non_contiguous=True)
    P.dma("sp", cc[:, 1, :], cctx_d.rearrange("o (kb p) -> p (o kb)", p=128), w=[bcc], allow_slow_non_contiguous=True)
    act(sc[:], cc[:], AF.Silu, [bcc], [bsc])
    k = 0
    for l in range(DEPTH):
        P.dma("sp", bm[:], Wd["b_mod"][l:l + 1, :].partition_broadcast(2), w=[bbm])
        P.dma("sp", ng[:], Wd["norm_g"][l:l + 1, :].partition_broadcast(2), w=[bng])
        for ch in range(6):
            wb_ = wm[k % 2]
            bw_ = bwm[k % 2]
            pb = k % 2
            k += 1
            P.dma("sp", wb_[:], kview(Wd["w_mod"][l][:, ch * 512:(ch + 1) * 512]), w=[bw_])
            for kb in range(8):
                mm(PS[pb][0:2, :], sc[:, :, kb], wb_[:, kb, :], kb == 0, kb == 7, [bsc, bw_], [bPS[pb]])
            tt("dve", modsb[:, ch * 512:(ch + 1) * 512], PS[pb][0:2, :], bm[:, ch * 512:(ch + 1) * 512], ALU.add,
               [bPS[pb], bbm], [bmodsb])
        stt(modsb[:, D:2 * D], modsb[:, D:2 * D], 1.0, ng[:], ALU.add, ALU.mult, [bmodsb, bng], [bmodsb])
        P.dma("sp", MOD[l], modsb[:], r=[bmodsb], w=[bMOD])

    if stop == '0':
        raise _Stop(nc, P)
    final_ops = []
    for l in range(DEPTH):
        need_ctx = l < DEPTH - 1
        last = l == DEPTH - 1
        w_in = Wd["w_in"][l]
        b_in = Wd["b_in"]

        def src_tile(t):
            if l == 0:
                return ctx_d[t * 128:(t + 1) * 128, :] if t < TCT else x_d[(t - TCT) * 128:(t - TCT + 1) * 128, :]
            return XS[t * 128:(t + 1) * 128, :]

        P.tag = "L%d.A" % l
        P.barrier()
        arena_reset()
        hxT = alloc("hxT", [128, 8, NT], BF16)
        mark_hx = state["top"]
        bhx = [Buf("hx%d" % i) for i in range(NTILE)]
        wlr = alloc("wlr", [128, 8, 32], BF16)
        blr = alloc("blr", [32, 1], F32)
        lrT = alloc("lrT", [33, NT], BF16)
        w_tm = alloc("w_tm", [128, 8, 1792], BF16)
        bias_tm = alloc("bias_tm", [128, 1792], F32)
        Gq = alloc("Gq", [128, 64], F32)
        Gk = alloc("Gk", [128, 64], F32)
        wupf = alloc("wupf", [33, 512], F32)
        wup = alloc("wup", [33, 512], BF16)
        bwlr, bblr, blrT, bwtm, bbtm, bG, bwupf, bwup = [Buf() for _ in range(8)]
        P.dma("pool", wlr[:], kview(w_in[:, 3072:3104]), w=[bwlr])
        P.dma("sp", blr[:], b_in[l:l + 1, 3072:3104].rearrange("o n -> n o"), w=[bblr], allow_slow_non_contiguous=True)
        memset("pool", lrT[32:33, :], 1.0, [blrT])
        tm_cols = [(1536, 2048, 0), (2048, 2560, 512), (3104, 3616, 1024), (3616, 3872, 1536)]
        for a, b, o in tm_cols:
            P.dma("pool", w_tm[:, :, o:o + (b - a)], kview(w_in[:, a:b]), w=[bwtm])
            P.dma("sp", bias_tm[:, o:o + (b - a)], b_in[l:l + 1, a:b].partition_broadcast(128), w=[bbtm])
        P.dma("sp", Gq[:], Wd["q_norm_g"][l:l + 1, :].partition_broadcast(128), w=[bG])
        P.dma("sp", Gk[:], Wd["k_norm_g"][l:l + 1, :].partition_broadcast(128), w=[bG])
        memset("dve", wupf[:], 0.0, [bwupf])
        P.dma("sp", wupf[0:16, 0:256], Wd["gla_w_gate"][l, 0], w=[bwupf])
        P.dma("sp", wupf[16:32, 256:512], Wd["gla_w_gate"][l, 1], w=[bwupf])
        P.dma("sp", wupf[32:33, 0:256], Wd["gla_b_gate"][l, 0:1, :], w=[bwupf])
        P.dma("sp", wupf[32:33, 256:512], Wd["gla_b_gate"][l, 1:2, :], w=[bwupf])
        cp("dve", wup[:], wupf[:], [bwupf], [bwup])

        mark_A = state["top"]
        modA = alloc("modA", [128, 2, 2, D], F32)
        bmodA = Buf()
        for s_ in range(2):
            P.dma("sp", modA[:, s_, 0, :], MOD[l, s_:s_ + 1, D:2 * D].partition_broadcast(128), r=[bMOD], w=[bmodA])
            P.dma("sp", modA[:, s_, 1, :], MOD[l, s_:s_ + 1, 0:D].partition_broadcast(128), r=[bMOD], w=[bmodA])
        WA = 3
        xt = [alloc("xt%d" % i, [128, D], F32) for i in range(WA)]
        sqj = [alloc("sqj%d" % i, [128, D], BF16) for i in range(WA)]
        junk = [alloc("junk%d" % i, [128, D], F32) for i in range(WA)]
        hb = [alloc("hb%d" % i, [128, D], BF16) for i in range(WA)]
        ssA = alloc("ssA", [128, WA], F32)
        bxt, bsqj, bjunk, bhb, bss = [[Buf() for _ in range(WA)] for _ in range(5)]
        PTvs = [PS[5 + i].bitcast(BF16).rearrange("p (a b) -> p a b", a=8) for i in range(3)]

        def genA(t, i):
            s_ = 0 if t >= TCT else 1
            pb = 5 + i
            P.dma("sp", xt[i][:], src_tile(t), r=[bXS], w=[bxt[i]])
            yield
            act(sqj[i][:], xt[i][:], AF.Square, [bxt[i]], [bsqj[i], bss[i]], accum=ssA[:, i:i + 1])
            yield
            act(ssA[:, i:i + 1], ssA[:, i:i + 1], AF.Ln, [bss[i], bCONST], [bss[i]], bias=eps_t[:], scale=1.0 / D)
            yield
            act(ssA[:, i:i + 1], ssA[:, i:i + 1], AF.Exp, [bss[i]], [bss[i]], scale=-0.5)
            yield
            stt(junk[i][:], xt[i][:], ssA[:, i:i + 1], modA[:, s_, 0, :], ALU.mult, ALU.mult, [bxt[i], bss[i], bmodA], [bjunk[i]])
            yield
            tt("dve", hb[i][:], junk[i][:], modA[:, s_, 1, :], ALU.add, [bjunk[i], bmodA], [bhb[i]])
            yield
            for kb in range(8):
                tr(PTvs[i][:, kb, :], hb[i][:, kb * 128:(kb + 1) * 128], ident_bf[:], [bhb[i], bCONST], [bPS[pb]])
            yield
            cp("act", hxT[:, :, t * 128:(t + 1) * 128], PTvs[i][:], [bPS[pb]], [bhx[t]])
            yield

        run_rr(NTILE, genA, WA)
        P.barrier()
        state["top"] = mark_A

        if stop == 'A':
            raise _Stop(nc, P)
        P.tag = "L%d.B0" % l
        for ci, (c0, cn) in enumerate(chunks):
            pb = ci % 2
            tl = list(range(c0 // 128, (c0 + cn) // 128))
            for kb in range(8):
                mm(PS[pb][0:32, :cn], wlr[:, kb, :], hxT[:, kb, c0:c0 + cn], kb == 0, kb == 7,
                   [bwlr] + [bhx[t] for t in tl], [bPS[pb]])
            act(lrT[0:32, c0:c0 + cn], PS[pb][0:32, :cn], AF.Identity, [bPS[pb], bblr], [blrT], bias=blr[:])

        if stop == 'B0':
            raise _Stop(nc, P)
        P.tag = "L%d.B1" % l
        WB = 2
        qk_sb = [alloc("qk_sb%d" % i, [128, 512], F32) for i in range(WB)]
        v_sb = [alloc("v_sb%d" % i, [128, 512], BF16) for i in range(WB)]
        la_e = [alloc("la_e%d" % i, [128, 512], F32) for i in range(WB)]
        la_sb = [alloc("la_sb%d" % i, [128, 512], F32) for i in range(WB)]
        qT_sb = [alloc("qT_sb%d" % i, [128, 4, 128], BF16) for i in range(WB)]
        nq_o = [alloc("nq_o%d" % i, [128, 512], BF16) for i in range(WB)]
        nk_o = [alloc("nk_o%d" % i, [128, 128], BF16) for i in range(WB)]
        cs_t = [alloc("cs_t%d" % i, [128, 2, 64], F32) for i in range(WB)]
        bqk_sb, bv_sb, bla_e, bla_sb, bqT_sb, bnq_o, bnk_o, bcs = [[Buf() for _ in range(WB)] for _ in range(8)]
        nscr = {}
        for i in range(WB):
            for nm, wdt in (("q", 512), ("k", 128)):
                nscr[(nm, i)] = ([alloc("ns%s%d_%d" % (nm, i, z), [128, wdt], F32) for z in range(4)] + [alloc("nss%s%d" % (nm, i), [128, 8], F32)],
                                 [Buf() for _ in range(5)])

        def norm_rope(src, bsrc, nh, G, boff, rope_i, out, bout, scr, perm=False):
            n = nh * 64
            tl_, bl_ = nscr[scr]
            b_, sq_, a_, r_ = [x[:, :n] for x in tl_[:4]]
            ss_ = tl_[4][:, :nh]
            Bb, Bsq, Ba, Br, Bss = bl_

            def v3(ap):
                return ap.rearrange("p (h d) -> p h d", h=nh)

            if perm:
                def vin(ap):
                    return ap.rearrange("p (a j d) -> p a j d", a=2, j=4)
                vout = out.rearrange("p (j a d) -> p a j d", a=2, j=4)
                rsb = ss_.rearrange("p (a j) -> p a j", a=2).unsqueeze(3).broadcast_to([128, 2, 4, 64])
            else:
                vin = v3
                vout = v3(out)
                rsb = ss_.unsqueeze(2).broadcast_to([128, nh, 64])

            tt("dve", b_, src, bias_tm[:, boff:boff + n], ALU.add, [bsrc, bbtm], [Bb])
            yield
            act(sq_, b_, AF.Square, [Bb], [Bsq])
            yield
            P.add("dve", lambda e: e.tensor_reduce(out=ss_, in_=v3(sq_), axis=AX.X, op=ALU.add), r=[Bsq], w=[Bss])
            yield
            act(ss_, ss_, AF.Ln, [Bss, bCONST], [Bss], bias=eps_t[:], scale=1.0 / 64)
            yield
            act(ss_, ss_, AF.Exp, [Bss], [Bss], scale=-0.5)
            yield
            tt("dve", v3(b_), v3(b_), G[:].unsqueeze(1).broadcast_to([128, nh, 64]), ALU.mult, [Bb, bG], [Bb])
            yield
            if rope_i is None:
                tt("dve", vout, vin(b_), rsb, ALU.mult, [Bb, Bss], [bout])
                yield
                return
            tt("dve", v3(b_), v3(b_), ss_.unsqueeze(2).broadcast_to([128, nh, 64]), ALU.mult, [Bb, Bss], [Bb])
            yield
            cs = cs_t[rope_i]
            tt("dve", v3(a_), v3(b_), cs[:, 0, :].unsqueeze(1).broadcast_to([128, nh, 64]), ALU.mult, [Bb, bcs[rope_i]], [Ba])
            yield

            def v5(ap):
                return ap.rearrange("p (h a b c) -> p h a b c", h=nh, a=2, b=2)

            sn5 = cs[:, 1, :].rearrange("p (a b c) -> p a b c", a=2, b=2)
            for hf in range(2):
                tt("dve", v5(r_)[:, :, :, hf, :], v5(b_)[:, :, :, 1 - hf, :],
                   sn5[:, :, hf, :].unsqueeze(1).broadcast_to([128, nh, 2, 16]), ALU.mult, [Bb, bcs[rope_i]], [Br])
                yield
            tt("dve", vout, vin(a_), vin(r_), ALU.add, [Ba, Br], [bout])
            yield

        PT6 = PS[6].bitcast(BF16)

        def genB1(t, i):
            bA, bB, bC, bD = 4 * i, 4 * i + 1, 4 * i + 2, 4 * i + 3
            PTk = PS[bB].bitcast(BF16)
            PTq = PS[bC].bitcast(BF16).rearrange("p (a b) -> p a b", a=8)
            tc_ = slice(t * 128, (t + 1) * 128)
            is_x = t >= TCT
            if is_x:
                xr = (t - TCT) * 128
                P.dma("sp", cs_t[i][:, 0, :], cos_d[xr:xr + 128, :], w=[bcs[i]])
                P.dma("sp", cs_t[i][:, 1, :], sin_d[xr:xr + 128, :], w=[bcs[i]])
            for kb in range(8):
                mm(PS[bA], hxT[:, kb, tc_], w_tm[:, kb, 0:512], kb == 0, kb == 7, [bhx[t], bwtm], [bPS[bA]])
            yield
            tt("dve", qk_sb[i][:], PS[bA], bias_tm[:, 0:512], ALU.add, [bPS[bA], bbtm], [bqk_sb[i]])
            P.dma("sp", GQK[tc_, :], qk_sb[i][:], r=[bqk_sb[i]], w=[bGQK])
            yield
            for kb in range(8):
                mm(PS[bB], hxT[:, kb, tc_], w_tm[:, kb, 512:1024], kb == 0, kb == 7, [bhx[t], bwtm], [bPS[bB]])
            yield
            tt("dve", v_sb[i][:], PS[bB], bias_tm[:, 512:1024], ALU.add, [bPS[bB], bbtm], [bv_sb[i]])
            P.dma("sp", GV[tc_, :], v_sb[i][:], r=[bv_sb[i]], w=[bGV])
            yield
            mm(PS[bA], lrT[0:33, tc_], wup[0:33, :], True, True, [blrT, bwup], [bPS[bA]])
            yield
            act(la_e[i][:], PS[bA], AF.Exp, [bPS[bA]], [bla_e[i]], scale=-1.0)
            yield
            for kb in range(8):
                mm(PS[bD][:, 0:256], hxT[:, kb, tc_], w_tm[:, kb, 1536:1792], kb == 0, kb == 7, [bhx[t], bwtm], [bPS[bD]])
            yield
            tt("dve", Vext[:, t, :, 0:64], PS[bD][:, 128:256].rearrange("p (h d) -> p h d", h=2),
               bias_tm[:, 1664:1792].rearrange("p (h d) -> p h d", h=2), ALU.add, [bPS[bD], bbtm], [bV[t]])
            yield
            act(la_sb[i][:], la_e[i][:], AF.Ln, [bla_e[i]], [bla_sb[i]], bias=1.0)
            P.dma("sp", GZ[tc_, :], la_sb[i][:], r=[bla_sb[i]], w=[bGZ])
            yield
            do_q = is_x or need_ctx
            if do_q:
                for kb in range(8):
                    mm(PS[bC], hxT[:, kb, tc_], w_tm[:, kb, 1024:1536], kb == 0, kb == 7, [bhx[t], bwtm], [bPS[bC]])
                yield
            gk = norm_rope(PS[bD][:, 0:128], bPS[bD], 2, Gk, 1536, i if is_x else None, nk_o[i][:], bnk_o[i], ("k", i))
            gq = norm_rope(PS[bC], bPS[bC], 8, Gq, 1024, i if is_x else None, nq_o[i][:], bnq_o[i], ("q", i), perm=True) if do_q else iter(())
            alive = [gk, gq]
            while alive:
                for g in list(alive):
                    try:
                        next(g)
                    except StopIteration:
                        alive.remove(g)
                yield
            tr(PTk[:, 0:128], nk_o[i][:], ident_bf[:], [bnk_o[i], bCONST], [bPS[bB]])
            yield
            cp("act", KT[:, tc_], PTk[:, 0:128], [bPS[bB]], [bKT[t]])
            yield
            if do_q:
                for j in range(4):
                    tr(PTq[:, j, :], nq_o[i][:, j * 128:(j + 1) * 128], ident_bf[:], [bnq_o[i], bCONST], [bPS[bC]])
                yield
                cp("act", qT_sb[i][:], PTq[:, 0:4, :], [bPS[bC]], [bqT_sb[i]])
                P.dma("sp", QT.rearrange("(j p) n -> p j n", p=128)[:, :, tc_], qT_sb[i][:], r=[bqT_sb[i]], w=[bQT])
                yield

        def genB1_tagged(t, i):
            for _ in genB1(t, i):
                P.tag = "L%d.B1" % l
                yield

        run_rr(NTILE, genB1_tagged, WB)

        if stop == 'B1':
            raise _Stop(nc, P)
        P.tag = "L%d.B2" % l
        P.barrier()
        state["top"] = mark_hx
        NG = 11
        grp_cols = [0, 512, 1024, 2560, 3872] + [4384 + 512 * g for g in range(6)]
        grp_rows = [None, U_, CG_, GG_, AG_] + [M13_ + 512 * g for g in range(6)]
        grp_func = [AF.Identity, AF.Sigmoid, AF.Silu, AF.Silu, AF.Silu] + [AF.Sigmoid] * 6
        bcol = alloc("bcol", [128, NG, 4], F32)
        bbcol = Buf()
        for g in range(NG):
            P.dma("sp", bcol[:, g, :], b_in[l:l + 1, grp_cols[g]:grp_cols[g] + 512].rearrange("o (nb p) -> p (o nb)", p=128),
                  w=[bbcol], allow_slow_non_contiguous=True)
        wst = [alloc("wst%d" % i, [128, 8, 512], BF16) for i in range(3)]
        bwst = [Buf() for _ in range(3)]
        stg = [alloc("stg%d" % i, [128, 512], BF16) for i in range(3)]
        bstg = [Buf() for _ in range(3)]
        tv = [alloc("tv%d" % i, [128, 512], F32) for i in range(2)]
        tg = [alloc("tg%d" % i, [128, 512], F32) for i in range(2)]
        btv, btg = [Buf(), Buf()], [Buf(), Buf()]
        kk = [0, 0]

        def proj(wbuf, bw, mb, c0, cn, pb):
            tl = [bhx[t] for t in range(c0 // 128, (c0 + cn) // 128)]
            for kb in range(8):
                mm(PS[pb][:, :cn], wbuf[:, kb, mb * 128:(mb + 1) * 128], hxT[:, kb, c0:c0 + cn], kb == 0, kb == 7,
                   [bw] + tl, [bPS[pb]])

        P.dma("pool", wst[0][:], kview(w_in[:, 0:512]), w=[bwst[0]])
        P.dma("pool", wst[1][:], kview(w_in[:, 512:1024]), w=[bwst[1]])
        for mb in range(4):
            for (c0, cn) in chunks:
                pa, pg = kk[0] % 4, (kk[0] + 1) % 4
                kk[0] += 2
                j = kk[1] % 2
                s3 = kk[1] % 3
                kk[1] += 1
                proj(wst[0], bwst[0], mb, c0, cn, pa)
                proj(wst[1], bwst[1], mb, c0, cn, pg)
                act(tv[j][:, :cn], PS[pa][:, :cn], AF.Identity, [bPS[pa], bbcol], [btv[j]], bias=bcol[:, 0, mb:mb + 1])
                act(tg[j][:, :cn], PS[pg][:, :cn], AF.Sigmoid, [bPS[pg], bbcol], [btg[j]], bias=bcol[:, 1, mb:mb + 1])
                tt("dve", stg[s3][:, :cn], tv[j][:, :cn], tg[j][:, :cn], ALU.mult, [btv[j], btg[j]], [bstg[s3]])
                P.dma("sp", PXF[U_ + mb * 128:U_ + (mb + 1) * 128, c0:c0 + cn], stg[s3][:, :cn], r=[bstg[s3]], w=[bPXF])
        for g in range(2, NG):
            wi = g % 3
            P.dma("pool", wst[wi][:], kview(w_in[:, grp_cols[g]:grp_cols[g] + 512]), w=[bwst[wi]])
            for mb in range(4):
                for (c0, cn) in chunks:
                    pa = kk[0] % 4
                    kk[0] += 1
                    s3 = kk[1] % 3
                    kk[1] += 1
                    proj(wst[wi], bwst[wi], mb, c0, cn, pa)
                    act(stg[s3][:, :cn], PS[pa][:, :cn], grp_func[g], [bPS[pa], bbcol], [bstg[s3]], bias=bcol[:, g, mb:mb + 1])
                    r0 = grp_rows[g] + mb * 128
                    P.dma("sp", PXF[r0:r0 + 128, c0:c0 + cn], stg[s3][:, :cn], r=[bstg[s3]], w=[bPXF])

        if stop == 'B2':
            raise _Stop(nc, P)
        P.tag = "L%d.G" % l
        P.barrier()
        arena_reset()
        WG = 2
        Sst = alloc("Sst", [128, 2, 128], F32)
        Sbf = [alloc("Sbf%d" % i, [128, 2, 128], BF16) for i in range(2)]
        bS, bSbf = Buf(), [Buf(), Buf()]
        g_la = [alloc("g_la%d" % i, [128, 256], F32) for i in range(WG)]
        g_qk = [alloc("g_qk%d" % i, [128, 512], F32) for i in range(WG)]
        g_v = [alloc("g_v%d" % i, [128, 512], BF16) for i in range(WG)]
        g_e13 = [alloc("g_e13%d" % i, [128, 512], F32) for i in range(WG)]
        g_e2 = [alloc("g_e2%d" % i, [128, 256], F32) for i in range(WG)]
        g_qin = [alloc("g_qin%d" % i, [128, 256], BF16) for i in range(WG)]
        g_kin = [alloc("g_kin%d" % i, [128, 256], BF16) for i in range(WG)]
        g_kst = [alloc("g_kst%d" % i, [128, 2, 2, 128], BF16) for i in range(WG)]
        g_qpad = [alloc("g_qpad%d" % i, [128, 2, 2, 128], BF16) for i in range(WG)]
        g_kT = [alloc("g_kT%d" % i, [128, 2, 128], BF16) for i in range(WG)]
        g_dec = [alloc("g_dec%d" % i, [128, 4], F32) for i in range(WG)]
        g_att = [alloc("g_att%d" % i, [128, 4, 128], BF16) for i in range(WG)]
        g_o = [alloc("g_o%d" % i, [128, 4, 128], F32) for i in range(WG)]
        (bg_la, bg_qk, bg_v, bg_e13, bg_e2, bg_qin, bg_kin, bg_kst, bg_qpad, bg_kT, bg_dec, bg_att, bg_o) = [[Buf() for _ in range(WG)] for _ in range(13)]
        for i in range(WG):
            memset("pool", g_qpad[i][:], 0.0, [bg_qpad[i]])
            memset("pool", g_kst[i][:], 0.0, [bg_kst[i]])

        def prepG(t, i, dr):
            mcum, mrest, matt = masks[:, 2 * dr, :], masks[:, 2 * dr + 1, :], masks[:, 4 + dr, :]
            tc_ = slice(t * 128, (t + 1) * 128)
            pc, pa, pcn = i, 2 + i, 6 + i
            PTc = PS[pc].bitcast(BF16).rearrange("p (a b) -> p a b", a=8)
            P.dma("sp", g_la[i][:], GZ[tc_, dr * 256:(dr + 1) * 256], r=[bGZ], w=[bg_la[i]])
            P.dma("sp", g_qk[i][:], GQK[tc_, :], r=[bGQK], w=[bg_qk[i]])
            P.dma("sp", g_v[i][:], GV[tc_, :], r=[bGV], w=[bg_v[i]])
            yield
            mm(PS[pc][:, 0:256], mcum, g_la[i][:], True, True, [bCONST, bg_la[i]], [bPS[pc]])
            mm(PS[pc][:, 256:512], mrest, g_la[i][:], True, True, [bCONST, bg_la[i]], [bPS[pc]])
            yield
            act(g_e13[i][:], PS[pc], AF.Exp, [bPS[pc]], [bg_e13[i]])
            yield
            act(g_e2[i][:], PS[pc][:, 0:256], AF.Exp, [bPS[pc]], [bg_e2[i]], scale=-1.0)
            yield
            stt(g_qin[i][:], g_qk[i][:, 0:256], 0.125, g_e13[i][:, 0:256], ALU.mult, ALU.mult, [bg_qk[i], bg_e13[i]], [bg_qin[i]])
            yield
            tt("dve", g_kin[i][:], g_qk[i][:, 256:512], g_e2[i][:], ALU.mult, [bg_qk[i], bg_e2[i]], [bg_kin[i]])
            yield
            for hh in range(2):
                tt("dve", g_kst[i][:, :, hh, hh * 64:(hh + 1) * 64],
                   g_qk[i][:, 256:512].rearrange("p (b h d) -> p b h d", b=2, h=2)[:, :, hh, :],
                   g_e13[i][:, 256:512].rearrange("p (b h d) -> p b h d", b=2, h=2)[:, :, hh, :], ALU.mult,
                   [bg_qk[i], bg_e13[i]], [bg_kst[i]])
                yield
            for b2 in range(2):
                tr(PTc[:, b2, :], g_qin[i][:, b2 * 128:(b2 + 1) * 128], ident_bf[:], [bg_qin[i], bCONST], [bPS[pc]])
            for b2 in range(2):
                tr(PTc[:, 2 + b2, :], g_kin[i][:, b2 * 128:(b2 + 1) * 128], ident_bf[:], [bg_kin[i], bCONST], [bPS[pc]])
            for b2 in range(2):
                mm(PS[pc][:, 256 + 2 * b2:256 + 2 * b2 + 2], g_la[i][:, b2 * 128:(b2 + 1) * 128], neg16[:], True, True, [bg_la[i], bCONST], [bPS[pc]])
            yield
            cp("act", g_qpad[i][0:64, :, 0, :], PTc[0:64, 0:2, :], [bPS[pc]], [bg_qpad[i]])
            yield
            cp("act", g_qpad[i][64:128, :, 1, :], PTc[64:128, 0:2, :], [bPS[pc]], [bg_qpad[i]])
            yield
            cp("act", g_kT[i][:], PTc[:, 2:4, :], [bPS[pc]], [bg_kT[i]])
            yield
            act(g_dec[i][:], PS[pc][:, 256:260], AF.Exp, [bPS[pc]], [bg_dec[i]])
            yield
            for h in range(4):
                b2 = h // 2
                mm(PS[pa][:, h * 128:(h + 1) * 128], g_kT[i][:, b2, :], g_qpad[i][:, b2, h % 2, :], True, True,
                   [bg_kT[i], bg_qpad[i]], [bPS[pa]])
            yield
            tt("dve", g_att[i][:], PS[pa].rearrange("p (h t) -> p h t", h=4), matt.unsqueeze(1).broadcast_to([128, 4, 128]),
               ALU.mult, [bPS[pa], bCONST], [bg_att[i]])
            yield
            for b2 in range(2):
                for hh in range(2):
                    h = 2 * b2 + hh
                    mm(PS[pcn][:, b2 * 128:(b2 + 1) * 128], g_kst[i][:, b2, hh, :], g_v[i][:, h * 128:(h + 1) * 128],
                       hh == 0, hh == 1, [bg_kst[i], bg_v[i]], [bPS[pcn]])
            yield

        def stateG(t, i, dr, seq):
            tc_ = slice(t * 128, (t + 1) * 128)
            po_, pcn = 4 + i, 6 + i
            sprev, snew = Sbf[(seq + 1) % 2], Sbf[seq % 2]
            bprev, bnew = bSbf[(seq + 1) % 2], bSbf[seq % 2]
            for h in range(4):
                b2 = h // 2
                mm(PS[po_][:, h * 128:(h + 1) * 128], g_v[i][:, h * 128:(h + 1) * 128], g_att[i][:, h, :], True, False,
                   [bg_v[i], bg_att[i]], [bPS[po_]])
                mm(PS[po_][:, h * 128:(h + 1) * 128], sprev[:, b2, :], g_qpad[i][:, b2, h % 2, :], False, True,
                   [bprev, bg_qpad[i]], [bPS[po_]])
            cp("act", g_o[i][:], PS[po_].rearrange("p (h t) -> p h t", h=4), [bPS[po_]], [bg_o[i]])
            P.dma("sp", OBF[1 - dr].rearrange("(h p) n -> p h n", p=128)[:, :, tc_], g_o[i][:], r=[bg_o[i]], w=[bOBF[1 - dr]])
            for b2 in range(2):
                stt(Sst[:, b2, :], Sst[:, b2, :], g_dec[i][:, 2 * b2:2 * b2 + 1], PS[pcn][:, b2 * 128:(b2 + 1) * 128],
                    ALU.mult, ALU.add, [bS, bg_dec[i], bPS[pcn]], [bS])
            cp("pool", snew[:], Sst[:], [bS], [bnew])

        for dr in (1, 0):
            order = (list(range(TCT)) + list(range(TCT, NTILE))) if dr == 0 else (list(range(TCT - 1, -1, -1)) + list(range(NTILE - 1, TCT - 1, -1)))
            memset("dve", Sst[:], 0.0, [bS])
            memset("pool", Sbf[0][:], 0.0, [bSbf[0]])
            memset("pool", Sbf[1][:], 0.0, [bSbf[1]])
            free = list(range(WG))
            active = []
            done = {}
            nxt = 0
            nstate = 0
            while nstate < len(order):
                while nxt < len(order) and free:
                    slot = free.pop(0)
                    active.append([prepG(order[nxt], slot, dr), slot, nxt])
                    nxt += 1
                for item in list(active):
                    try:
                        next(item[0])
                    except StopIteration:
                        active.remove(item)
                        done[item[2]] = item[1]
                while nstate in done:
                    slot = done.pop(nstate)
                    stateG(order[nstate], slot, dr, nstate)
                    free.append(slot)
                    nstate += 1

        if stop == 'G':
            raise _Stop(nc, P)
        P.tag = "L%d.Cs" % l
        P.barrier()
        arena_reset()
        wco = alloc("wco", [128, 4, D], BF16)
        wgo = alloc("wgo", [128, 4, D], BF16)
        wao = alloc("wao", [64, 8, D], BF16)
        wo = alloc("wo", [128, 8, D], BF16)
        bwc = Buf()
        P.dma("pool", wco[:], kview(Wd["w_conv_out"][l]), w=[bwc])
        P.dma("pool", wgo[:], kview(Wd["w_gla_out"][l]), w=[bwc])
        P.dma("pool", wao[:], Wd["w_attn_out"][l].rearrange("(h p) n -> p h n", p=64), w=[bwc])
        P.dma("pool", wo[:], kview(Wd["w_out"][l]), w=[bwc])
        gateC = alloc("gateC", [128, D], F32)
        bgateC = Buf()
        dwT = alloc("dwT", [32, 512], F32)
        dww = alloc("dww", [128, 4, 32], F32)
        cpar = alloc("cpar", [128, 3, 4], F32)
        gng = alloc("gng", [128, 1], F32)
        diag = alloc("diag", [128, 4, 31, 128], BF16)
        bdwT, bdww, bcpar, bdiag = Buf(), Buf(), Buf(), Buf()
        memset("dve", dwT[:], 0.0, [bdwT])
        P.dma("sp", dwT[0:31, :], Wd["conv_dw_w"][l], w=[bdwT])
        for cb in range(4):
            tr(PS[0][:, cb * 32:cb * 32 + 32], dwT[0:32, cb * 128:(cb + 1) * 128], ident_f[0:32, 0:32], [bdwT, bCONST], [bPS[0]])
        cp("dve", dww[:, :, 0:31], PS[0][:, 0:128].rearrange("p (c j) -> p c j", c=4)[:, :, 0:31], [bPS[0]], [bdww])
        for pi, nm in enumerate(("conv_dw_b", "conv_ln_g", "conv_ln_b")):
            P.dma("sp", cpar[:, pi, :], Wd[nm][l:l + 1, :].rearrange("o (cb p) -> p (o cb)", p=128), w=[bcpar], allow_slow_non_contiguous=True)
        P.dma("sp", gng[:], Wd["gla_norm_g"][l:l + 1, :].rearrange("o n -> n o"), w=[bcpar], allow_slow_non_contiguous=True)
        for cb in range(4):
            tt("dve", diag[:, cb, :, :], ident_f[:].unsqueeze(1).broadcast_to([128, 31, 128]),
               dww[:, cb, 0:31].unsqueeze(2).broadcast_to([128, 31, 128]), ALU.mult, [bCONST, bdww], [bdiag])

        qT = [alloc("c_qT%d" % i, [128, 512], BF16) for i in range(2)]
        bqT = [Buf(), Buf()]
        Ee = [alloc("c_E%d" % i, [128, 2, 512], BF16) for i in range(3)]
        bE = [Buf() for _ in range(3)]
        attg = [alloc("c_attg%d" % i, [64, 2, 512], BF16) for i in range(3)]
        OAT = alloc("c_OAT", [64, 8, 512], BF16)
        rinv = [alloc("c_rinv%d" % i, [65, 512], F32) for i in range(2)]
        og = alloc("c_og", [64, 512], F32)
        battg, bOAT, brinv, bog = [Buf(), Buf(), Buf()], Buf(), [Buf(), Buf()], Buf()
        uh = alloc("c_uh", [128, 4, 512 + 30], BF16)
        cgt = alloc("c_cg", [128, 4, 512], BF16)
        cv = alloc("c_cv", [128, 4, 512], F32)
        csq = alloc("c_csq", [128, 512], F32)
        mean = alloc("c_mean", [128, 512], F32)
        msq = alloc("c_msq", [128, 512], F32)
        rstd = alloc("c_rstd", [128, 512], F32)
        yt = alloc("c_yt", [128, 512], F32)
        yt2 = alloc("c_yt2", [128, 512], F32)
        uo = alloc("c_uo", [128, 4, 512], BF16)
        buh, bcgt, bcv, bcsq, bmean, bmsq, brstd, byt, byt2, buo = [Buf() for _ in range(10)]
        ob = [alloc("c_ob0", [128, 512], F32)] * 2
        of_ = [alloc("c_of0", [128, 512], F32)] * 2
        bob, bof = [Buf()] * 2, [Buf()] * 2
        gsum = mean
        ggt = [alloc("c_gg%d" % i, [128, 512], BF16) for i in range(2)]
        go = alloc("c_go", [128, 4, 512], BF16)
        bgsum, bggt, bgo = bmean, [Buf(), Buf()], Buf()
        mg = [alloc("c_mg0", [128, 3, 512], BF16)] * 2
        bmg = [Buf()] * 2
        mt1, mt2, mt3 = yt, yt2, csq
        bmt1, bmt2, bmt3 = byt, byt2, bcsq
        mT = alloc("c_mT", [128, 8, 512], BF16)
        bmT = Buf()
        xr_ = alloc("c_xr", [128, D], F32)
        xo_ = alloc("c_xo", [128, D], F32)
        bxr, bxo = Buf(), Buf()

        for ci, (c0, cn) in enumerate(chunks):
            is_ctx = ci == 0
            if is_ctx and not need_ctx:
                continue
            cs_ = slice(c0, c0 + cn)
            ktiles = list(range(TCT)) if is_ctx else list(range(NTILE))
            if ci <= 1:
                s_ = 1 if is_ctx else 0
                P.dma("sp", gateC[:], MOD[l, s_:s_ + 1, 2 * D:3 * D].partition_broadcast(128), r=[bMOD], w=[bgateC])
            seq0, seq1 = (0, TC) if is_ctx else (TC, NT)
            nkt = len(ktiles)

            def att_gen():
                P_tag = "L%d.Catt" % l
                its = [(j, kt_i, kt) for j in range(4) for kt_i, kt in enumerate(ktiles)]
                n = len(its)
                LA = 1

                def loads(j):
                    qi = j % 2
                    P.dma("sp", qT[qi][:, :cn], QT[j * 128:(j + 1) * 128, cs_], r=[bQT], w=[bqT[qi]])
                    for hh in range(2):
                        r0 = AG_ + (j + 4 * hh) * 64
                        P.dma("sp", attg[j % 3][:, hh, :cn], PXF[r0:r0 + 64, cs_], r=[bPXF], w=[battg[j % 3]])

                for idx in range(n + LA):
                    P.tag = P_tag
                    if idx < n:
                        j, kt_i, kt = its[idx]
                        qi, st_, eb_ = j % 2, idx % 2, idx % 3
                        if kt_i == 0:
                            if j == 0:
                                loads(0)
                            if j + 1 < 4:
                                loads(j + 1)
                        for hh in range(2):
                            p0 = 64 * hh
                            mm(PS[2 * st_ + hh][:, :cn], KT[p0:p0 + 64, kt * 128:(kt + 1) * 128], qT[qi][p0:p0 + 64, :cn], True, True,
                               [bKT[kt], bqT[qi]], [bPS[2 * st_ + hh]])
                        src2 = PSALL[:, st_ * 1024:(st_ + 1) * 1024].rearrange("p (h n) -> p h n", h=2)[:, :, :cn]
                        act(Ee[eb_][:, :, :cn], src2, AF.Exp, [bPS[2 * st_], bPS[2 * st_ + 1]], [bE[eb_]], scale=0.125)
                    if idx >= LA:
                        j, kt_i, kt = its[idx - LA]
                        qi, eb_ = j % 2, (idx - LA) % 3
                        for hh in range(2):
                            mm(PS[4 + hh][0:65, :cn], Vext[:, kt, hh, :], Ee[eb_][:, hh, :cn], kt_i == 0, kt_i == nkt - 1,
                               [bV[kt], bE[eb_]], [bPS[4 + hh]])
                        if kt_i == nkt - 1:
                            for hh in range(2):
                                h8 = j + 4 * hh
                                recip(rinv[hh][64:65, :cn], PS[4 + hh][64:65, :cn], [bPS[4 + hh]], [brinv[hh]])
                                mm(PS[6][0:64, :cn], ones_f[64:65, 0:64], rinv[hh][64:65, :cn], True, True, [bCONST, brinv[hh]], [bPS[6]])
                                tt("dve", og[:, :cn], PS[4 + hh][0:64, :cn], attg[j % 3][:, hh, :cn], ALU.mult, [bPS[4 + hh], battg[j % 3]], [bog])
                                tt("dve", OAT[:, h8, :cn], og[:, :cn], PS[6][0:64, :cn], ALU.mult, [bog, bPS[6]], [bOAT])
                    yield

            def side_gen():
                P.tag = "L%d.Cconv" % l
                lo, hi = max(c0 - 15, seq0), min(c0 + cn + 15, seq1)
                if lo > c0 - 15:
                    memset("pool", uh[:, :, 0:15], 0.0, [buh])
                if hi < c0 + cn + 15:
                    memset("pool", uh[:, :, cn + 15:cn + 30], 0.0, [buh])
                P.dma("sp", uh[:, :, lo - (c0 - 15):hi - (c0 - 15)], PXF[U_:U_ + 512, lo:hi].rearrange("(cb p) n -> p cb n", p=128),
                      r=[bPXF], w=[buh])
                P.dma("sp", cgt[:, :, :cn], PXF[CG_:CG_ + 512, cs_].rearrange("(cb p) n -> p cb n", p=128), r=[bPXF], w=[bcgt])
                yield
                for cb in range(4):
                    P.tag = "L%d.Cconv" % l
                    for j in range(31):
                        mm(PS[7][:, :cn], diag[:, cb, j, :], uh[:, cb, j:j + cn], j == 0, j == 30, [bdiag, buh], [bPS[7]])
                    yield
                    P.tag = "L%d.Cconv" % l
                    act(cv[:, cb, :cn], PS[7][:, :cn], AF.Identity, [bPS[7], bcpar], [bcv], bias=cpar[:, 0, cb:cb + 1])
                    yield
                P.tag = "L%d.Cconv" % l
                for cb in range(4):
                    mm(PS[7][:, :cn], ones_f[:], cv[:, cb, :cn], cb == 0, cb == 3, [bCONST, bcv], [bPS[7]])
                yield
                P.tag = "L%d.Cconv" % l
                act(mean[:, :cn], PS[7][:, :cn], AF.Copy, [bPS[7]], [bmean], scale=1.0 / 512)
                tt("dve", msq[:, :cn], mean[:, :cn], mean[:, :cn], ALU.mult, [bmean], [bmsq])
                yield
                for cb in range(4):
                    P.tag = "L%d.Cconv" % l
                    act(csq[:, :cn], cv[:, cb, :cn], AF.Square, [bcv], [bcsq])
                    yield
                    P.tag = "L%d.Cconv" % l
                    mm(PS[7][:, :cn], ones_f[:], csq[:, :cn], cb == 0, cb == 3, [bCONST, bcsq], [bPS[7]])
                    yield
                P.tag = "L%d.Cconv" % l
                stt(rstd[:, :cn], PS[7][:, :cn], 1.0 / 512, msq[:, :cn], ALU.mult, ALU.subtract, [bPS[7], bmsq], [brstd])
                yield
                P.tag = "L%d.Cconv" % l
                act(rstd[:, :cn], rstd[:, :cn], AF.Ln, [brstd, bCONST], [brstd], bias=eps_t[:])
                yield
                P.tag = "L%d.Cconv" % l
                act(rstd[:, :cn], rstd[:, :cn], AF.Exp, [brstd], [brstd], scale=-0.5)
                yield
                for cb in range(4):
                    P.tag = "L%d.Cconv" % l
                    tt("dve", yt[:, :cn], cv[:, cb, :cn], mean[:, :cn], ALU.subtract, [bcv, bmean], [byt])
                    yield
                    P.tag = "L%d.Cconv" % l
                    tt("dve", yt2[:, :cn], yt[:, :cn], rstd[:, :cn], ALU.mult, [byt, brstd], [byt2])
                    yield
                    P.tag = "L%d.Cconv" % l
                    ts("dve", yt[:, :cn], yt2[:, :cn], cpar[:, 1, cb:cb + 1], cpar[:, 2, cb:cb + 1], ALU.mult, ALU.add, [byt2, bcpar], [byt])
                    yield
                    P.tag = "L%d.Cconv" % l
                    act(yt2[:, :cn], yt[:, :cn], AF.Exp, [byt], [byt2], scale=-1.0)
                    yield
                    P.tag = "L%d.Cconv" % l
                    ts("pool", yt2[:, :cn], yt2[:, :cn], 1.0, 1.0, ALU.add, ALU.mult, [byt2], [byt2])
                    yield
                    P.tag = "L%d.Cconv" % l
                    recip(yt2[:, :cn], yt2[:, :cn], [byt2], [byt2])
                    yield
                    P.tag = "L%d.Cconv" % l
                    tt("dve", yt[:, :cn], yt[:, :cn], yt2[:, :cn], ALU.mult, [byt, byt2], [byt])
                    yield
                    P.tag = "L%d.Cconv" % l
                    tt("pool", uo[:, cb, :cn], yt[:, :cn], cgt[:, cb, :cn], ALU.mult, [byt, bcgt], [buo])
                    yield
                for h in range(4):
                    i = h % 2
                    P.tag = "L%d.Cgla" % l
                    P.dma("sp", ggt[i][:, :cn], PXF[GG_ + h * 128:GG_ + (h + 1) * 128, cs_], r=[bPXF], w=[bggt[i]])
                    P.dma("sp", ob[i][:, :cn], OBF[0][h * 128:(h + 1) * 128, cs_], r=[bOBF[0]], w=[bob[i]])
                    P.dma("sp", of_[i][:, :cn], OBF[1][h * 128:(h + 1) * 128, cs_], r=[bOBF[1]], w=[bof[i]])
                    yield
                    P.tag = "L%d.Cgla" % l
                    tt("pool", gsum[:, :cn], ob[i][:, :cn], of_[i][:, :cn], ALU.add, [bob[i], bof[i]], [bgsum])
                    yield
                    P.tag = "L%d.Cgla" % l
                    act(csq[:, :cn], gsum[:, :cn], AF.Square, [bgsum], [bcsq])
                    yield
                    P.tag = "L%d.Cgla" % l
                    mm(PS[7][:, :cn], ones_f[:], csq[:, :cn], True, True, [bCONST, bcsq], [bPS[7]])
                    yield
                    P.tag = "L%d.Cgla" % l
                    act(msq[:, :cn], PS[7][:, :cn], AF.Ln, [bPS[7], bCONST], [bmsq], bias=eps_t[:], scale=1.0 / 128)
                    yield
                    P.tag = "L%d.Cgla" % l
                    act(msq[:, :cn], msq[:, :cn], AF.Exp, [bmsq], [bmsq], scale=-0.5)
                    yield
                    P.tag = "L%d.Cgla" % l
                    tt("dve", yt[:, :cn], gsum[:, :cn], msq[:, :cn], ALU.mult, [bgsum, bmsq], [byt])
                    yield
                    P.tag = "L%d.Cgla" % l
                    stt(go[:, h, :cn], yt[:, :cn], gng[:], ggt[i][:, :cn], ALU.mult, ALU.mult, [byt, bcpar, bggt[i]], [bgo])
                    yield

            gA, gB = att_gen(), side_gen()
            nA = 4 * nkt + 1
            nB = 90
            acc = 0.0
            doneB = False
            for _ in gA:
                acc += float(nB) / nA
                while acc >= 1.0 and not doneB:
                    acc -= 1.0
                    try:
                        next(gB)
                    except StopIteration:
                        doneB = True
            if not doneB:
                for _ in gB:
                    pass
            P.tag = "L%d.Cmrg" % l
            for fb in range(8):
                i = fb % 2
                fs = slice(fb * 128, (fb + 1) * 128)
                P.dma("sp", mg[i][:, :, :cn], PXF[M13_:M13_ + 3 * D, cs_].rearrange("(g f p) n -> p g f n", g=3, f=8, p=128)[:, :, fb, :],
                      r=[bPXF], w=[bmg[i]])
                for cb in range(4):
                    mm(PS[0][:, :cn], wco[:, cb, fs], uo[:, cb, :cn], cb == 0, cb == 3, [bwc, buo], [bPS[0]])
                for h in range(4):
                    mm(PS[1][:, :cn], wgo[:, h, fs], go[:, h, :cn], h == 0, h == 3, [bwc, bgo], [bPS[1]])
                for h8 in range(8):
                    mm(PS[2][:, :cn], wao[:, h8, fs], OAT[:, h8, :cn], h8 == 0, h8 == 7, [bwc, bOAT], [bPS[2]])
                tt("dve", mt1[:, :cn], PS[0][:, :cn], mg[i][:, 0, :cn], ALU.mult, [bPS[0], bmg[i]], [bmt1])
                tt("dve", mt2[:, :cn], PS[1][:, :cn], mg[i][:, 1, :cn], ALU.mult, [bPS[1], bmg[i]], [bmt2])
                tt("dve", mt3[:, :cn], PS[2][:, :cn], mg[i][:, 2, :cn], ALU.mult, [bPS[2], bmg[i]], [bmt3])
                tt("pool", mt1[:, :cn], mt1[:, :cn], mt2[:, :cn], ALU.add, [bmt1, bmt2], [bmt1])
                tt("pool", mT[:, fb, :cn], mt1[:, :cn], mt3[:, :cn], ALU.add, [bmt1, bmt3], [bmT])
            P.tag = "L%d.Cfin" % l
            for s4 in range(cn // 128):
                t = c0 // 128 + s4
                P.dma("sp", xr_[:], src_tile(t), r=[bXS], w=[bxr])
                for hf in range(2):
                    pb = 3 + hf
                    for fb in range(8):
                        mm(PS[pb][:], mT[:, fb, s4 * 128:(s4 + 1) * 128], wo[:, fb, hf * 512:(hf + 1) * 512], fb == 0, fb == 7,
                           [bmT, bwc], [bPS[pb]])
                    tt("dve", xo_[:, hf * 512:(hf + 1) * 512], PS[pb][:], gateC[:, hf * 512:(hf + 1) * 512],
                       ALU.mult, [bPS[pb], bgateC], [bxo])
                tt("pool", xo_[:], xo_[:], xr_[:], ALU.add, [bxo, bxr], [bxo])
                if last:
                    xrow = (t - TCT) * 128
                    final_ops.append(P.dma("sp", out_d[xrow:xrow + 128, :], xo_[:], r=[bxo], w=[bOUT]))
                else:
                    P.dma("sp", XS[t * 128:(t + 1) * 128, :], xo_[:], r=[bxo], w=[Buf()])

    P.emit(final_ops)
    return nc, P


_CACHE = {}


def kernel(**inputs):
    T, TC, DEPTH, B = 4096, 256, 4, 8
    x = np.asarray(inputs["x"], np.float32)
    B, T = x.shape[0], x.shape[1]
    TC = inputs["ctx"].shape[1]
    DEPTH = inputs["w_in"].shape[0]
    key = (T, TC, DEPTH)
    if key not in _CACHE:
        _CACHE[key] = build(T, TC, DEPTH)[0]
    nc = _CACHE[key]
    consts = host_consts(T)
    shared = {n: np.ascontiguousarray(np.asarray(inputs[n], np.float32)) for n, _ in WEIGHT_SPECS}
    shared["c_ctx"] = np.ascontiguousarray(np.asarray(inputs["c_ctx"], np.float32).reshape(1, D))
    shared.update(consts)
    in_maps = []
    for b in range(B):
        m = dict(shared)
        m["x"] = np.ascontiguousarray(x[b])
        m["ctx"] = np.ascontiguousarray(np.asarray(inputs["ctx"], np.float32)[b])
        m["c"] = np.ascontiguousarray(np.asarray(inputs["c"], np.float32)[b:b + 1])
        in_maps.append(m)
    res = run_bass_kernel_spmd(nc, in_maps, core_ids=list(range(B)))
    return np.stack([np.asarray(r["out"], np.float32) for r in res.results], axis=0)
```

```python
import contextlib
import numpy as np
import ml_dtypes
import concourse.bass as bass
import concourse.mybir as mybir
from concourse.bass_utils import run_bass_kernel_spmd

F32 = mybir.dt.float32
BF16 = mybir.dt.bfloat16
AF = mybir.ActivationFunctionType
ALU = mybir.AluOpType
AX = mybir.AxisListType

D = 1024
NIN = 7456
EPS = 1e-6


class Buf:
    __slots__ = ("name", "w", "r", "rd")

    def __init__(self, name=""):
        self.name = name
        self.w = None
        self.r = {}
        self.rd = []


class Op:
    __slots__ = ("eng", "fn", "deps", "needs_inc", "val", "dma", "dsem", "dval", "idx", "tag")

    def __init__(self, eng, fn, dma, idx):
        self.eng = eng
        self.fn = fn
        self.dma = dma
        self.deps = []
        self.needs_inc = False
        self.val = None
        self.dsem = None
        self.dval = None
        self.idx = idx


class Prog:
    ENGS = ("pe", "act", "dve", "pool", "sp")
    NDMA = 56

    def __init__(self, nc):
        self.nc = nc
        self.ops = {e: [] for e in self.ENGS}
        self.all = []
        self.dma_since_barrier = []
        self.tag = ""
        self.annotate = False

    def add(self, eng, fn, r=(), w=(), dma=False):
        op = Op(eng, fn, dma, len(self.all))
        op.tag = self.tag
        edeps = {}
        ddeps = {}

        def dep(x):
            if x is None or x is op:
                return
            if x.dma:
                ddeps[x.idx] = x
            else:
                if x.eng == "pe" and eng == "pe" and not dma:
                    return
                o = edeps.get(x.eng)
                if o is None or o.idx < x.idx:
                    edeps[x.eng] = x

        for b in r:
            dep(b.w)
        for b in w:
            dep(b.w)
            for x in b.r.values():
                dep(x)
            for x in b.rd:
                dep(x)
        for b in w:
            b.w = op
            b.r = {}
            b.rd = []
        for b in r:
            if b.w is not op:
                if dma:
                    b.rd.append(op)
                else:
                    b.r[eng] = op
        op.deps = list(edeps.values()) + list(ddeps.values())
        self.ops[eng].append(op)
        self.all.append(op)
        if dma:
            self.dma_since_barrier.append(op)
        return op

    def dma(self, q, out, in_, r=(), w=(), **kw):
        return self.add(q, lambda e: e.dma_start(out=out, in_=in_, **kw), r=r, w=w, dma=True)

    def barrier(self):
        last = {}
        for e in self.ENGS:
            last[e] = None
            for o in reversed(self.ops[e]):
                if not o.dma and o.fn is not None:
                    last[e] = o
                    break
        dmas = list(self.dma_since_barrier)
        self.dma_since_barrier = []
        for e in self.ENGS:
            op = Op(e, None, False, len(self.all))
            for e2 in self.ENGS:
                if e2 != e and last[e2] is not None:
                    op.deps.append(last[e2])
            op.deps.extend(dmas)
            self.ops[e].append(op)
            self.all.append(op)

    def emit(self, final_wait_ops=()):
        nc = self.nc
        fin = Op("sp", None, False, len(self.all))
        fin.deps = list(final_wait_ops)
        self.ops["sp"].append(fin)
        self.all.append(fin)
        sem_last = [None] * self.NDMA
        sem_cnt = [0] * self.NDMA
        k = 0
        for op in self.all:
            if op.dma:
                s = k % self.NDMA
                k += 1
                if sem_last[s] is not None:
                    op.deps.append(sem_last[s])
                sem_last[s] = op
                sem_cnt[s] += 16
                op.dsem = s
                op.dval = sem_cnt[s]
        for op in self.all:
            for d in op.deps:
                d.needs_inc = True
        cnt = {e: 0 for e in self.ENGS}
        for op in self.all:
            if not op.dma and op.fn is not None:
                if op.needs_inc:
                    cnt[op.eng] += 1
                op.val = cnt[op.eng]
        self.stats = {e: [len(self.ops[e]), cnt[e], 0] for e in self.ENGS}
        with contextlib.ExitStack() as st:
            esem = {e: st.enter_context(nc.semaphore("s_" + e)) for e in self.ENGS}
            dsem = [st.enter_context(nc.semaphore("d%d" % i)) for i in range(self.NDMA)]
            block = st.enter_context(nc.Block())

            def run(ename):
                def body(e):
                    waited = {}
                    nwait = 0
                    for op in self.ops[ename]:
                        for d in op.deps:
                            if d.dma:
                                key, v, sem = ("d", d.dsem), d.dval, dsem[d.dsem]
                            else:
                                if d.fn is None:
                                    continue
                                key, v, sem = ("e", d.eng), d.val, esem[d.eng]
                            if waited.get(key, 0) >= v:
                                continue
                            waited[key] = v
                            e.wait_ge(sem, v)
                            nwait += 1
                        if op.fn is None:
                            continue
                        ins = op.fn(e)
                        if self.annotate:
                            ins.annotate(op.tag)
                        if op.dma:
                            ins.then_inc(dsem[op.dsem], 16)
                        elif op.needs_inc:
                            ins.then_inc(esem[ename], 1)
                    self.stats[ename][2] = nwait

                return body

            block.tensor(run("pe"))
            block.scalar(run("act"))
            block.vector(run("dve"))
            block.gpsimd(run("pool"))
            block.sync(run("sp"))


def run_rr(n_items, make_gen, width):
    free = list(range(width))
    active = []
    nxt = 0
    while nxt < n_items or active:
        while nxt < n_items and free:
            slot = free.pop(0)
            active.append((make_gen(nxt, slot), slot))
            nxt += 1
        for item in list(active):
            try:
                next(item[0])
            except StopIteration:
                active.remove(item)
                free.append(item[1])


def host_consts(T):
    c = {}
    c["ident_bf"] = np.eye(128, dtype=np.float32).astype(ml_dtypes.bfloat16)
    c["ident_f"] = np.eye(128, dtype=np.float32)
    tp = np.arange(128)[:, None]
    t = np.arange(128)[None, :]
    mle = (tp <= t).astype(np.float32)
    mge = (tp >= t).astype(np.float32)
    s = -1.0 / 16.0
    masks = np.stack([s * mle, s * (1 - mle), s * mge, s * (1 - mge), mle, mge], axis=1)
    c["masks"] = np.ascontiguousarray(masks.astype(np.float32))
    c["ones_f"] = np.ones((128, 128), np.float32)
    c["neg16"] = np.full((128, 2), s, np.float32)
    n_rows = T // 64
    row = np.repeat(np.arange(n_rows, dtype=np.float32), 64)
    col = np.tile(np.arange(64, dtype=np.float32), n_rows)
    freqs = (np.float32(10000.0) ** (-np.arange(16, dtype=np.float32) / np.float32(16))).astype(np.float32)
    ar = (row[:, None] * freqs).astype(np.float32)
    ac = (col[:, None] * freqs).astype(np.float32)
    cr, sr, cc, sc = np.cos(ar), np.sin(ar), np.cos(ac), np.sin(ac)
    c["rope_cos"] = np.concatenate([cr, cr, cc, cc], axis=1).astype(np.float32)
    c["rope_sin"] = np.concatenate([-sr, sr, -sc, sc], axis=1).astype(np.float32)
    return c


CONST_SPECS = [("ident_bf", [128, 128], BF16), ("ident_f", [128, 128], F32), ("masks", [128, 6, 128], F32),
               ("ones_f", [128, 128], F32), ("neg16", [128, 2], F32)]

WEIGHT_SPECS = [("norm_g", [D]), ("w_mod", [D, 3 * D]), ("b_mod", [3 * D]), ("w_in", [D, NIN]), ("b_in", [NIN]),
                ("conv_dw_w", [31, 512]), ("conv_dw_b", [512]), ("conv_ln_g", [512]), ("conv_ln_b", [512]),
                ("w_conv_out", [512, D]), ("gla_w_gate", [2, 16, 256]), ("gla_b_gate", [2, 256]),
                ("gla_norm_g", [128]), ("w_gla_out", [512, D]), ("q_norm_g", [64]), ("k_norm_g", [64]),
                ("w_attn_out", [512, D]), ("w_out", [D, D])]


class _Stop(Exception):
    pass


def build(T, TC, DEPTH, debug=False, stop=None, annotate=False):
    try:
        return _build(T, TC, DEPTH, debug, stop, annotate)
    except _Stop as e:
        nc, P = e.args
        P.emit([])
        return nc, P


def _build(T, TC, DEPTH, debug=False, stop=None, annotate=False):
    NT = TC + T
    NTILE = NT // 128
    TCT = TC // 128
    chunks = [(0, TC)] + [(TC + 512 * i, 512) for i in range(T // 512)]

    nc = bass.Bass("TRN2", target_bir_lowering=False)
    P = Prog(nc)
    P.annotate = annotate

    def din(name, shape, dt=F32):
        return nc.dram_tensor(name, shape, dt, kind="ExternalInput").ap()

    x_d = din("x", [T, D])
    ctx_d = din("ctx", [TC, D])
    c_d = din("c", [1, D])
    cctx_d = din("c_ctx", [1, D])
    Wd = {n: din(n, [DEPTH] + s) for n, s in WEIGHT_SPECS}
    Cd = {n: din(n, s, dt) for n, s, dt in CONST_SPECS}
    cos_d = din("rope_cos", [T, 64])
    sin_d = din("rope_sin", [T, 64])
    out_d = nc.dram_tensor("out", [T, D], F32, kind="ExternalOutput").ap()

    skind = "ExternalOutput" if debug else "Internal"

    def dscr(name, shape, dt):
        return nc.dram_tensor(name, shape, dt, kind=skind).ap()

    XS = dscr("XS", [NT, D], F32)
    MOD = dscr("MOD", [DEPTH, 2, 3 * D], F32)
    NPX = 5120
    PXF = dscr("PXF", [NPX, NT], BF16)
    QT = dscr("QT", [512, NT], BF16)
    GQK = dscr("GQK", [NT, 512], F32)
    GV = dscr("GV", [NT, 512], BF16)
    GZ = dscr("GZ", [NT, 512], F32)
    OBF = [dscr("OB", [512, NT], F32), dscr("OF", [512, NT], F32)]
    U_, CG_, GG_, AG_, M13_ = 0, 512, 1024, 1536, 2048
    bXS, bMOD, bPXF, bQT, bGQK, bGV, bGZ = [Buf(n) for n in ("XS", "MOD", "PXF", "QT", "GQK", "GV", "GZ")]
    bOBF = [Buf("OB"), Buf("OF")]
    bOUT = Buf("OUT")

    SB_LO, SB_HI = 16512, 229344
    state = {"persist": SB_LO, "top": SB_LO, "n": 0}

    def alloc(name, shape, dt):
        esz = 2 if dt == BF16 else 4
        nbytes = esz
        for s_ in shape[1:]:
            nbytes *= s_
        nbytes = (nbytes + 63) // 64 * 64
        off = state["top"]
        state["top"] += nbytes
        assert state["top"] <= SB_HI, ("SBUF overflow", name, state["top"])
        state["n"] += 1
        return nc.alloc_sbuf_tensor_at("%s_%d" % (name, state["n"]), shape, dt, offset=off)

    def arena_reset():
        state["top"] = state["persist"]

    PSALL = nc.alloc_psum_tensor("psall", [128, 4096], F32)
    PS = [PSALL[:, i * 512:(i + 1) * 512] for i in range(8)]
    bPS = [Buf("ps%d" % i) for i in range(8)]

    def mm(out, lhsT, rhs, start, stop, r, w):
        return P.add("pe", lambda e: e.matmul(out, lhsT=lhsT, rhs=rhs, start=start, stop=stop), r=r, w=w)

    def tr(out, in_, ident, r, w):
        return P.add("pe", lambda e: e.transpose(out=out, in_=in_, identity=ident), r=r, w=w)

    def act(out, in_, func, r, w, bias=None, scale=None, accum=None):
        kw = {}
        if bias is not None:
            kw["bias"] = bias
        if scale is not None:
            kw["scale"] = scale
        if accum is not None:
            kw["accum_out"] = accum
        return P.add("act", lambda e: e.activation(out=out, in_=in_, func=func, **kw), r=r, w=w)

    def tt(eng, out, in0, in1, op, r, w):
        return P.add(eng, lambda e: e.tensor_tensor(out=out, in0=in0, in1=in1, op=op), r=r, w=w)

    def ts(eng, out, in0, s1, s2, op0, op1, r, w):
        return P.add(eng, lambda e: e.tensor_scalar(out=out, in0=in0, scalar1=s1, scalar2=s2, op0=op0, op1=op1), r=r, w=w)

    def stt(out, in0, scalar, in1, op0, op1, r, w):
        return P.add("dve", lambda e: e.scalar_tensor_tensor(out=out, in0=in0, scalar=scalar, in1=in1, op0=op0, op1=op1), r=r, w=w)

    def cp(eng, out, in_, r, w):
        if eng == "act":
            return P.add("act", lambda e: e.copy(out=out, in_=in_), r=r, w=w)
        return P.add(eng, lambda e: e.tensor_copy(out=out, in_=in_), r=r, w=w)

    def memset(eng, ap, v, w):
        return P.add(eng, lambda e: e.memset(ap, v), w=w)

    def recip(out, in_, r, w):
        return P.add("dve", lambda e: e.reciprocal(out=out, in_=in_), r=r, w=w)

    def kview(w2d):
        return w2d.rearrange("(kb p) n -> p kb n", p=128)

    ident_bf = alloc("ident_bf", [128, 128], BF16)
    ident_f = alloc("ident_f", [128, 128], F32)
    masks = alloc("masks", [128, 6, 128], F32)
    ones_f = alloc("ones_f", [128, 128], F32)
    neg16 = alloc("neg16", [128, 2], F32)
    eps_t = alloc("eps", [128, 1], F32)
    ones_bf = alloc("ones_bf", [128, 128], BF16)
    bCONST = Buf("const")
    for n, t_ in (("ident_bf", ident_bf), ("ident_f", ident_f), ("masks", masks), ("ones_f", ones_f), ("neg16", neg16)):
        P.dma("sp", t_[:], Cd[n], w=[bCONST])
    memset("dve", eps_t[:], EPS, [bCONST])
    memset("dve", ones_bf[:], 1.0, [bCONST])
    KT = alloc("KT", [128, NT], BF16)
    Vext = alloc("Vext", [128, NTILE, 2, 65], BF16)
    bKT = [Buf("KT%d" % i) for i in range(NTILE)]
    bV = [Buf("V%d" % i) for i in range(NTILE)]
    memset("pool", Vext[:, :, :, 64:65], 1.0, bV)
    state["persist"] = state["top"]

    P.tag = "0"
    arena_reset()
    cc = alloc("cc", [128, 2, 8], F32)
    sc = alloc("sc", [128, 2, 8], F32)
    bm = alloc("bm", [2, 3 * D], F32)
    ng = alloc("ng", [2, D], F32)
    modsb = alloc("modsb", [2, 3 * D], F32)
    wm = [alloc("wm%d" % i, [128, 8, 512], F32) for i in range(2)]
    bcc, bsc, bbm, bng, bmodsb = Buf(), Buf(), Buf(), Buf(), Buf()
    bwm = [Buf(), Buf()]
    P.dma("sp", cc[:, 0, :], c_d.rearrange("o (kb p) -> p (o kb)", p=128), w=[bcc], allow_slow_non_contiguous=True)
    P.dma("sp", cc[:, 1, :], cctx_d.rearrange("o (kb p) -> p (o kb)", p=128), w=[bcc], allow_slow_non_contiguous=True)
    act(sc[:], cc[:], AF.Silu, [bcc], [bsc])
    k = 0
    for l in range(DEPTH):
        P.dma("sp", bm[:], Wd["b_mod"][l:l + 1, :].partition_broadcast(2), w=[bbm])
        P.dma("sp", ng[:], Wd["norm_g"][l:l + 1, :].partition_broadcast(2), w=[bng])
        for ch in range(6):
            wb_ = wm[k % 2]
            bw_ = bwm[k % 2]
            pb = k % 2
            k += 1
            P.dma("sp", wb_[:], kview(Wd["w_mod"][l][:, ch * 512:(ch + 1) * 512]), w=[bw_])
            for kb in range(8):
                mm(PS[pb][0:2, :], sc[:, :, kb], wb_[:, kb, :], kb == 0, kb == 7, [bsc, bw_], [bPS[pb]])
            tt("dve", modsb[:, ch * 512:(ch + 1) * 512], PS[pb][0:2, :], bm[:, ch * 512:(ch + 1) * 512], ALU.add,
               [bPS[pb], bbm], [bmodsb])
        stt(modsb[:, D:2 * D], modsb[:, D:2 * D], 1.0, ng[:], ALU.add, ALU.mult, [bmodsb, bng], [bmodsb])
        P.dma("sp", MOD[l], modsb[:], r=[bmodsb], w=[bMOD])

    if stop == '0':
        raise _Stop(nc, P)
    final_ops = []
    for l in range(DEPTH):
        need_ctx = l < DEPTH - 1
        last = l == DEPTH - 1
        w_in = Wd["w_in"][l]
        b_in = Wd["b_in"]

        def src_tile(t):
            if l == 0:
                return ctx_d[t * 128:(t + 1) * 128, :] if t < TCT else x_d[(t - TCT) * 128:(t - TCT + 1) * 128, :]
            return XS[t * 128:(t + 1) * 128, :]

        P.tag = "L%d.A" % l
        P.barrier()
        arena_reset()
        hxT = alloc("hxT", [128, 8, NT], BF16)
        mark_hx = state["top"]
        bhx = [Buf("hx%d" % i) for i in range(NTILE)]
        wlr = alloc("wlr", [128, 8, 32], BF16)
        blr = alloc("blr", [32, 1], F32)
        lrT = alloc("lrT", [33, NT], BF16)
        w_tm = alloc("w_tm", [128, 8, 1792], BF16)
        bias_tm = alloc("bias_tm", [128, 1792], F32)
        Gq = alloc("Gq", [128, 64], F32)
        Gk = alloc("Gk", [128, 64], F32)
        wupf = alloc("wupf", [33, 512], F32)
        wup = alloc("wup", [33, 512], BF16)
        bwlr, bblr, blrT, bwtm, bbtm, bG, bwupf, bwup = [Buf() for _ in range(8)]
        P.dma("pool", wlr[:], kview(w_in[:, 3072:3104]), w=[bwlr])
        P.dma("sp", blr[:], b_in[l:l + 1, 3072:3104].rearrange("o n -> n o"), w=[bblr], allow_slow_non_contiguous=True)
        memset("pool", lrT[32:33, :], 1.0, [blrT])
        tm_cols = [(1536, 2048, 0), (2048, 2560, 512), (3104, 3616, 1024), (3616, 3872, 1536)]
        for a, b, o in tm_cols:
            P.dma("pool", w_tm[:, :, o:o + (b - a)], kview(w_in[:, a:b]), w=[bwtm])
            P.dma("sp", bias_tm[:, o:o + (b - a)], b_in[l:l + 1, a:b].partition_broadcast(128), w=[bbtm])
        P.dma("sp", Gq[:], Wd["q_norm_g"][l:l + 1, :].partition_broadcast(128), w=[bG])
        P.dma("sp", Gk[:], Wd["k_norm_g"][l:l + 1, :].partition_broadcast(128), w=[bG])
        memset("dve", wupf[:], 0.0, [bwupf])
        P.dma("sp", wupf[0:16, 0:256], Wd["gla_w_gate"][l, 0], w=[bwupf])
        P.dma("sp", wupf[16:32, 256:512], Wd["gla_w_gate"][l, 1], w=[bwupf])
        P.dma("sp", wupf[32:33, 0:256], Wd["gla_b_gate"][l, 0:1, :], w=[bwupf])
        P.dma("sp", wupf[32:33, 256:512], Wd["gla_b_gate"][l, 1:2, :], w=[bwupf])
        cp("dve", wup[:], wupf[:], [bwupf], [bwup])

        mark_A = state["top"]
        modA = alloc("modA", [128, 2, 2, D], F32)
        bmodA = Buf()
        for s_ in range(2):
            P.dma("sp", modA[:, s_, 0, :], MOD[l, s_:s_ + 1, D:2 * D].partition_broadcast(128), r=[bMOD], w=[bmodA])
            P.dma("sp", modA[:, s_, 1, :], MOD[l, s_:s_ + 1, 0:D].partition_broadcast(128), r=[bMOD], w=[bmodA])
        WA = 4
        xt = [alloc("xt%d" % i, [128, D], F32) for i in range(WA)]
        sqj = [alloc("sqj%d" % i, [128, D], BF16) for i in range(WA)]
        junk = [alloc("junk%d" % i, [128, D], F32) for i in range(WA)]
        hb = [alloc("hb%d" % i, [128, D], BF16) for i in range(WA)]
        ssA = alloc("ssA", [128, WA], F32)
        bxt, bsqj, bjunk, bhb, bss = [[Buf() for _ in range(WA)] for _ in range(5)]
        PTvs = [PS[3 + i].bitcast(BF16).rearrange("p (a b) -> p a b", a=8) for i in range(WA)]

        def genA(t, i):
            s_ = 0 if t >= TCT else 1
            pb = 3 + i
            P.dma("sp", xt[i][:], src_tile(t), r=[bXS], w=[bxt[i]])
            yield
            act(sqj[i][:], xt[i][:], AF.Square, [bxt[i]], [bsqj[i], bss[i]], accum=ssA[:, i:i + 1])
            yield
            act(ssA[:, i:i + 1], ssA[:, i:i + 1], AF.Ln, [bss[i], bCONST], [bss[i]], bias=eps_t[:], scale=1.0 / D)
            yield
            act(ssA[:, i:i + 1], ssA[:, i:i + 1], AF.Exp, [bss[i]], [bss[i]], scale=-0.5)
            yield
            stt(junk[i][:], xt[i][:], ssA[:, i:i + 1], modA[:, s_, 0, :], ALU.mult, ALU.mult, [bxt[i], bss[i], bmodA], [bjunk[i]])
            yield
            tt("dve", hb[i][:], junk[i][:], modA[:, s_, 1, :], ALU.add, [bjunk[i], bmodA], [bhb[i]])
            yield
            for kb in range(8):
                tr(PTvs[i][:, kb, :], hb[i][:, kb * 128:(kb + 1) * 128], ident_bf[:], [bhb[i], bCONST], [bPS[pb]])
            yield
            cp("act", hxT[:, :, t * 128:(t + 1) * 128], PTvs[i][:], [bPS[pb]], [bhx[t]])
            yield

        run_rr(NTILE, genA, WA)
        P.barrier()
        state["top"] = mark_A

        if stop == 'A':
            raise _Stop(nc, P)
        P.tag = "L%d.B0" % l
        for ci, (c0, cn) in enumerate(chunks):
            pb = ci % 2
            tl = list(range(c0 // 128, (c0 + cn) // 128))
            for kb in range(8):
                mm(PS[pb][0:32, :cn], wlr[:, kb, :], hxT[:, kb, c0:c0 + cn], kb == 0, kb == 7,
                   [bwlr] + [bhx[t] for t in tl], [bPS[pb]])
            act(lrT[0:32, c0:c0 + cn], PS[pb][0:32, :cn], AF.Identity, [bPS[pb], bblr], [blrT], bias=blr[:])

        if stop == 'B0':
            raise _Stop(nc, P)
        P.tag = "L%d.B1" % l
        WB = 2
        qk_sb = [alloc("qk_sb%d" % i, [128, 512], F32) for i in range(WB)]
        v_sb = [alloc("v_sb%d" % i, [128, 512], BF16) for i in range(WB)]
        la_e = [alloc("la_e%d" % i, [128, 512], F32) for i in range(WB)]
        la_sb = [alloc("la_sb%d" % i, [128, 512], F32) for i in range(WB)]
        qT_sb = [alloc("qT_sb%d" % i, [128, 4, 128], BF16) for i in range(WB)]
        nq_o = [alloc("nq_o%d" % i, [128, 512], BF16) for i in range(WB)]
        nk_o = [alloc("nk_o%d" % i, [128, 128], BF16) for i in range(WB)]
        cs_t = [alloc("cs_t%d" % i, [128, 2, 64], F32) for i in range(WB)]
        bqk_sb, bv_sb, bla_e, bla_sb, bqT_sb, bnq_o, bnk_o, bcs = [[Buf() for _ in range(WB)] for _ in range(8)]
        nscr = {}
        for i in range(WB):
            for nm, wdt in (("q", 512), ("k", 128)):
                nscr[(nm, i)] = ([alloc("ns%s%d_%d" % (nm, i, z), [128, wdt], F32) for z in range(4)] + [alloc("nss%s%d" % (nm, i), [128, 8], F32)],
                                 [Buf() for _ in range(5)])

        def norm_rope(src, bsrc, nh, G, boff, rope_i, out, bout, scr, perm=False):
            n = nh * 64
            tl_, bl_ = nscr[scr]
            b_, sq_, a_, r_ = [x[:, :n] for x in tl_[:4]]
            ss_ = tl_[4][:, :nh]
            Bb, Bsq, Ba, Br, Bss = bl_

            def v3(ap):
                return ap.rearrange("p (h d) -> p h d", h=nh)

            if perm:
                def vin(ap):
                    return ap.rearrange("p (a j d) -> p a j d", a=2, j=4)
                vout = out.rearrange("p (j a d) -> p a j d", a=2, j=4)
                rsb = ss_.rearrange("p (a j) -> p a j", a=2).unsqueeze(3).broadcast_to([128, 2, 4, 64])
            else:
                vin = v3
                vout = v3(out)
                rsb = ss_.unsqueeze(2).broadcast_to([128, nh, 64])

            tt("dve", b_, src, bias_tm[:, boff:boff + n], ALU.add, [bsrc, bbtm], [Bb])
            yield
            act(sq_, b_, AF.Square, [Bb], [Bsq])
            yield
            P.add("dve", lambda e: e.tensor_reduce(out=ss_, in_=v3(sq_), axis=AX.X, op=ALU.add), r=[Bsq], w=[Bss])
            yield
            act(ss_, ss_, AF.Ln, [Bss, bCONST], [Bss], bias=eps_t[:], scale=1.0 / 64)
            yield
            act(ss_, ss_, AF.Exp, [Bss], [Bss], scale=-0.5)
            yield
            tt("dve", v3(b_), v3(b_), G[:].unsqueeze(1).broadcast_to([128, nh, 64]), ALU.mult, [Bb, bG], [Bb])
            yield
            if rope_i is None:
                tt("dve", vout, vin(b_), rsb, ALU.mult, [Bb, Bss], [bout])
                yield
                return
            tt("dve", v3(b_), v3(b_), ss_.unsqueeze(2).broadcast_to([128, nh, 64]), ALU.mult, [Bb, Bss], [Bb])
            yield
            cs = cs_t[rope_i]
            tt("dve", v3(a_), v3(b_), cs[:, 0, :].unsqueeze(1).broadcast_to([128, nh, 64]), ALU.mult, [Bb, bcs[rope_i]], [Ba])
            yield

            def v5(ap):
                return ap.rearrange("p (h a b c) -> p h a b c", h=nh, a=2, b=2)

            sn5 = cs[:, 1, :].rearrange("p (a b c) -> p a b c", a=2, b=2)
            for hf in range(2):
                tt("dve", v5(r_)[:, :, :, hf, :], v5(b_)[:, :, :, 1 - hf, :],
                   sn5[:, :, hf, :].unsqueeze(1).broadcast_to([128, nh, 2, 16]), ALU.mult, [Bb, bcs[rope_i]], [Br])
                yield
            tt("dve", vout, vin(a_), vin(r_), ALU.add, [Ba, Br], [bout])
            yield

        PT6 = PS[6].bitcast(BF16)

        def genB1(t, i):
            bA, bB, bC, bD = 4 * i, 4 * i + 1, 4 * i + 2, 4 * i + 3
            PTk = PS[bB].bitcast(BF16)
            PTq = PS[bC].bitcast(BF16).rearrange("p (a b) -> p a b", a=8)
            tc_ = slice(t * 128, (t + 1) * 128)
            is_x = t >= TCT
            if is_x:
                xr = (t - TCT) * 128
                P.dma("sp", cs_t[i][:, 0, :], cos_d[xr:xr + 128, :], w=[bcs[i]])
                P.dma("sp", cs_t[i][:, 1, :], sin_d[xr:xr + 128, :], w=[bcs[i]])
            for kb in range(8):
                mm(PS[bA], hxT[:, kb, tc_], w_tm[:, kb, 0:512], kb == 0, kb == 7, [bhx[t], bwtm], [bPS[bA]])
            yield
            tt("dve", qk_sb[i][:], PS[bA], bias_tm[:, 0:512], ALU.add, [bPS[bA], bbtm], [bqk_sb[i]])
            P.dma("sp", GQK[tc_, :], qk_sb[i][:], r=[bqk_sb[i]], w=[bGQK])
            yield
            for kb in range(8):
                mm(PS[bB], hxT[:, kb, tc_], w_tm[:, kb, 512:1024], kb == 0, kb == 7, [bhx[t], bwtm], [bPS[bB]])
            yield
            tt("dve", v_sb[i][:], PS[bB], bias_tm[:, 512:1024], ALU.add, [bPS[bB], bbtm], [bv_sb[i]])
            P.dma("sp", GV[tc_, :], v_sb[i][:], r=[bv_sb[i]], w=[bGV])
            yield
            mm(PS[bA], lrT[0:33, tc_], wup[0:33, :], True, True, [blrT, bwup], [bPS[bA]])
            yield
            act(la_e[i][:], PS[bA], AF.Exp, [bPS[bA]], [bla_e[i]], scale=-1.0)
            yield
            for kb in range(8):
                mm(PS[bD][:, 0:256], hxT[:, kb, tc_], w_tm[:, kb, 1536:1792], kb == 0, kb == 7, [bhx[t], bwtm], [bPS[bD]])
            yield
            tt("dve", Vext[:, t, :, 0:64], PS[bD][:, 128:256].rearrange("p (h d) -> p h d", h=2),
               bias_tm[:, 1664:1792].rearrange("p (h d) -> p h d", h=2), ALU.add, [bPS[bD], bbtm], [bV[t]])
            yield
            act(la_sb[i][:], la_e[i][:], AF.Ln, [bla_e[i]], [bla_sb[i]], bias=1.0)
            P.dma("sp", GZ[tc_, :], la_sb[i][:], r=[bla_sb[i]], w=[bGZ])
            yield
            do_q = is_x or need_ctx
            if do_q:
                for kb in range(8):
                    mm(PS[bC], hxT[:, kb, tc_], w_tm[:, kb, 1024:1536], kb == 0, kb == 7, [bhx[t], bwtm], [bPS[bC]])
                yield
            gk = norm_rope(PS[bD][:, 0:128], bPS[bD], 2, Gk, 1536, i if is_x else None, nk_o[i][:], bnk_o[i], ("k", i))
            gq = norm_rope(PS[bC], bPS[bC], 8, Gq, 1024, i if is_x else None, nq_o[i][:], bnq_o[i], ("q", i), perm=True) if do_q else iter(())
            alive = [gk, gq]
            while alive:
                for g in list(alive):
                    try:
                        next(g)
                    except StopIteration:
                        alive.remove(g)
                yield
            tr(PTk[:, 0:128], nk_o[i][:], ident_bf[:], [bnk_o[i], bCONST], [bPS[bB]])
            yield
            cp("act", KT[:, tc_], PTk[:, 0:128], [bPS[bB]], [bKT[t]])
            yield
            if do_q:
                for j in range(4):
                    tr(PTq[:, j, :], nq_o[i][:, j * 128:(j + 1) * 128], ident_bf[:], [bnq_o[i], bCONST], [bPS[bC]])
                yield
                cp("act", qT_sb[i][:], PTq[:, 0:4, :], [bPS[bC]], [bqT_sb[i]])
                P.dma("sp", QT.rearrange("(j p) n -> p j n", p=128)[:, :, tc_], qT_sb[i][:], r=[bqT_sb[i]], w=[bQT])
                yield

        def genB1_tagged(t, i):
            for _ in genB1(t, i):
                P.tag = "L%d.B1" % l
                yield

        run_rr(NTILE, genB1_tagged, WB)

        if stop == 'B1':
            raise _Stop(nc, P)
        P.tag = "L%d.B2" % l
        P.barrier()
        state["top"] = mark_hx
        NG = 11
        grp_cols = [0, 512, 1024, 2560, 3872] + [4384 + 512 * g for g in range(6)]
        grp_rows = [None, U_, CG_, GG_, AG_] + [M13_ + 512 * g for g in range(6)]
        grp_func = [AF.Identity, AF.Sigmoid, AF.Silu, AF.Silu, AF.Silu] + [AF.Sigmoid] * 6
        bcol = alloc("bcol", [128, NG, 4], F32)
        bbcol = Buf()
        for g in range(NG):
            P.dma("sp", bcol[:, g, :], b_in[l:l + 1, grp_cols[g]:grp_cols[g] + 512].rearrange("o (nb p) -> p (o nb)", p=128),
                  w=[bbcol], allow_slow_non_contiguous=True)
        wst = [alloc("wst%d" % i, [128, 8, 512], BF16) for i in range(3)]
        bwst = [Buf() for _ in range(3)]
        stg = [alloc("stg%d" % i, [128, 512], BF16) for i in range(3)]
        bstg = [Buf() for _ in range(3)]
        tv = [alloc("tv%d" % i, [128, 512], F32) for i in range(2)]
        tg = [alloc("tg%d" % i, [128, 512], F32) for i in range(2)]
        btv, btg = [Buf(), Buf()], [Buf(), Buf()]
        kk = [0, 0]

        def proj(wbuf, bw, mb, c0, cn, pb):
            tl = [bhx[t] for t in range(c0 // 128, (c0 + cn) // 128)]
            for kb in range(8):
                mm(PS[pb][:, :cn], wbuf[:, kb, mb * 128:(mb + 1) * 128], hxT[:, kb, c0:c0 + cn], kb == 0, kb == 7,
                   [bw] + tl, [bPS[pb]])

        P.dma("pool", wst[0][:], kview(w_in[:, 0:512]), w=[bwst[0]])
        P.dma("pool", wst[1][:], kview(w_in[:, 512:1024]), w=[bwst[1]])
        for mb in range(4):
            for (c0, cn) in chunks:
                pa, pg = kk[0] % 4, (kk[0] + 1) % 4
                kk[0] += 2
                j = kk[1] % 2
                s3 = kk[1] % 3
                kk[1] += 1
                proj(wst[0], bwst[0], mb, c0, cn, pa)
                proj(wst[1], bwst[1], mb, c0, cn, pg)
                act(tv[j][:, :cn], PS[pa][:, :cn], AF.Identity, [bPS[pa], bbcol], [btv[j]], bias=bcol[:, 0, mb:mb + 1])
                act(tg[j][:, :cn], PS[pg][:, :cn], AF.Sigmoid, [bPS[pg], bbcol], [btg[j]], bias=bcol[:, 1, mb:mb + 1])
                tt("dve", stg[s3][:, :cn], tv[j][:, :cn], tg[j][:, :cn], ALU.mult, [btv[j], btg[j]], [bstg[s3]])
                P.dma("sp", PXF[U_ + mb * 128:U_ + (mb + 1) * 128, c0:c0 + cn], stg[s3][:, :cn], r=[bstg[s3]], w=[bPXF])
        for g in range(2, NG):
            wi = g % 3
            P.dma("pool", wst[wi][:], kview(w_in[:, grp_cols[g]:grp_cols[g] + 512]), w=[bwst[wi]])
            for mb in range(4):
                for (c0, cn) in chunks:
                    pa = kk[0] % 4
                    kk[0] += 1
                    s3 = kk[1] % 3
                    kk[1] += 1
                    proj(wst[wi], bwst[wi], mb, c0, cn, pa)
                    act(stg[s3][:, :cn], PS[pa][:, :cn], grp_func[g], [bPS[pa], bbcol], [bstg[s3]], bias=bcol[:, g, mb:mb + 1])
                    r0 = grp_rows[g] + mb * 128
                    P.dma("sp", PXF[r0:r0 + 128, c0:c0 + cn], stg[s3][:, :cn], r=[bstg[s3]], w=[bPXF])

        if stop == 'B2':
            raise _Stop(nc, P)
        P.tag = "L%d.G" % l
        P.barrier()
        arena_reset()
        WG = 2
        Sst = alloc("Sst", [128, 2, 128], F32)
        Sbf = [alloc("Sbf%d" % i, [128, 2, 128], BF16) for i in range(2)]
        bS, bSbf = Buf(), [Buf(), Buf()]
        g_la = [alloc("g_la%d" % i, [128, 256], F32) for i in range(WG)]
        g_qk = [alloc("g_qk%d" % i, [128, 512], F32) for i in range(WG)]
        g_v = [alloc("g_v%d" % i, [128, 512], BF16) for i in range(WG)]
        g_e13 = [alloc("g_e13%d" % i, [128, 512], F32) for i in range(WG)]
        g_e2 = [alloc("g_e2%d" % i, [128, 256], F32) for i in range(WG)]
        g_qin = [alloc("g_qin%d" % i, [128, 256], BF16) for i in range(WG)]
        g_kin = [alloc("g_kin%d" % i, [128, 256], BF16) for i in range(WG)]
        g_kst = [alloc("g_kst%d" % i, [128, 2, 2, 128], BF16) for i in range(WG)]
        g_qpad = [alloc("g_qpad%d" % i, [128, 2, 2, 128], BF16) for i in range(WG)]
        g_kT = [alloc("g_kT%d" % i, [128, 2, 128], BF16) for i in range(WG)]
        g_dec = [alloc("g_dec%d" % i, [128, 4], F32) for i in range(WG)]
        g_att = [alloc("g_att%d" % i, [128, 4, 128], BF16) for i in range(WG)]
        g_o = [alloc("g_o%d" % i, [128, 4, 128], F32) for i in range(WG)]
        (bg_la, bg_qk, bg_v, bg_e13, bg_e2, bg_qin, bg_kin, bg_kst, bg_qpad, bg_kT, bg_dec, bg_att, bg_o) = [[Buf() for _ in range(WG)] for _ in range(13)]
        for i in range(WG):
            memset("pool", g_qpad[i][:], 0.0, [bg_qpad[i]])
            memset("pool", g_kst[i][:], 0.0, [bg_kst[i]])

        def prepG(t, i, dr):
            mcum, mrest, matt = masks[:, 2 * dr, :], masks[:, 2 * dr + 1, :], masks[:, 4 + dr, :]
            tc_ = slice(t * 128, (t + 1) * 128)
            pc, pa, pcn = i, 2 + i, 6 + i
            PTc = PS[pc].bitcast(BF16).rearrange("p (a b) -> p a b", a=8)
            P.dma("sp", g_la[i][:], GZ[tc_, dr * 256:(dr + 1) * 256], r=[bGZ], w=[bg_la[i]])
            P.dma("sp", g_qk[i][:], GQK[tc_, :], r=[bGQK], w=[bg_qk[i]])
            P.dma("sp", g_v[i][:], GV[tc_, :], r=[bGV], w=[bg_v[i]])
            yield
            mm(PS[pc][:, 0:256], mcum, g_la[i][:], True, True, [bCONST, bg_la[i]], [bPS[pc]])
            mm(PS[pc][:, 256:512], mrest, g_la[i][:], True, True, [bCONST, bg_la[i]], [bPS[pc]])
            yield
            act(g_e13[i][:], PS[pc], AF.Exp, [bPS[pc]], [bg_e13[i]])
            yield
            act(g_e2[i][:], PS[pc][:, 0:256], AF.Exp, [bPS[pc]], [bg_e2[i]], scale=-1.0)
            yield
            stt(g_qin[i][:], g_qk[i][:, 0:256], 0.125, g_e13[i][:, 0:256], ALU.mult, ALU.mult, [bg_qk[i], bg_e13[i]], [bg_qin[i]])
            yield
            tt("dve", g_kin[i][:], g_qk[i][:, 256:512], g_e2[i][:], ALU.mult, [bg_qk[i], bg_e2[i]], [bg_kin[i]])
            yield
            for hh in range(2):
                tt("dve", g_kst[i][:, :, hh, hh * 64:(hh + 1) * 64],
                   g_qk[i][:, 256:512].rearrange("p (b h d) -> p b h d", b=2, h=2)[:, :, hh, :],
                   g_e13[i][:, 256:512].rearrange("p (b h d) -> p b h d", b=2, h=2)[:, :, hh, :], ALU.mult,
                   [bg_qk[i], bg_e13[i]], [bg_kst[i]])
                yield
            for b2 in range(2):
                tr(PTc[:, b2, :], g_qin[i][:, b2 * 128:(b2 + 1) * 128], ident_bf[:], [bg_qin[i], bCONST], [bPS[pc]])
            for b2 in range(2):
                tr(PTc[:, 2 + b2, :], g_kin[i][:, b2 * 128:(b2 + 1) * 128], ident_bf[:], [bg_kin[i], bCONST], [bPS[pc]])
            for b2 in range(2):
                mm(PS[pc][:, 256 + 2 * b2:256 + 2 * b2 + 2], g_la[i][:, b2 * 128:(b2 + 1) * 128], neg16[:], True, True, [bg_la[i], bCONST], [bPS[pc]])
            yield
            cp("act", g_qpad[i][0:64, :, 0, :], PTc[0:64, 0:2, :], [bPS[pc]], [bg_qpad[i]])
            yield
            cp("act", g_qpad[i][64:128, :, 1, :], PTc[64:128, 0:2, :], [bPS[pc]], [bg_qpad[i]])
            yield
            cp("act", g_kT[i][:], PTc[:, 2:4, :], [bPS[pc]], [bg_kT[i]])
            yield
            act(g_dec[i][:], PS[pc][:, 256:260], AF.Exp, [bPS[pc]], [bg_dec[i]])
            yield
            for h in range(4):
                b2 = h // 2
                mm(PS[pa][:, h * 128:(h + 1) * 128], g_kT[i][:, b2, :], g_qpad[i][:, b2, h % 2, :], True, True,
                   [bg_kT[i], bg_qpad[i]], [bPS[pa]])
            yield
            tt("dve", g_att[i][:], PS[pa].rearrange("p (h t) -> p h t", h=4), matt.unsqueeze(1).broadcast_to([128, 4, 128]),
               ALU.mult, [bPS[pa], bCONST], [bg_att[i]])
            yield
            for b2 in range(2):
                for hh in range(2):
                    h = 2 * b2 + hh
                    mm(PS[pcn][:, b2 * 128:(b2 + 1) * 128], g_kst[i][:, b2, hh, :], g_v[i][:, h * 128:(h + 1) * 128],
                       hh == 0, hh == 1, [bg_kst[i], bg_v[i]], [bPS[pcn]])
            yield

        def stateG(t, i, dr, seq):
            tc_ = slice(t * 128, (t + 1) * 128)
            po_, pcn = 4 + i, 6 + i
            sprev, snew = Sbf[(seq + 1) % 2], Sbf[seq % 2]
            bprev, bnew = bSbf[(seq + 1) % 2], bSbf[seq % 2]
            for h in range(4):
                b2 = h // 2
                mm(PS[po_][:, h * 128:(h + 1) * 128], g_v[i][:, h * 128:(h + 1) * 128], g_att[i][:, h, :], True, False,
                   [bg_v[i], bg_att[i]], [bPS[po_]])
                mm(PS[po_][:, h * 128:(h + 1) * 128], sprev[:, b2, :], g_qpad[i][:, b2, h % 2, :], False, True,
                   [bprev, bg_qpad[i]], [bPS[po_]])
            cp("act", g_o[i][:], PS[po_].rearrange("p (h t) -> p h t", h=4), [bPS[po_]], [bg_o[i]])
            P.dma("sp", OBF[1 - dr].rearrange("(h p) n -> p h n", p=128)[:, :, tc_], g_o[i][:], r=[bg_o[i]], w=[bOBF[1 - dr]])
            for b2 in range(2):
                stt(Sst[:, b2, :], Sst[:, b2, :], g_dec[i][:, 2 * b2:2 * b2 + 1], PS[pcn][:, b2 * 128:(b2 + 1) * 128],
                    ALU.mult, ALU.add, [bS, bg_dec[i], bPS[pcn]], [bS])
            cp("pool", snew[:], Sst[:], [bS], [bnew])

        for dr in (1, 0):
            order = (list(range(TCT)) + list(range(TCT, NTILE))) if dr == 0 else (list(range(TCT - 1, -1, -1)) + list(range(NTILE - 1, TCT - 1, -1)))
            memset("dve", Sst[:], 0.0, [bS])
            memset("pool", Sbf[0][:], 0.0, [bSbf[0]])
            memset("pool", Sbf[1][:], 0.0, [bSbf[1]])
            free = list(range(WG))
            active = []
            done = {}
            nxt = 0
            nstate = 0
            while nstate < len(order):
                while nxt < len(order) and free:
                    slot = free.pop(0)
                    active.append([prepG(order[nxt], slot, dr), slot, nxt])
                    nxt += 1
                for item in list(active):
                    try:
                        next(item[0])
                    except StopIteration:
                        active.remove(item)
                        done[item[2]] = item[1]
                while nstate in done:
                    slot = done.pop(nstate)
                    stateG(order[nstate], slot, dr, nstate)
                    free.append(slot)
                    nstate += 1

        if stop == 'G':
            raise _Stop(nc, P)
        P.tag = "L%d.Cs" % l
        P.barrier()
        arena_reset()
        wco = alloc("wco", [128, 4, D], BF16)
        wgo = alloc("wgo", [128, 4, D], BF16)
        wao = alloc("wao", [64, 8, D], BF16)
        wo = alloc("wo", [128, 8, D], BF16)
        bwc = Buf()
        P.dma("pool", wco[:], kview(Wd["w_conv_out"][l]), w=[bwc])
        P.dma("pool", wgo[:], kview(Wd["w_gla_out"][l]), w=[bwc])
        P.dma("pool", wao[:], Wd["w_attn_out"][l].rearrange("(h p) n -> p h n", p=64), w=[bwc])
        P.dma("pool", wo[:], kview(Wd["w_out"][l]), w=[bwc])
        gateC = alloc("gateC", [128, D], F32)
        bgateC = Buf()
        dwT = alloc("dwT", [32, 512], F32)
        dww = alloc("dww", [128, 4, 32], F32)
        cpar = alloc("cpar", [128, 3, 4], F32)
        gng = alloc("gng", [128, 1], F32)
        diag = alloc("diag", [128, 4, 31, 128], BF16)
        bdwT, bdww, bcpar, bdiag = Buf(), Buf(), Buf(), Buf()
        memset("dve", dwT[:], 0.0, [bdwT])
        P.dma("sp", dwT[0:31, :], Wd["conv_dw_w"][l], w=[bdwT])
        for cb in range(4):
            tr(PS[0][:, cb * 32:cb * 32 + 32], dwT[0:32, cb * 128:(cb + 1) * 128], ident_f[0:32, 0:32], [bdwT, bCONST], [bPS[0]])
        cp("dve", dww[:, :, 0:31], PS[0][:, 0:128].rearrange("p (c j) -> p c j", c=4)[:, :, 0:31], [bPS[0]], [bdww])
        for pi, nm in enumerate(("conv_dw_b", "conv_ln_g", "conv_ln_b")):
            P.dma("sp", cpar[:, pi, :], Wd[nm][l:l + 1, :].rearrange("o (cb p) -> p (o cb)", p=128), w=[bcpar], allow_slow_non_contiguous=True)
        P.dma("sp", gng[:], Wd["gla_norm_g"][l:l + 1, :].rearrange("o n -> n o"), w=[bcpar], allow_slow_non_contiguous=True)
        for cb in range(4):
            tt("dve", diag[:, cb, :, :], ident_f[:].unsqueeze(1).broadcast_to([128, 31, 128]),
               dww[:, cb, 0:31].unsqueeze(2).broadcast_to([128, 31, 128]), ALU.mult, [bCONST, bdww], [bdiag])

        qT = [alloc("c_qT%d" % i, [128, 512], BF16) for i in range(2)]
        bqT = [Buf(), Buf()]
        Ee = [alloc("c_E%d" % i, [128, 2, 512], BF16) for i in range(3)]
        bE = [Buf() for _ in range(3)]
        attg = [alloc("c_attg%d" % i, [64, 2, 512], BF16) for i in range(3)]
        OAT = alloc("c_OAT", [64, 8, 512], BF16)
        rinv = [alloc("c_rinv%d" % i, [65, 512], F32) for i in range(2)]
        og = alloc("c_og", [64, 512], F32)
        battg, bOAT, brinv, bog = [Buf(), Buf(), Buf()], Buf(), [Buf(), Buf()], Buf()
        uh = alloc("c_uh", [128, 4, 512 + 30], BF16)
        cgt = alloc("c_cg", [128, 4, 512], BF16)
        cv = alloc("c_cv", [128, 4, 512], F32)
        csq = alloc("c_csq", [128, 512], F32)
        csqb = alloc("c_csqb", [128, 512], BF16)
        bcsqb = Buf()
        mean = alloc("c_mean", [128, 512], F32)
        msq = alloc("c_msq", [128, 512], F32)
        rstd = alloc("c_rstd", [128, 512], F32)
        yt = alloc("c_yt", [128, 512], F32)
        yt2 = alloc("c_yt2", [128, 512], F32)
        uo = alloc("c_uo", [128, 4, 512], BF16)
        buh, bcgt, bcv, bcsq, bmean, bmsq, brstd, byt, byt2, buo = [Buf() for _ in range(10)]
        ob = [alloc("c_ob0", [128, 512], F32)] * 2
        of_ = [alloc("c_of0", [128, 512], F32)] * 2
        bob, bof = [Buf()] * 2, [Buf()] * 2
        gsum = mean
        ggt = [alloc("c_gg%d" % i, [128, 512], BF16) for i in range(2)]
        go = alloc("c_go", [128, 4, 512], BF16)
        bgsum, bggt, bgo = bmean, [Buf(), Buf()], Buf()
        mg = [alloc("c_mg%d" % i, [128, 3, 512], BF16) for i in range(2)]
        bmg = [Buf(), Buf()]
        mt1, mt2, mt3 = yt, yt2, csq
        bmt1, bmt2, bmt3 = byt, byt2, bcsq
        mT = alloc("c_mT", [128, 8, 512], BF16)
        bmT = Buf()
        xr_ = alloc("c_xr", [128, D], F32)
        xo2 = [alloc("c_xo%d" % i, [128, D], F32) for i in range(2)]
        bxr, bxo2 = Buf(), [Buf(), Buf()]

        for ci, (c0, cn) in enumerate(chunks):
            is_ctx = ci == 0
            if is_ctx and not need_ctx:
                continue
            cs_ = slice(c0, c0 + cn)
            ktiles = list(range(TCT)) if is_ctx else list(range(NTILE))
            if ci <= 1:
                s_ = 1 if is_ctx else 0
                P.dma("sp", gateC[:], MOD[l, s_:s_ + 1, 2 * D:3 * D].partition_broadcast(128), r=[bMOD], w=[bgateC])
            seq0, seq1 = (0, TC) if is_ctx else (TC, NT)
            nkt = len(ktiles)

            def att_gen():
                P_tag = "L%d.Catt" % l
                its = [(j, kt_i, kt) for j in range(4) for kt_i, kt in enumerate(ktiles)]
                n = len(its)
                LA = 1

                def loads(j):
                    qi = j % 2
                    P.dma("sp", qT[qi][:, :cn], QT[j * 128:(j + 1) * 128, cs_], r=[bQT], w=[bqT[qi]])
                    for hh in range(2):
                        r0 = AG_ + (j + 4 * hh) * 64
                        P.dma("sp", attg[j % 3][:, hh, :cn], PXF[r0:r0 + 64, cs_], r=[bPXF], w=[battg[j % 3]])

                for idx in range(n + LA):
                    P.tag = P_tag
                    if idx < n:
                        j, kt_i, kt = its[idx]
                        qi, st_, eb_ = j % 2, idx % 2, idx % 3
                        if kt_i == 0:
                            if j == 0:
                                loads(0)
                            if j + 1 < 4:
                                loads(j + 1)
                        for hh in range(2):
                            p0 = 64 * hh
                            mm(PS[2 * st_ + hh][:, :cn], KT[p0:p0 + 64, kt * 128:(kt + 1) * 128], qT[qi][p0:p0 + 64, :cn], True, True,
                               [bKT[kt], bqT[qi]], [bPS[2 * st_ + hh]])
                        src2 = PSALL[:, st_ * 1024:(st_ + 1) * 1024].rearrange("p (h n) -> p h n", h=2)[:, :, :cn]
                        act(Ee[eb_][:, :, :cn], src2, AF.Exp, [bPS[2 * st_], bPS[2 * st_ + 1]], [bE[eb_]], scale=0.125)
                    if idx >= LA:
                        j, kt_i, kt = its[idx - LA]
                        qi, eb_ = j % 2, (idx - LA) % 3
                        for hh in range(2):
                            mm(PS[4 + hh][0:65, :cn], Vext[:, kt, hh, :], Ee[eb_][:, hh, :cn], kt_i == 0, kt_i == nkt - 1,
                               [bV[kt], bE[eb_]], [bPS[4 + hh]])
                        if kt_i == nkt - 1:
                            for hh in range(2):
                                h8 = j + 4 * hh
                                recip(rinv[hh][64:65, :cn], PS[4 + hh][64:65, :cn], [bPS[4 + hh]], [brinv[hh]])
                                mm(PS[6][0:64, :cn], ones_f[64:65, 0:64], rinv[hh][64:65, :cn], True, True, [bCONST, brinv[hh]], [bPS[6]])
                                tt("dve", og[:, :cn], PS[4 + hh][0:64, :cn], attg[j % 3][:, hh, :cn], ALU.mult, [bPS[4 + hh], battg[j % 3]], [bog])
                                tt("dve", OAT[:, h8, :cn], og[:, :cn], PS[6][0:64, :cn], ALU.mult, [bog, bPS[6]], [bOAT])
                    yield

            def side_gen():
                P.tag = "L%d.Cconv" % l
                lo, hi = max(c0 - 15, seq0), min(c0 + cn + 15, seq1)
                if lo > c0 - 15:
                    memset("pool", uh[:, :, 0:15], 0.0, [buh])
                if hi < c0 + cn + 15:
                    memset("pool", uh[:, :, cn + 15:cn + 30], 0.0, [buh])
                P.dma("sp", uh[:, :, lo - (c0 - 15):hi - (c0 - 15)], PXF[U_:U_ + 512, lo:hi].rearrange("(cb p) n -> p cb n", p=128),
                      r=[bPXF], w=[buh])
                P.dma("sp", cgt[:, :, :cn], PXF[CG_:CG_ + 512, cs_].rearrange("(cb p) n -> p cb n", p=128), r=[bPXF], w=[bcgt])
                yield
                for cb in range(4):
                    P.tag = "L%d.Cconv" % l
                    for j in range(31):
                        mm(PS[7][:, :cn], diag[:, cb, j, :], uh[:, cb, j:j + cn], j == 0, j == 30, [bdiag, buh], [bPS[7]])
                    yield
                    P.tag = "L%d.Cconv" % l
                    act(cv[:, cb, :cn], PS[7][:, :cn], AF.Identity, [bPS[7], bcpar], [bcv], bias=cpar[:, 0, cb:cb + 1])
                    yield
                P.tag = "L%d.Cconv" % l
                for cb in range(4):
                    mm(PS[7][:, :cn], ones_f[:], cv[:, cb, :cn], cb == 0, cb == 3, [bCONST, bcv], [bPS[7]])
                yield
                P.tag = "L%d.Cconv" % l
                act(mean[:, :cn], PS[7][:, :cn], AF.Copy, [bPS[7]], [bmean], scale=1.0 / 512)
                tt("dve", msq[:, :cn], mean[:, :cn], mean[:, :cn], ALU.mult, [bmean], [bmsq])
                yield
                for cb in range(4):
                    P.tag = "L%d.Cconv" % l
                    act(csqb[:, :cn], cv[:, cb, :cn], AF.Square, [bcv], [bcsqb])
                    yield
                    P.tag = "L%d.Cconv" % l
                    mm(PS[7][:, :cn], ones_bf[:], csqb[:, :cn], cb == 0, cb == 3, [bCONST, bcsqb], [bPS[7]])
                    yield
                P.tag = "L%d.Cconv" % l
                stt(rstd[:, :cn], PS[7][:, :cn], 1.0 / 512, msq[:, :cn], ALU.mult, ALU.subtract, [bPS[7], bmsq], [brstd])
                yield
                P.tag = "L%d.Cconv" % l
                act(rstd[:, :cn], rstd[:, :cn], AF.Ln, [brstd, bCONST], [brstd], bias=eps_t[:])
                yield
                P.tag = "L%d.Cconv" % l
                act(rstd[:, :cn], rstd[:, :cn], AF.Exp, [brstd], [brstd], scale=-0.5)
                yield
                for cb in range(4):
                    P.tag = "L%d.Cconv" % l
                    tt("dve", yt[:, :cn], cv[:, cb, :cn], mean[:, :cn], ALU.subtract, [bcv, bmean], [byt])
                    yield
                    P.tag = "L%d.Cconv" % l
                    tt("dve", yt2[:, :cn], yt[:, :cn], rstd[:, :cn], ALU.mult, [byt, brstd], [byt2])
                    yield
                    P.tag = "L%d.Cconv" % l
                    ts("dve", yt[:, :cn], yt2[:, :cn], cpar[:, 1, cb:cb + 1], cpar[:, 2, cb:cb + 1], ALU.mult, ALU.add, [byt2, bcpar], [byt])
                    yield
                    P.tag = "L%d.Cconv" % l
                    act(yt2[:, :cn], yt[:, :cn], AF.Exp, [byt], [byt2], scale=-1.0)
                    yield
                    P.tag = "L%d.Cconv" % l
                    ts("pool", yt2[:, :cn], yt2[:, :cn], 1.0, 1.0, ALU.add, ALU.mult, [byt2], [byt2])
                    yield
                    P.tag = "L%d.Cconv" % l
                    recip(yt2[:, :cn], yt2[:, :cn], [byt2], [byt2])
                    yield
                    P.tag = "L%d.Cconv" % l
                    tt("dve", yt[:, :cn], yt[:, :cn], yt2[:, :cn], ALU.mult, [byt, byt2], [byt])
                    yield
                    P.tag = "L%d.Cconv" % l
                    tt("pool", uo[:, cb, :cn], yt[:, :cn], cgt[:, cb, :cn], ALU.mult, [byt, bcgt], [buo])
                    yield
                for h in range(4):
                    i = h % 2
                    P.tag = "L%d.Cgla" % l
                    P.dma("sp", ggt[i][:, :cn], PXF[GG_ + h * 128:GG_ + (h + 1) * 128, cs_], r=[bPXF], w=[bggt[i]])
                    P.dma("sp", ob[i][:, :cn], OBF[0][h * 128:(h + 1) * 128, cs_], r=[bOBF[0]], w=[bob[i]])
                    P.dma("sp", of_[i][:, :cn], OBF[1][h * 128:(h + 1) * 128, cs_], r=[bOBF[1]], w=[bof[i]])
                    yield
                    P.tag = "L%d.Cgla" % l
                    tt("pool", gsum[:, :cn], ob[i][:, :cn], of_[i][:, :cn], ALU.add, [bob[i], bof[i]], [bgsum])
                    yield
                    P.tag = "L%d.Cgla" % l
                    act(csqb[:, :cn], gsum[:, :cn], AF.Square, [bgsum], [bcsqb])
                    yield
                    P.tag = "L%d.Cgla" % l
                    mm(PS[7][:, :cn], ones_bf[:], csqb[:, :cn], True, True, [bCONST, bcsqb], [bPS[7]])
                    yield
                    P.tag = "L%d.Cgla" % l
                    act(msq[:, :cn], PS[7][:, :cn], AF.Ln, [bPS[7], bCONST], [bmsq], bias=eps_t[:], scale=1.0 / 128)
                    yield
                    P.tag = "L%d.Cgla" % l
                    act(msq[:, :cn], msq[:, :cn], AF.Exp, [bmsq], [bmsq], scale=-0.5)
                    yield
                    P.tag = "L%d.Cgla" % l
                    tt("dve", yt[:, :cn], gsum[:, :cn], msq[:, :cn], ALU.mult, [bgsum, bmsq], [byt])
                    yield
                    P.tag = "L%d.Cgla" % l
                    stt(go[:, h, :cn], yt[:, :cn], gng[:], ggt[i][:, :cn], ALU.mult, ALU.mult, [byt, bcpar, bggt[i]], [bgo])
                    yield

            gA, gB = att_gen(), side_gen()
            nA = 4 * nkt + 1
            nB = 90
            acc = 0.0
            doneB = False
            for _ in gA:
                acc += float(nB) / nA
                while acc >= 1.0 and not doneB:
                    acc -= 1.0
                    try:
                        next(gB)
                    except StopIteration:
                        doneB = True
            if not doneB:
                for _ in gB:
                    pass
            P.tag = "L%d.Cmrg" % l
            for fb in range(8):
                i = fb % 2
                fs = slice(fb * 128, (fb + 1) * 128)
                P.dma("sp", mg[i][:, :, :cn], PXF[M13_:M13_ + 3 * D, cs_].rearrange("(g f p) n -> p g f n", g=3, f=8, p=128)[:, :, fb, :],
                      r=[bPXF], w=[bmg[i]])
                for cb in range(4):
                    mm(PS[0][:, :cn], wco[:, cb, fs], uo[:, cb, :cn], cb == 0, cb == 3, [bwc, buo], [bPS[0]])
                for h in range(4):
                    mm(PS[1][:, :cn], wgo[:, h, fs], go[:, h, :cn], h == 0, h == 3, [bwc, bgo], [bPS[1]])
                for h8 in range(8):
                    mm(PS[2][:, :cn], wao[:, h8, fs], OAT[:, h8, :cn], h8 == 0, h8 == 7, [bwc, bOAT], [bPS[2]])
                tt("dve", mt1[:, :cn], PS[0][:, :cn], mg[i][:, 0, :cn], ALU.mult, [bPS[0], bmg[i]], [bmt1])
                tt("dve", mt2[:, :cn], PS[1][:, :cn], mg[i][:, 1, :cn], ALU.mult, [bPS[1], bmg[i]], [bmt2])
                tt("dve", mt3[:, :cn], PS[2][:, :cn], mg[i][:, 2, :cn], ALU.mult, [bPS[2], bmg[i]], [bmt3])
                tt("pool", mt1[:, :cn], mt1[:, :cn], mt2[:, :cn], ALU.add, [bmt1, bmt2], [bmt1])
                tt("pool", mT[:, fb, :cn], mt1[:, :cn], mt3[:, :cn], ALU.add, [bmt1, bmt3], [bmT])
            P.tag = "L%d.Cfin" % l
            for s4 in range(cn // 128):
                t = c0 // 128 + s4
                xo_, bxo = xo2[s4 % 2], bxo2[s4 % 2]
                P.dma("sp", xr_[:], src_tile(t), r=[bXS], w=[bxr])
                for hf in range(2):
                    pb = 3 + hf
                    for fb in range(8):
                        mm(PS[pb][:], mT[:, fb, s4 * 128:(s4 + 1) * 128], wo[:, fb, hf * 512:(hf + 1) * 512], fb == 0, fb == 7,
                           [bmT, bwc], [bPS[pb]])
                    tt("dve", xo_[:, hf * 512:(hf + 1) * 512], PS[pb][:], gateC[:, hf * 512:(hf + 1) * 512],
                       ALU.mult, [bPS[pb], bgateC], [bxo])
                tt("pool", xo_[:], xo_[:], xr_[:], ALU.add, [bxo, bxr], [bxo])
                if last:
                    xrow = (t - TCT) * 128
                    final_ops.append(P.dma("sp", out_d[xrow:xrow + 128, :], xo_[:], r=[bxo], w=[bOUT]))
                else:
                    P.dma("sp", XS[t * 128:(t + 1) * 128, :], xo_[:], r=[bxo], w=[Buf()])

    P.emit(final_ops)
    return nc, P


_CACHE = {}


def kernel(**inputs):
    T, TC, DEPTH, B = 4096, 256, 4, 8
    x = np.asarray(inputs["x"], np.float32)
    B, T = x.shape[0], x.shape[1]
    TC = inputs["ctx"].shape[1]
    DEPTH = inputs["w_in"].shape[0]
    key = (T, TC, DEPTH)
    if key not in _CACHE:
        _CACHE[key] = build(T, TC, DEPTH)[0]
    nc = _CACHE[key]
    consts = host_consts(T)
    shared = {n: np.ascontiguousarray(np.asarray(inputs[n], np.float32)) for n, _ in WEIGHT_SPECS}
    shared["c_ctx"] = np.ascontiguousarray(np.asarray(inputs["c_ctx"], np.float32).reshape(1, D))
    shared.update(consts)
    in_maps = []
    for b in range(B):
        m = dict(shared)
        m["x"] = np.ascontiguousarray(x[b])
        m["ctx"] = np.ascontiguousarray(np.asarray(inputs["ctx"], np.float32)[b])
        m["c"] = np.ascontiguousarray(np.asarray(inputs["c"], np.float32)[b:b + 1])
        in_maps.append(m)
    res = run_bass_kernel_spmd(nc, in_maps, core_ids=list(range(B)))
    return np.stack([np.asarray(r["out"], np.float32) for r in res.results], axis=0)
```

```python
import contextlib
import numpy as np
import ml_dtypes
import concourse.bass as bass
import concourse.mybir as mybir
from concourse.bass_utils import run_bass_kernel_spmd

F32 = mybir.dt.float32
BF16 = mybir.dt.bfloat16
AF = mybir.ActivationFunctionType
ALU = mybir.AluOpType
AX = mybir.AxisListType

D = 1024
NIN = 7456
EPS = 1e-6


class Buf:
    __slots__ = ("name", "w", "r", "rd")

    def __init__(self, name=""):
        self.name = name
        self.w = None
        self.r = {}
        self.rd = []


class Op:
    __slots__ = ("eng", "fn", "deps", "needs_inc", "val", "dma", "dsem", "dval", "idx", "tag")

    def __init__(self, eng, fn, dma, idx):
        self.eng = eng
        self.fn = fn
        self.dma = dma
        self.deps = []
        self.needs_inc = False
        self.val = None
        self.dsem = None
        self.dval = None
        self.idx = idx


class Prog:
    ENGS = ("pe", "act", "dve", "pool", "sp")
    NDMA = 56

    def __init__(self, nc):
        self.nc = nc
        self.ops = {e: [] for e in self.ENGS}
        self.all = []
        self.dma_since_barrier = []
        self.tag = ""
        self.annotate = False

    def add(self, eng, fn, r=(), w=(), dma=False):
        op = Op(eng, fn, dma, len(self.all))
        op.tag = self.tag
        edeps = {}
        ddeps = {}

        def dep(x):
            if x is None or x is op:
                return
            if x.dma:
                ddeps[x.idx] = x
            else:
                if x.eng == "pe" and eng == "pe" and not dma:
                    return
                o = edeps.get(x.eng)
                if o is None or o.idx < x.idx:
                    edeps[x.eng] = x

        for b in r:
            dep(b.w)
        for b in w:
            dep(b.w)
            for x in b.r.values():
                dep(x)
            for x in b.rd:
                dep(x)
        for b in w:
            b.w = op
            b.r = {}
            b.rd = []
        for b in r:
            if b.w is not op:
                if dma:
                    b.rd.append(op)
                else:
                    b.r[eng] = op
        op.deps = list(edeps.values()) + list(ddeps.values())
        self.ops[eng].append(op)
        self.all.append(op)
        if dma:
            self.dma_since_barrier.append(op)
        return op

    def dma(self, q, out, in_, r=(), w=(), **kw):
        return self.add(q, lambda e: e.dma_start(out=out, in_=in_, **kw), r=r, w=w, dma=True)

    def barrier(self):
        last = {}
        for e in self.ENGS:
            last[e] = None
            for o in reversed(self.ops[e]):
                if not o.dma and o.fn is not None:
                    last[e] = o
                    break
        dmas = list(self.dma_since_barrier)
        self.dma_since_barrier = []
        for e in self.ENGS:
            op = Op(e, None, False, len(self.all))
            for e2 in self.ENGS:
                if e2 != e and last[e2] is not None:
                    op.deps.append(last[e2])
            op.deps.extend(dmas)
            self.ops[e].append(op)
            self.all.append(op)

    def emit(self, final_wait_ops=()):
        nc = self.nc
        fin = Op("sp", None, False, len(self.all))
        fin.deps = list(final_wait_ops)
        self.ops["sp"].append(fin)
        self.all.append(fin)
        sem_last = [None] * self.NDMA
        sem_cnt = [0] * self.NDMA
        k = 0
        for op in self.all:
            if op.dma:
                s = k % self.NDMA
                k += 1
                if sem_last[s] is not None:
                    op.deps.append(sem_last[s])
                sem_last[s] = op
                sem_cnt[s] += 16
                op.dsem = s
                op.dval = sem_cnt[s]
        for op in self.all:
            for d in op.deps:
                d.needs_inc = True
        cnt = {e: 0 for e in self.ENGS}
        for op in self.all:
            if not op.dma and op.fn is not None:
                if op.needs_inc:
                    cnt[op.eng] += 1
                op.val = cnt[op.eng]
        self.stats = {e: [len(self.ops[e]), cnt[e], 0] for e in self.ENGS}
        with contextlib.ExitStack() as st:
            esem = {e: st.enter_context(nc.semaphore("s_" + e)) for e in self.ENGS}
            dsem = [st.enter_context(nc.semaphore("d%d" % i)) for i in range(self.NDMA)]
            block = st.enter_context(nc.Block())

            def run(ename):
                def body(e):
                    waited = {}
                    nwait = 0
                    for op in self.ops[ename]:
                        for d in op.deps:
                            if d.dma:
                                key, v, sem = ("d", d.dsem), d.dval, dsem[d.dsem]
                            else:
                                if d.fn is None:
                                    continue
                                key, v, sem = ("e", d.eng), d.val, esem[d.eng]
                            if waited.get(key, 0) >= v:
                                continue
                            waited[key] = v
                            e.wait_ge(sem, v)
                            nwait += 1
                        if op.fn is None:
                            continue
                        ins = op.fn(e)
                        if self.annotate:
                            ins.annotate(op.tag)
                        if op.dma:
                            ins.then_inc(dsem[op.dsem], 16)
                        elif op.needs_inc:
                            ins.then_inc(esem[ename], 1)
                    self.stats[ename][2] = nwait

                return body

            block.tensor(run("pe"))
            block.scalar(run("act"))
            block.vector(run("dve"))
            block.gpsimd(run("pool"))
            block.sync(run("sp"))


def run_rr(n_items, make_gen, width):
    free = list(range(width))
    active = []
    nxt = 0
    while nxt < n_items or active:
        while nxt < n_items and free:
            slot = free.pop(0)
            active.append((make_gen(nxt, slot), slot))
            nxt += 1
        for item in list(active):
            try:
                next(item[0])
            except StopIteration:
                active.remove(item)
                free.append(item[1])


def host_consts(T):
    c = {}
    c["ident_bf"] = np.eye(128, dtype=np.float32).astype(ml_dtypes.bfloat16)
    c["ident_f"] = np.eye(128, dtype=np.float32)
    tp = np.arange(128)[:, None]
    t = np.arange(128)[None, :]
    mle = (tp <= t).astype(np.float32)
    mge = (tp >= t).astype(np.float32)
    s = -1.0 / 16.0
    masks = np.stack([s * mle, s * (1 - mle), s * mge, s * (1 - mge), mle, mge], axis=1)
    c["masks"] = np.ascontiguousarray(masks.astype(np.float32))
    c["ones_f"] = np.ones((128, 128), np.float32)
    c["neg16"] = np.full((128, 2), s, np.float32)
    n_rows = T // 64
    row = np.repeat(np.arange(n_rows, dtype=np.float32), 64)
    col = np.tile(np.arange(64, dtype=np.float32), n_rows)
    freqs = (np.float32(10000.0) ** (-np.arange(16, dtype=np.float32) / np.float32(16))).astype(np.float32)
    ar = (row[:, None] * freqs).astype(np.float32)
    ac = (col[:, None] * freqs).astype(np.float32)
    cr, sr, cc, sc = np.cos(ar), np.sin(ar), np.cos(ac), np.sin(ac)
    c["rope_cos"] = np.concatenate([cr, cr, cc, cc], axis=1).astype(np.float32)
    c["rope_sin"] = np.concatenate([-sr, sr, -sc, sc], axis=1).astype(np.float32)
    return c


CONST_SPECS = [("ident_bf", [128, 128], BF16), ("ident_f", [128, 128], F32), ("masks", [128, 6, 128], F32),
               ("ones_f", [128, 128], F32), ("neg16", [128, 2], F32)]

WEIGHT_SPECS = [("norm_g", [D]), ("w_mod", [D, 3 * D]), ("b_mod", [3 * D]), ("w_in", [D, NIN]), ("b_in", [NIN]),
                ("conv_dw_w", [31, 512]), ("conv_dw_b", [512]), ("conv_ln_g", [512]), ("conv_ln_b", [512]),
                ("w_conv_out", [512, D]), ("gla_w_gate", [2, 16, 256]), ("gla_b_gate", [2, 256]),
                ("gla_norm_g", [128]), ("w_gla_out", [512, D]), ("q_norm_g", [64]), ("k_norm_g", [64]),
                ("w_attn_out", [512, D]), ("w_out", [D, D])]


class _Stop(Exception):
    pass


def build(T, TC, DEPTH, debug=False, stop=None, annotate=False):
    try:
        return _build(T, TC, DEPTH, debug, stop, annotate)
    except _Stop as e:
        nc, P = e.args
        P.emit([])
        return nc, P


def _build(T, TC, DEPTH, debug=False, stop=None, annotate=False):
    NT = TC + T
    NTILE = NT // 128
    TCT = TC // 128
    chunks = [(0, TC)] + [(TC + 512 * i, 512) for i in range(T // 512)]

    nc = bass.Bass("TRN2", target_bir_lowering=False)
    P = Prog(nc)
    P.annotate = annotate

    def din(name, shape, dt=F32):
        return nc.dram_tensor(name, shape, dt, kind="ExternalInput").ap()

    x_d = din("x", [T, D])
    ctx_d = din("ctx", [TC, D])
    c_d = din("c", [1, D])
    cctx_d = din("c_ctx", [1, D])
    Wd = {n: din(n, [DEPTH] + s) for n, s in WEIGHT_SPECS}
    Cd = {n: din(n, s, dt) for n, s, dt in CONST_SPECS}
    cos_d = din("rope_cos", [T, 64])
    sin_d = din("rope_sin", [T, 64])
    out_d = nc.dram_tensor("out", [T, D], F32, kind="ExternalOutput").ap()

    skind = "ExternalOutput" if debug else "Internal"

    def dscr(name, shape, dt):
        return nc.dram_tensor(name, shape, dt, kind=skind).ap()

    XS = dscr("XS", [NT, D], F32)
    MOD = dscr("MOD", [DEPTH, 2, 3 * D], F32)
    NPX = 5120
    PXF = dscr("PXF", [NPX, NT], BF16)
    QT = dscr("QT", [512, NT], BF16)
    GQK = dscr("GQK", [NT, 512], F32)
    GV = dscr("GV", [NT, 512], BF16)
    GZ = dscr("GZ", [NT, 512], F32)
    OBF = [dscr("OB", [512, NT], F32), dscr("OF", [512, NT], F32)]
    U_, CG_, GG_, AG_, M13_ = 0, 512, 1024, 1536, 2048
    bXS, bMOD, bPXF, bQT, bGQK, bGV, bGZ = [Buf(n) for n in ("XS", "MOD", "PXF", "QT", "GQK", "GV", "GZ")]
    bOBF = [Buf("OB"), Buf("OF")]
    bOUT = Buf("OUT")

    SB_LO, SB_HI = 16512, 229344
    state = {"persist": SB_LO, "top": SB_LO, "n": 0}

    def alloc(name, shape, dt):
        esz = 2 if dt == BF16 else 4
        nbytes = esz
        for s_ in shape[1:]:
            nbytes *= s_
        nbytes = (nbytes + 63) // 64 * 64
        off = state["top"]
        state["top"] += nbytes
        assert state["top"] <= SB_HI, ("SBUF overflow", name, state["top"])
        state["n"] += 1
        return nc.alloc_sbuf_tensor_at("%s_%d" % (name, state["n"]), shape, dt, offset=off)

    def arena_reset():
        state["top"] = state["persist"]

    PSALL = nc.alloc_psum_tensor("psall", [128, 4096], F32)
    PS = [PSALL[:, i * 512:(i + 1) * 512] for i in range(8)]
    bPS = [Buf("ps%d" % i) for i in range(8)]

    def mm(out, lhsT, rhs, start, stop, r, w):
        return P.add("pe", lambda e: e.matmul(out, lhsT=lhsT, rhs=rhs, start=start, stop=stop), r=r, w=w)

    def tr(out, in_, ident, r, w):
        return P.add("pe", lambda e: e.transpose(out=out, in_=in_, identity=ident), r=r, w=w)

    def act(out, in_, func, r, w, bias=None, scale=None, accum=None):
        kw = {}
        if bias is not None:
            kw["bias"] = bias
        if scale is not None:
            kw["scale"] = scale
        if accum is not None:
            kw["accum_out"] = accum
        return P.add("act", lambda e: e.activation(out=out, in_=in_, func=func, **kw), r=r, w=w)

    def tt(eng, out, in0, in1, op, r, w):
        return P.add(eng, lambda e: e.tensor_tensor(out=out, in0=in0, in1=in1, op=op), r=r, w=w)

    def ts(eng, out, in0, s1, s2, op0, op1, r, w):
        return P.add(eng, lambda e: e.tensor_scalar(out=out, in0=in0, scalar1=s1, scalar2=s2, op0=op0, op1=op1), r=r, w=w)

    def stt(out, in0, scalar, in1, op0, op1, r, w):
        return P.add("dve", lambda e: e.scalar_tensor_tensor(out=out, in0=in0, scalar=scalar, in1=in1, op0=op0, op1=op1), r=r, w=w)

    def cp(eng, out, in_, r, w):
        if eng == "act":
            return P.add("act", lambda e: e.copy(out=out, in_=in_), r=r, w=w)
        return P.add(eng, lambda e: e.tensor_copy(out=out, in_=in_), r=r, w=w)

    def memset(eng, ap, v, w):
        return P.add(eng, lambda e: e.memset(ap, v), w=w)

    def recip(out, in_, r, w):
        return P.add("dve", lambda e: e.reciprocal(out=out, in_=in_), r=r, w=w)

    def kview(w2d):
        return w2d.rearrange("(kb p) n -> p kb n", p=128)

    ident_bf = alloc("ident_bf", [128, 128], BF16)
    ident_f = alloc("ident_f", [128, 128], F32)
    masks = alloc("masks", [128, 6, 128], F32)
    ones_f = alloc("ones_f", [128, 128], F32)
    neg16 = alloc("neg16", [128, 2], F32)
    eps_t = alloc("eps", [128, 1], F32)
    ones_bf = alloc("ones_bf", [128, 128], BF16)
    bCONST = Buf("const")
    for n, t_ in (("ident_bf", ident_bf), ("ident_f", ident_f), ("masks", masks), ("ones_f", ones_f), ("neg16", neg16)):
        P.dma("sp", t_[:], Cd[n], w=[bCONST])
    memset("dve", eps_t[:], EPS, [bCONST])
    memset("dve", ones_bf[:], 1.0, [bCONST])
    KT = alloc("KT", [128, NT], BF16)
    Vext = alloc("Vext", [128, NTILE, 2, 128], BF16)
    bKT = [Buf("KT%d" % i) for i in range(NTILE)]
    bV = [Buf("V%d" % i) for i in range(NTILE)]
    memset("pool", Vext[:], 1.0, bV)
    state["persist"] = state["top"]

    P.tag = "0"
    arena_reset()
    cc = alloc("cc", [128, 2, 8], F32)
    sc = alloc("sc", [128, 2, 8], F32)
    bm = alloc("bm", [2, 3 * D], F32)
    ng = alloc("ng", [2, D], F32)
    modsb = alloc("modsb", [2, 3 * D], F32)
    wm = [alloc("wm%d" % i, [128, 8, 512], F32) for i in range(2)]
    bcc, bsc, bbm, bng, bmodsb = Buf(), Buf(), Buf(), Buf(), Buf()
    bwm = [Buf(), Buf()]
    P.dma("sp", cc[:, 0, :], c_d.rearrange("o (kb p) -> p (o kb)", p=128), w=[bcc], allow_slow_non_contiguous=True)
    P.dma("sp", cc[:, 1, :], cctx_d.rearrange("o (kb p) -> p (o kb)", p=128), w=[bcc], allow_slow_non_contiguous=True)
    act(sc[:], cc[:], AF.Silu, [bcc], [bsc])
    k = 0
    for l in range(DEPTH):
        P.dma("sp", bm[:], Wd["b_mod"][l:l + 1, :].partition_broadcast(2), w=[bbm])
        P.dma("sp", ng[:], Wd["norm_g"][l:l + 1, :].partition_broadcast(2), w=[bng])
        for ch in range(6):
            wb_ = wm[k % 2]
            bw_ = bwm[k % 2]
            pb = k % 2
            k += 1
            P.dma("sp", wb_[:], kview(Wd["w_mod"][l][:, ch * 512:(ch + 1) * 512]), w=[bw_])
            for kb in range(8):
                mm(PS[pb][0:2, :], sc[:, :, kb], wb_[:, kb, :], kb == 0, kb == 7, [bsc, bw_], [bPS[pb]])
            tt("dve", modsb[:, ch * 512:(ch + 1) * 512], PS[pb][0:2, :], bm[:, ch * 512:(ch + 1) * 512], ALU.add,
               [bPS[pb], bbm], [bmodsb])
        stt(modsb[:, D:2 * D], modsb[:, D:2 * D], 1.0, ng[:], ALU.add, ALU.mult, [bmodsb, bng], [bmodsb])
        P.dma("sp", MOD[l], modsb[:], r=[bmodsb], w=[bMOD])

    if stop == '0':
        raise _Stop(nc, P)
    final_ops = []
    for l in range(DEPTH):
        need_ctx = l < DEPTH - 1
        last = l == DEPTH - 1
        w_in = Wd["w_in"][l]
        b_in = Wd["b_in"]

        def src_tile(t):
            if l == 0:
                return ctx_d[t * 128:(t + 1) * 128, :] if t < TCT else x_d[(t - TCT) * 128:(t - TCT + 1) * 128, :]
            return XS[t * 128:(t + 1) * 128, :]

        P.tag = "L%d.A" % l
        P.barrier()
        arena_reset()
        hxT = alloc("hxT", [128, 8, NT], BF16)
        mark_hx = state["top"]
        bhx = [Buf("hx%d" % i) for i in range(NTILE)]
        wlr = alloc("wlr", [128, 8, 32], BF16)
        blr = alloc("blr", [32, 1], F32)
        lrT = alloc("lrT", [33, NT], BF16)
        w_tm = alloc("w_tm", [128, 8, 1792], BF16)
        bias_tm = alloc("bias_tm", [128, 1792], F32)
        Gq = alloc("Gq", [128, 64], F32)
        Gk = alloc("Gk", [128, 64], F32)
        wupf = alloc("wupf", [33, 512], F32)
        wup = alloc("wup", [33, 512], BF16)
        bwlr, bblr, blrT, bwtm, bbtm, bG, bwupf, bwup = [Buf() for _ in range(8)]
        P.dma("pool", wlr[:], kview(w_in[:, 3072:3104]), w=[bwlr])
        P.dma("sp", blr[:], b_in[l:l + 1, 3072:3104].rearrange("o n -> n o"), w=[bblr], allow_slow_non_contiguous=True)
        memset("pool", lrT[32:33, :], 1.0, [blrT])
        tm_cols = [(1536, 2048, 0), (2048, 2560, 512), (3104, 3616, 1024), (3616, 3872, 1536)]
        for a, b, o in tm_cols:
            P.dma("pool", w_tm[:, :, o:o + (b - a)], kview(w_in[:, a:b]), w=[bwtm])
            P.dma("sp", bias_tm[:, o:o + (b - a)], b_in[l:l + 1, a:b].partition_broadcast(128), w=[bbtm])
        P.dma("sp", Gq[:], Wd["q_norm_g"][l:l + 1, :].partition_broadcast(128), w=[bG])
        P.dma("sp", Gk[:], Wd["k_norm_g"][l:l + 1, :].partition_broadcast(128), w=[bG])
        memset("dve", wupf[:], 0.0, [bwupf])
        P.dma("sp", wupf[0:16, 0:256], Wd["gla_w_gate"][l, 0], w=[bwupf])
        P.dma("sp", wupf[16:32, 256:512], Wd["gla_w_gate"][l, 1], w=[bwupf])
        P.dma("sp", wupf[32:33, 0:256], Wd["gla_b_gate"][l, 0:1, :], w=[bwupf])
        P.dma("sp", wupf[32:33, 256:512], Wd["gla_b_gate"][l, 1:2, :], w=[bwupf])
        cp("dve", wup[:], wupf[:], [bwupf], [bwup])

        mark_A = state["top"]
        modA = alloc("modA", [128, 2, 2, D], F32)
        bmodA = Buf()
        for s_ in range(2):
            P.dma("sp", modA[:, s_, 0, :], MOD[l, s_:s_ + 1, D:2 * D].partition_broadcast(128), r=[bMOD], w=[bmodA])
            P.dma("sp", modA[:, s_, 1, :], MOD[l, s_:s_ + 1, 0:D].partition_broadcast(128), r=[bMOD], w=[bmodA])
        WA = 3
        xt = [alloc("xt%d" % i, [128, D], F32) for i in range(WA)]
        sqj = [alloc("sqj%d" % i, [128, D], BF16) for i in range(WA)]
        junk = [alloc("junk%d" % i, [128, D], F32) for i in range(WA)]
        hb = [alloc("hb%d" % i, [128, D], BF16) for i in range(WA)]
        ssA = alloc("ssA", [128, WA], F32)
        bxt, bsqj, bjunk, bhb, bss = [[Buf() for _ in range(WA)] for _ in range(5)]
        PTvs = [PS[3 + i].bitcast(BF16).rearrange("p (a b) -> p a b", a=8) for i in range(WA)]

        def genA(t, i):
            s_ = 0 if t >= TCT else 1
            pb = 3 + i
            P.dma("sp", xt[i][:], src_tile(t), r=[bXS], w=[bxt[i]])
            yield
            act(sqj[i][:], xt[i][:], AF.Square, [bxt[i]], [bsqj[i], bss[i]], accum=ssA[:, i:i + 1])
            yield
            act(ssA[:, i:i + 1], ssA[:, i:i + 1], AF.Ln, [bss[i], bCONST], [bss[i]], bias=eps_t[:], scale=1.0 / D)
            yield
            act(ssA[:, i:i + 1], ssA[:, i:i + 1], AF.Exp, [bss[i]], [bss[i]], scale=-0.5)
            yield
            stt(junk[i][:], xt[i][:], ssA[:, i:i + 1], modA[:, s_, 0, :], ALU.mult, ALU.mult, [bxt[i], bss[i], bmodA], [bjunk[i]])
            yield
            tt("dve", hb[i][:], junk[i][:], modA[:, s_, 1, :], ALU.add, [bjunk[i], bmodA], [bhb[i]])
            yield
            for kb in range(8):
                tr(PTvs[i][:, kb, :], hb[i][:, kb * 128:(kb + 1) * 128], ident_bf[:], [bhb[i], bCONST], [bPS[pb]])
            yield
            cp("act", hxT[:, :, t * 128:(t + 1) * 128], PTvs[i][:], [bPS[pb]], [bhx[t]])
            yield

        run_rr(NTILE, genA, WA)
        P.barrier()
        state["top"] = mark_A

        if stop == 'A':
            raise _Stop(nc, P)
        P.tag = "L%d.B0" % l
        for ci, (c0, cn) in enumerate(chunks):
            pb = ci % 2
            tl = list(range(c0 // 128, (c0 + cn) // 128))
            for kb in range(8):
                mm(PS[pb][0:32, :cn], wlr[:, kb, :], hxT[:, kb, c0:c0 + cn], kb == 0, kb == 7,
                   [bwlr] + [bhx[t] for t in tl], [bPS[pb]])
            act(lrT[0:32, c0:c0 + cn], PS[pb][0:32, :cn], AF.Identity, [bPS[pb], bblr], [blrT], bias=blr[:])

        if stop == 'B0':
            raise _Stop(nc, P)
        P.tag = "L%d.B1" % l
        WB = 2
        qk_sb = [alloc("qk_sb%d" % i, [128, 512], F32) for i in range(WB)]
        v_sb = [alloc("v_sb%d" % i, [128, 512], BF16) for i in range(WB)]
        la_e = [alloc("la_e%d" % i, [128, 512], F32) for i in range(WB)]
        la_sb = [alloc("la_sb%d" % i, [128, 512], F32) for i in range(WB)]
        qT_sb = [alloc("qT_sb%d" % i, [128, 4, 128], BF16) for i in range(WB)]
        nq_o = [alloc("nq_o%d" % i, [128, 512], BF16) for i in range(WB)]
        nk_o = [alloc("nk_o%d" % i, [128, 128], BF16) for i in range(WB)]
        cs_t = [alloc("cs_t%d" % i, [128, 2, 64], F32) for i in range(WB)]
        bqk_sb, bv_sb, bla_e, bla_sb, bqT_sb, bnq_o, bnk_o, bcs = [[Buf() for _ in range(WB)] for _ in range(8)]
        nscr = {}
        for i in range(WB):
            for nm, wdt in (("q", 512), ("k", 128)):
                nscr[(nm, i)] = ([alloc("ns%s%d_%d" % (nm, i, z), [128, wdt], F32) for z in range(4)] + [alloc("nss%s%d" % (nm, i), [128, 8], F32)],
                                 [Buf() for _ in range(5)])

        def norm_rope(src, bsrc, nh, G, boff, rope_i, out, bout, scr, perm=False):
            n = nh * 64
            tl_, bl_ = nscr[scr]
            b_, sq_, a_, r_ = [x[:, :n] for x in tl_[:4]]
            ss_ = tl_[4][:, :nh]
            Bb, Bsq, Ba, Br, Bss = bl_

            def v3(ap):
                return ap.rearrange("p (h d) -> p h d", h=nh)

            if perm:
                def vin(ap):
                    return ap.rearrange("p (a j d) -> p a j d", a=2, j=4)
                vout = out.rearrange("p (j a d) -> p a j d", a=2, j=4)
                rsb = ss_.rearrange("p (a j) -> p a j", a=2).unsqueeze(3).broadcast_to([128, 2, 4, 64])
            else:
                vin = v3
                vout = v3(out)
                rsb = ss_.unsqueeze(2).broadcast_to([128, nh, 64])

            tt("dve", b_, src, bias_tm[:, boff:boff + n], ALU.add, [bsrc, bbtm], [Bb])
            yield
            act(sq_, b_, AF.Square, [Bb], [Bsq])
            yield
            P.add("dve", lambda e: e.tensor_reduce(out=ss_, in_=v3(sq_), axis=AX.X, op=ALU.add), r=[Bsq], w=[Bss])
            yield
            act(ss_, ss_, AF.Ln, [Bss, bCONST], [Bss], bias=eps_t[:], scale=1.0 / 64)
            yield
            act(ss_, ss_, AF.Exp, [Bss], [Bss], scale=-0.5)
            yield
            tt("dve", v3(b_), v3(b_), G[:].unsqueeze(1).broadcast_to([128, nh, 64]), ALU.mult, [Bb, bG], [Bb])
            yield
            if rope_i is None:
                tt("dve", vout, vin(b_), rsb, ALU.mult, [Bb, Bss], [bout])
                yield
                return
            tt("dve", v3(b_), v3(b_), ss_.unsqueeze(2).broadcast_to([128, nh, 64]), ALU.mult, [Bb, Bss], [Bb])
            yield
            cs = cs_t[rope_i]
            tt("dve", v3(a_), v3(b_), cs[:, 0, :].unsqueeze(1).broadcast_to([128, nh, 64]), ALU.mult, [Bb, bcs[rope_i]], [Ba])
            yield

            def v5(ap):
                return ap.rearrange("p (h a b c) -> p h a b c", h=nh, a=2, b=2)

            sn5 = cs[:, 1, :].rearrange("p (a b c) -> p a b c", a=2, b=2)
            for hf in range(2):
                tt("dve", v5(r_)[:, :, :, hf, :], v5(b_)[:, :, :, 1 - hf, :],
                   sn5[:, :, hf, :].unsqueeze(1).broadcast_to([128, nh, 2, 16]), ALU.mult, [Bb, bcs[rope_i]], [Br])
                yield
            tt("dve", vout, vin(a_), vin(r_), ALU.add, [Ba, Br], [bout])
            yield

        PT6 = PS[6].bitcast(BF16)

        def genB1(t, i):
            bA, bB, bC, bD = 4 * i, 4 * i + 1, 4 * i + 2, 4 * i + 3
            PTk = PS[bB].bitcast(BF16)
            PTq = PS[bC].bitcast(BF16).rearrange("p (a b) -> p a b", a=8)
            tc_ = slice(t * 128, (t + 1) * 128)
            is_x = t >= TCT
            if is_x:
                xr = (t - TCT) * 128
                P.dma("sp", cs_t[i][:, 0, :], cos_d[xr:xr + 128, :], w=[bcs[i]])
                P.dma("sp", cs_t[i][:, 1, :], sin_d[xr:xr + 128, :], w=[bcs[i]])
            for kb in range(8):
                mm(PS[bA], hxT[:, kb, tc_], w_tm[:, kb, 0:512], kb == 0, kb == 7, [bhx[t], bwtm], [bPS[bA]])
            yield
            tt("dve", qk_sb[i][:], PS[bA], bias_tm[:, 0:512], ALU.add, [bPS[bA], bbtm], [bqk_sb[i]])
            P.dma("sp", GQK[tc_, :], qk_sb[i][:], r=[bqk_sb[i]], w=[bGQK])
            yield
            for kb in range(8):
                mm(PS[bB], hxT[:, kb, tc_], w_tm[:, kb, 512:1024], kb == 0, kb == 7, [bhx[t], bwtm], [bPS[bB]])
            yield
            tt("dve", v_sb[i][:], PS[bB], bias_tm[:, 512:1024], ALU.add, [bPS[bB], bbtm], [bv_sb[i]])
            P.dma("sp", GV[tc_, :], v_sb[i][:], r=[bv_sb[i]], w=[bGV])
            yield
            mm(PS[bA], lrT[0:33, tc_], wup[0:33, :], True, True, [blrT, bwup], [bPS[bA]])
            yield
            act(la_e[i][:], PS[bA], AF.Exp, [bPS[bA]], [bla_e[i]], scale=-1.0)
            yield
            for kb in range(8):
                mm(PS[bD][:, 0:256], hxT[:, kb, tc_], w_tm[:, kb, 1536:1792], kb == 0, kb == 7, [bhx[t], bwtm], [bPS[bD]])
            yield
            for hh in range(2):
                tt("dve", Vext[:, t, hh, hh * 64:(hh + 1) * 64], PS[bD][:, 128 + hh * 64:192 + hh * 64],
                   bias_tm[:, 1664 + hh * 64:1728 + hh * 64], ALU.add, [bPS[bD], bbtm], [bV[t]])
            yield
            act(la_sb[i][:], la_e[i][:], AF.Ln, [bla_e[i]], [bla_sb[i]], bias=1.0)
            P.dma("sp", GZ[tc_, :], la_sb[i][:], r=[bla_sb[i]], w=[bGZ])
            yield
            do_q = is_x or need_ctx
            if do_q:
                for kb in range(8):
                    mm(PS[bC], hxT[:, kb, tc_], w_tm[:, kb, 1024:1536], kb == 0, kb == 7, [bhx[t], bwtm], [bPS[bC]])
                yield
            gk = norm_rope(PS[bD][:, 0:128], bPS[bD], 2, Gk, 1536, i if is_x else None, nk_o[i][:], bnk_o[i], ("k", i))
            gq = norm_rope(PS[bC], bPS[bC], 8, Gq, 1024, i if is_x else None, nq_o[i][:], bnq_o[i], ("q", i), perm=True) if do_q else iter(())
            alive = [gk, gq]
            while alive:
                for g in list(alive):
                    try:
                        next(g)
                    except StopIteration:
                        alive.remove(g)
                yield
            tr(PTk[:, 0:128], nk_o[i][:], ident_bf[:], [bnk_o[i], bCONST], [bPS[bB]])
            yield
            cp("act", KT[:, tc_], PTk[:, 0:128], [bPS[bB]], [bKT[t]])
            yield
            if do_q:
                for j in range(4):
                    tr(PTq[:, j, :], nq_o[i][:, j * 128:(j + 1) * 128], ident_bf[:], [bnq_o[i], bCONST], [bPS[bC]])
                yield
                cp("act", qT_sb[i][:], PTq[:, 0:4, :], [bPS[bC]], [bqT_sb[i]])
                P.dma("sp", QT.rearrange("(j p) n -> p j n", p=128)[:, :, tc_], qT_sb[i][:], r=[bqT_sb[i]], w=[bQT])
                yield

        def genB1_tagged(t, i):
            for _ in genB1(t, i):
                P.tag = "L%d.B1" % l
                yield

        run_rr(NTILE, genB1_tagged, WB)

        if stop == 'B1':
            raise _Stop(nc, P)
        P.tag = "L%d.B2" % l
        P.barrier()
        state["top"] = mark_hx
        NG = 11
        grp_cols = [0, 512, 1024, 2560, 3872] + [4384 + 512 * g for g in range(6)]
        grp_rows = [None, U_, CG_, GG_, AG_] + [M13_ + 512 * g for g in range(6)]
        grp_func = [AF.Identity, AF.Sigmoid, AF.Silu, AF.Silu, AF.Silu] + [AF.Sigmoid] * 6
        bcol = alloc("bcol", [128, NG, 4], F32)
        bbcol = Buf()
        for g in range(NG):
            P.dma("sp", bcol[:, g, :], b_in[l:l + 1, grp_cols[g]:grp_cols[g] + 512].rearrange("o (nb p) -> p (o nb)", p=128),
                  w=[bbcol], allow_slow_non_contiguous=True)
        wst = [alloc("wst%d" % i, [128, 8, 512], BF16) for i in range(3)]
        bwst = [Buf() for _ in range(3)]
        stg = [alloc("stg%d" % i, [128, 512], BF16) for i in range(3)]
        bstg = [Buf() for _ in range(3)]
        tv = [alloc("tv%d" % i, [128, 512], F32) for i in range(2)]
        tg = [alloc("tg%d" % i, [128, 512], F32) for i in range(2)]
        btv, btg = [Buf(), Buf()], [Buf(), Buf()]
        kk = [0, 0]

        def proj(wbuf, bw, mb, c0, cn, pb):
            tl = [bhx[t] for t in range(c0 // 128, (c0 + cn) // 128)]
            for kb in range(8):
                mm(PS[pb][:, :cn], wbuf[:, kb, mb * 128:(mb + 1) * 128], hxT[:, kb, c0:c0 + cn], kb == 0, kb == 7,
                   [bw] + tl, [bPS[pb]])

        P.dma("pool", wst[0][:], kview(w_in[:, 0:512]), w=[bwst[0]])
        P.dma("pool", wst[1][:], kview(w_in[:, 512:1024]), w=[bwst[1]])
        for mb in range(4):
            for (c0, cn) in chunks:
                pa, pg = kk[0] % 4, (kk[0] + 1) % 4
                kk[0] += 2
                j = kk[1] % 2
                s3 = kk[1] % 3
                kk[1] += 1
                proj(wst[0], bwst[0], mb, c0, cn, pa)
                proj(wst[1], bwst[1], mb, c0, cn, pg)
                act(tv[j][:, :cn], PS[pa][:, :cn], AF.Identity, [bPS[pa], bbcol], [btv[j]], bias=bcol[:, 0, mb:mb + 1])
                act(tg[j][:, :cn], PS[pg][:, :cn], AF.Sigmoid, [bPS[pg], bbcol], [btg[j]], bias=bcol[:, 1, mb:mb + 1])
                tt("dve", stg[s3][:, :cn], tv[j][:, :cn], tg[j][:, :cn], ALU.mult, [btv[j], btg[j]], [bstg[s3]])
                P.dma("sp", PXF[U_ + mb * 128:U_ + (mb + 1) * 128, c0:c0 + cn], stg[s3][:, :cn], r=[bstg[s3]], w=[bPXF])
        for g in range(2, NG):
            wi = g % 3
            P.dma("pool", wst[wi][:], kview(w_in[:, grp_cols[g]:grp_cols[g] + 512]), w=[bwst[wi]])
            for mb in range(4):
                for (c0, cn) in chunks:
                    pa = kk[0] % 4
                    kk[0] += 1
                    s3 = kk[1] % 3
                    kk[1] += 1
                    proj(wst[wi], bwst[wi], mb, c0, cn, pa)
                    act(stg[s3][:, :cn], PS[pa][:, :cn], grp_func[g], [bPS[pa], bbcol], [bstg[s3]], bias=bcol[:, g, mb:mb + 1])
                    r0 = grp_rows[g] + mb * 128
                    P.dma("sp", PXF[r0:r0 + 128, c0:c0 + cn], stg[s3][:, :cn], r=[bstg[s3]], w=[bPXF])

        if stop == 'B2':
            raise _Stop(nc, P)
        P.tag = "L%d.G" % l
        P.barrier()
        arena_reset()
        WG = 2
        Sst = alloc("Sst", [128, 2, 128], F32)
        Sbf = [alloc("Sbf%d" % i, [128, 2, 128], BF16) for i in range(2)]
        bS, bSbf = Buf(), [Buf(), Buf()]
        g_la = [alloc("g_la%d" % i, [128, 256], F32) for i in range(WG)]
        g_qk = [alloc("g_qk%d" % i, [128, 512], F32) for i in range(WG)]
        g_v = [alloc("g_v%d" % i, [128, 512], BF16) for i in range(WG)]
        g_e13 = [alloc("g_e13%d" % i, [128, 512], F32) for i in range(WG)]
        g_e2 = [alloc("g_e2%d" % i, [128, 256], F32) for i in range(WG)]
        g_qin = [alloc("g_qin%d" % i, [128, 256], BF16) for i in range(WG)]
        g_kin = [alloc("g_kin%d" % i, [128, 256], BF16) for i in range(WG)]
        g_kst = [alloc("g_kst%d" % i, [128, 2, 2, 128], BF16) for i in range(WG)]
        g_qpad = [alloc("g_qpad%d" % i, [128, 2, 2, 128], BF16) for i in range(WG)]
        g_kT = [alloc("g_kT%d" % i, [128, 2, 128], BF16) for i in range(WG)]
        g_dec = [alloc("g_dec%d" % i, [128, 4], F32) for i in range(WG)]
        g_att = [alloc("g_att%d" % i, [128, 4, 128], BF16) for i in range(WG)]
        g_o = [alloc("g_o%d" % i, [128, 4, 128], F32) for i in range(WG)]
        (bg_la, bg_qk, bg_v, bg_e13, bg_e2, bg_qin, bg_kin, bg_kst, bg_qpad, bg_kT, bg_dec, bg_att, bg_o) = [[Buf() for _ in range(WG)] for _ in range(13)]
        for i in range(WG):
            memset("pool", g_qpad[i][:], 0.0, [bg_qpad[i]])
            memset("pool", g_kst[i][:], 0.0, [bg_kst[i]])

        def prepG(t, i, dr):
            mcum, mrest, matt = masks[:, 2 * dr, :], masks[:, 2 * dr + 1, :], masks[:, 4 + dr, :]
            tc_ = slice(t * 128, (t + 1) * 128)
            pc, pa, pcn = i, 2 + i, 6 + i
            PTc = PS[pc].bitcast(BF16).rearrange("p (a b) -> p a b", a=8)
            P.dma("sp", g_la[i][:], GZ[tc_, dr * 256:(dr + 1) * 256], r=[bGZ], w=[bg_la[i]])
            P.dma("sp", g_qk[i][:], GQK[tc_, :], r=[bGQK], w=[bg_qk[i]])
            P.dma("sp", g_v[i][:], GV[tc_, :], r=[bGV], w=[bg_v[i]])
            yield
            mm(PS[pc][:, 0:256], mcum, g_la[i][:], True, True, [bCONST, bg_la[i]], [bPS[pc]])
            mm(PS[pc][:, 256:512], mrest, g_la[i][:], True, True, [bCONST, bg_la[i]], [bPS[pc]])
            yield
            act(g_e13[i][:], PS[pc], AF.Exp, [bPS[pc]], [bg_e13[i]])
            yield
            act(g_e2[i][:], PS[pc][:, 0:256], AF.Exp, [bPS[pc]], [bg_e2[i]], scale=-1.0)
            yield
            stt(g_qin[i][:], g_qk[i][:, 0:256], 0.125, g_e13[i][:, 0:256], ALU.mult, ALU.mult, [bg_qk[i], bg_e13[i]], [bg_qin[i]])
            yield
            tt("dve", g_kin[i][:], g_qk[i][:, 256:512], g_e2[i][:], ALU.mult, [bg_qk[i], bg_e2[i]], [bg_kin[i]])
            yield
            for hh in range(2):
                tt("dve", g_kst[i][:, :, hh, hh * 64:(hh + 1) * 64],
                   g_qk[i][:, 256:512].rearrange("p (b h d) -> p b h d", b=2, h=2)[:, :, hh, :],
                   g_e13[i][:, 256:512].rearrange("p (b h d) -> p b h d", b=2, h=2)[:, :, hh, :], ALU.mult,
                   [bg_qk[i], bg_e13[i]], [bg_kst[i]])
                yield
            for b2 in range(2):
                tr(PTc[:, b2, :], g_qin[i][:, b2 * 128:(b2 + 1) * 128], ident_bf[:], [bg_qin[i], bCONST], [bPS[pc]])
            for b2 in range(2):
                tr(PTc[:, 2 + b2, :], g_kin[i][:, b2 * 128:(b2 + 1) * 128], ident_bf[:], [bg_kin[i], bCONST], [bPS[pc]])
            for b2 in range(2):
                mm(PS[pc][:, 256 + 2 * b2:256 + 2 * b2 + 2], g_la[i][:, b2 * 128:(b2 + 1) * 128], neg16[:], True, True, [bg_la[i], bCONST], [bPS[pc]])
            yield
            cp("act", g_qpad[i][0:64, :, 0, :], PTc[0:64, 0:2, :], [bPS[pc]], [bg_qpad[i]])
            yield
            cp("act", g_qpad[i][64:128, :, 1, :], PTc[64:128, 0:2, :], [bPS[pc]], [bg_qpad[i]])
            yield
            cp("act", g_kT[i][:], PTc[:, 2:4, :], [bPS[pc]], [bg_kT[i]])
            yield
            act(g_dec[i][:], PS[pc][:, 256:260], AF.Exp, [bPS[pc]], [bg_dec[i]])
            yield
            for h in range(4):
                b2 = h // 2
                mm(PS[pa][:, h * 128:(h + 1) * 128], g_kT[i][:, b2, :], g_qpad[i][:, b2, h % 2, :], True, True,
                   [bg_kT[i], bg_qpad[i]], [bPS[pa]])
            yield
            tt("dve", g_att[i][:], PS[pa].rearrange("p (h t) -> p h t", h=4), matt.unsqueeze(1).broadcast_to([128, 4, 128]),
               ALU.mult, [bPS[pa], bCONST], [bg_att[i]])
            yield
            for b2 in range(2):
                for hh in range(2):
                    h = 2 * b2 + hh
                    mm(PS[pcn][:, b2 * 128:(b2 + 1) * 128], g_kst[i][:, b2, hh, :], g_v[i][:, h * 128:(h + 1) * 128],
                       hh == 0, hh == 1, [bg_kst[i], bg_v[i]], [bPS[pcn]])
            yield

        def stateG(t, i, dr, seq):
            tc_ = slice(t * 128, (t + 1) * 128)
            po_, pcn = 4 + i, 6 + i
            sprev, snew = Sbf[(seq + 1) % 2], Sbf[seq % 2]
            bprev, bnew = bSbf[(seq + 1) % 2], bSbf[seq % 2]
            for h in range(4):
                b2 = h // 2
                mm(PS[po_][:, h * 128:(h + 1) * 128], g_v[i][:, h * 128:(h + 1) * 128], g_att[i][:, h, :], True, False,
                   [bg_v[i], bg_att[i]], [bPS[po_]])
                mm(PS[po_][:, h * 128:(h + 1) * 128], sprev[:, b2, :], g_qpad[i][:, b2, h % 2, :], False, True,
                   [bprev, bg_qpad[i]], [bPS[po_]])
            cp("act", g_o[i][:], PS[po_].rearrange("p (h t) -> p h t", h=4), [bPS[po_]], [bg_o[i]])
            P.dma("sp", OBF[1 - dr].rearrange("(h p) n -> p h n", p=128)[:, :, tc_], g_o[i][:], r=[bg_o[i]], w=[bOBF[1 - dr]])
            for b2 in range(2):
                stt(Sst[:, b2, :], Sst[:, b2, :], g_dec[i][:, 2 * b2:2 * b2 + 1], PS[pcn][:, b2 * 128:(b2 + 1) * 128],
                    ALU.mult, ALU.add, [bS, bg_dec[i], bPS[pcn]], [bS])
            cp("pool", snew[:], Sst[:], [bS], [bnew])

        for dr in (1, 0):
            order = (list(range(TCT)) + list(range(TCT, NTILE))) if dr == 0 else (list(range(TCT - 1, -1, -1)) + list(range(NTILE - 1, TCT - 1, -1)))
            memset("dve", Sst[:], 0.0, [bS])
            memset("pool", Sbf[0][:], 0.0, [bSbf[0]])
            memset("pool", Sbf[1][:], 0.0, [bSbf[1]])
            free = list(range(WG))
            active = []
            done = {}
            nxt = 0
            nstate = 0
            while nstate < len(order):
                while nxt < len(order) and free:
                    slot = free.pop(0)
                    active.append([prepG(order[nxt], slot, dr), slot, nxt])
                    nxt += 1
                for item in list(active):
                    try:
                        next(item[0])
                    except StopIteration:
                        active.remove(item)
                        done[item[2]] = item[1]
                while nstate in done:
                    slot = done.pop(nstate)
                    stateG(order[nstate], slot, dr, nstate)
                    free.append(slot)
                    nstate += 1

        if stop == 'G':
            raise _Stop(nc, P)
        P.tag = "L%d.Cs" % l
        P.barrier()
        arena_reset()
        wco = alloc("wco", [128, 4, D], BF16)
        wgo = alloc("wgo", [128, 4, D], BF16)
        wao = alloc("wao", [128, 4, D], BF16)
        wo = alloc("wo", [128, 8, D], BF16)
        bwc = Buf()
        P.dma("pool", wco[:], kview(Wd["w_conv_out"][l]), w=[bwc])
        P.dma("pool", wgo[:], kview(Wd["w_gla_out"][l]), w=[bwc])
        for hh in range(2):
            P.dma("pool", wao[hh * 64:(hh + 1) * 64, :, :], Wd["w_attn_out"][l].rearrange("(hh j p) n -> p hh j n", hh=2, j=4, p=64)[:, hh, :, :], w=[bwc])
        P.dma("pool", wo[:], kview(Wd["w_out"][l]), w=[bwc])
        gateC = alloc("gateC", [128, D], F32)
        bgateC = Buf()
        dwT = alloc("dwT", [32, 512], F32)
        dww = alloc("dww", [128, 4, 32], F32)
        cpar = alloc("cpar", [128, 3, 4], F32)
        gng = alloc("gng", [128, 1], F32)
        diag = alloc("diag", [128, 4, 31, 128], BF16)
        bdwT, bdww, bcpar, bdiag = Buf(), Buf(), Buf(), Buf()
        memset("dve", dwT[:], 0.0, [bdwT])
        P.dma("sp", dwT[0:31, :], Wd["conv_dw_w"][l], w=[bdwT])
        for cb in range(4):
            tr(PS[0][:, cb * 32:cb * 32 + 32], dwT[0:32, cb * 128:(cb + 1) * 128], ident_f[0:32, 0:32], [bdwT, bCONST], [bPS[0]])
        cp("dve", dww[:, :, 0:31], PS[0][:, 0:128].rearrange("p (c j) -> p c j", c=4)[:, :, 0:31], [bPS[0]], [bdww])
        for pi, nm in enumerate(("conv_dw_b", "conv_ln_g", "conv_ln_b")):
            P.dma("sp", cpar[:, pi, :], Wd[nm][l:l + 1, :].rearrange("o (cb p) -> p (o cb)", p=128), w=[bcpar], allow_slow_non_contiguous=True)
        P.dma("sp", gng[:], Wd["gla_norm_g"][l:l + 1, :].rearrange("o n -> n o"), w=[bcpar], allow_slow_non_contiguous=True)
        for cb in range(4):
            tt("dve", diag[:, cb, :, :], ident_f[:].unsqueeze(1).broadcast_to([128, 31, 128]),
               dww[:, cb, 0:31].unsqueeze(2).broadcast_to([128, 31, 128]), ALU.mult, [bCONST, bdww], [bdiag])

        qT = [[alloc("c_qT%d_%d" % (i, hh), [128, 512], BF16) for hh in range(2)] for i in range(2)]
        bqT = [Buf(), Buf()]
        for i in range(2):
            for hh in range(2):
                memset("pool", qT[i][hh][:], 0.0, [bqT[i]])
        Ee = [alloc("c_E%d" % i, [128, 2, 512], BF16) for i in range(3)]
        bE = [Buf() for _ in range(3)]
        attg = [alloc("c_attg%d" % i, [128, 512], BF16) for i in range(3)]
        OAT = alloc("c_OAT", [128, 4, 512], BF16)
        rinv = [alloc("c_rinv%d" % i, [65, 512], F32) for i in range(2)]
        og = alloc("c_og", [128, 512], F32)
        battg, bOAT, brinv, bog = [Buf(), Buf(), Buf()], Buf(), [Buf(), Buf()], Buf()
        uh = alloc("c_uh", [128, 4, 512 + 30], BF16)
        cgt = alloc("c_cg", [128, 4, 512], BF16)
        cv = alloc("c_cv", [128, 4, 512], F32)
        csq = alloc("c_csq", [128, 512], F32)
        csqb = alloc("c_csqb", [128, 512], BF16)
        bcsqb = Buf()
        mean = alloc("c_mean", [128, 512], F32)
        msq = alloc("c_msq", [128, 512], F32)
        rstd = alloc("c_rstd", [128, 512], F32)
        yt = alloc("c_yt", [128, 512], F32)
        yt2 = alloc("c_yt2", [128, 512], F32)
        uo = alloc("c_uo", [128, 4, 512], BF16)
        buh, bcgt, bcv, bcsq, bmean, bmsq, brstd, byt, byt2, buo = [Buf() for _ in range(10)]
        ob = [alloc("c_ob0", [128, 512], F32)] * 2
        of_ = [alloc("c_of0", [128, 512], F32)] * 2
        bob, bof = [Buf()] * 2, [Buf()] * 2
        gsum = mean
        ggt = [alloc("c_gg%d" % i, [128, 512], BF16) for i in range(2)]
        go = alloc("c_go", [128, 4, 512], BF16)
        bgsum, bggt, bgo = bmean, [Buf(), Buf()], Buf()
        mg = [alloc("c_mg%d" % i, [128, 3, 512], BF16) for i in range(2)]
        bmg = [Buf(), Buf()]
        mt1, mt2, mt3 = yt, yt2, csq
        bmt1, bmt2, bmt3 = byt, byt2, bcsq
        mT = alloc("c_mT", [128, 8, 512], BF16)
        bmT = Buf()
        xr_ = alloc("c_xr", [128, D], F32)
        xo2 = [alloc("c_xo%d" % i, [128, D], F32) for i in range(2)]
        bxr, bxo2 = Buf(), [Buf(), Buf()]

        for ci, (c0, cn) in enumerate(chunks):
            is_ctx = ci == 0
            if is_ctx and not need_ctx:
                continue
            cs_ = slice(c0, c0 + cn)
            ktiles = list(range(TCT)) if is_ctx else list(range(NTILE))
            if ci <= 1:
                s_ = 1 if is_ctx else 0
                P.dma("sp", gateC[:], MOD[l, s_:s_ + 1, 2 * D:3 * D].partition_broadcast(128), r=[bMOD], w=[bgateC])
            seq0, seq1 = (0, TC) if is_ctx else (TC, NT)
            nkt = len(ktiles)

            def att_gen():
                P_tag = "L%d.Catt" % l
                its = [(j, kt_i, kt) for j in range(4) for kt_i, kt in enumerate(ktiles)]
                n = len(its)
                LA = 1

                def loads(j):
                    qi = j % 2
                    for hh in range(2):
                        P.dma("sp", qT[qi][hh][hh * 64:(hh + 1) * 64, :cn], QT[j * 128 + hh * 64:j * 128 + (hh + 1) * 64, cs_], r=[bQT], w=[bqT[qi]])
                        r0 = AG_ + (j + 4 * hh) * 64
                        P.dma("sp", attg[j % 3][hh * 64:(hh + 1) * 64, :cn], PXF[r0:r0 + 64, cs_], r=[bPXF], w=[battg[j % 3]])

                for idx in range(n + LA):
                    P.tag = P_tag
                    if idx < n:
                        j, kt_i, kt = its[idx]
                        qi, st_, eb_ = j % 2, idx % 2, idx % 3
                        if kt_i == 0:
                            if j == 0:
                                loads(0)
                            if j + 1 < 4:
                                loads(j + 1)
                        for hh in range(2):
                            mm(PS[2 * st_ + hh][:, :cn], KT[:, kt * 128:(kt + 1) * 128], qT[qi][hh][:, :cn], True, True,
                               [bKT[kt], bqT[qi]], [bPS[2 * st_ + hh]])
                        src2 = PSALL[:, st_ * 1024:(st_ + 1) * 1024].rearrange("p (h n) -> p h n", h=2)[:, :, :cn]
                        act(Ee[eb_][:, :, :cn], src2, AF.Exp, [bPS[2 * st_], bPS[2 * st_ + 1]], [bE[eb_]], scale=0.125)
                    if idx >= LA:
                        j, kt_i, kt = its[idx - LA]
                        qi, eb_ = j % 2, (idx - LA) % 3
                        for hh in range(2):
                            mm(PS[4 + hh][:, :cn], Vext[:, kt, hh, :], Ee[eb_][:, hh, :cn], kt_i == 0, kt_i == nkt - 1,
                               [bV[kt], bE[eb_]], [bPS[4 + hh]])
                        if kt_i == nkt - 1:
                            for hh in range(2):
                                sp_ = 64 if hh == 0 else 0
                                vs_ = slice(hh * 64, (hh + 1) * 64)
                                recip(rinv[hh][sp_:sp_ + 1, :cn], PS[4 + hh][sp_:sp_ + 1, :cn], [bPS[4 + hh]], [brinv[hh]])
                                mm(PS[6][:, :cn], ones_f[sp_:sp_ + 1, :], rinv[hh][sp_:sp_ + 1, :cn], True, True, [bCONST, brinv[hh]], [bPS[6]])
                                tt("dve", og[vs_, :cn], PS[4 + hh][vs_, :cn], attg[j % 3][vs_, :cn], ALU.mult, [bPS[4 + hh], battg[j % 3]], [bog])
                                tt("dve", OAT[vs_, j, :cn], og[vs_, :cn], PS[6][vs_, :cn], ALU.mult, [bog, bPS[6]], [bOAT])
                    yield

            def side_gen():
                P.tag = "L%d.Cconv" % l
                lo, hi = max(c0 - 15, seq0), min(c0 + cn + 15, seq1)
                if lo > c0 - 15:
                    memset("pool", uh[:, :, 0:15], 0.0, [buh])
                if hi < c0 + cn + 15:
                    memset("pool", uh[:, :, cn + 15:cn + 30], 0.0, [buh])
                P.dma("sp", uh[:, :, lo - (c0 - 15):hi - (c0 - 15)], PXF[U_:U_ + 512, lo:hi].rearrange("(cb p) n -> p cb n", p=128),
                      r=[bPXF], w=[buh])
                P.dma("sp", cgt[:, :, :cn], PXF[CG_:CG_ + 512, cs_].rearrange("(cb p) n -> p cb n", p=128), r=[bPXF], w=[bcgt])
                yield
                for cb in range(4):
                    P.tag = "L%d.Cconv" % l
                    for j in range(31):
                        mm(PS[7][:, :cn], diag[:, cb, j, :], uh[:, cb, j:j + cn], j == 0, j == 30, [bdiag, buh], [bPS[7]])
                    yield
                    P.tag = "L%d.Cconv" % l
                    act(cv[:, cb, :cn], PS[7][:, :cn], AF.Identity, [bPS[7], bcpar], [bcv], bias=cpar[:, 0, cb:cb + 1])
                    yield
                P.tag = "L%d.Cconv" % l
                for cb in range(4):
                    mm(PS[7][:, :cn], ones_f[:], cv[:, cb, :cn], cb == 0, cb == 3, [bCONST, bcv], [bPS[7]])
                yield
                P.tag = "L%d.Cconv" % l
                act(mean[:, :cn], PS[7][:, :cn], AF.Copy, [bPS[7]], [bmean], scale=1.0 / 512)
                tt("dve", msq[:, :cn], mean[:, :cn], mean[:, :cn], ALU.mult, [bmean], [bmsq])
                yield
                for cb in range(4):
                    P.tag = "L%d.Cconv" % l
                    act(csqb[:, :cn], cv[:, cb, :cn], AF.Square, [bcv], [bcsqb])
                    yield
                    P.tag = "L%d.Cconv" % l
                    mm(PS[7][:, :cn], ones_bf[:], csqb[:, :cn], cb == 0, cb == 3, [bCONST, bcsqb], [bPS[7]])
                    yield
                P.tag = "L%d.Cconv" % l
                stt(rstd[:, :cn], PS[7][:, :cn], 1.0 / 512, msq[:, :cn], ALU.mult, ALU.subtract, [bPS[7], bmsq], [brstd])
                yield
                P.tag = "L%d.Cconv" % l
                act(rstd[:, :cn], rstd[:, :cn], AF.Ln, [brstd, bCONST], [brstd], bias=eps_t[:])
                yield
                P.tag = "L%d.Cconv" % l
                act(rstd[:, :cn], rstd[:, :cn], AF.Exp, [brstd], [brstd], scale=-0.5)
                yield
                for cb in range(4):
                    P.tag = "L%d.Cconv" % l
                    tt("dve", yt[:, :cn], cv[:, cb, :cn], mean[:, :cn], ALU.subtract, [bcv, bmean], [byt])
                    yield
                    P.tag = "L%d.Cconv" % l
                    tt("dve", yt2[:, :cn], yt[:, :cn], rstd[:, :cn], ALU.mult, [byt, brstd], [byt2])
                    yield
                    P.tag = "L%d.Cconv" % l
                    ts("dve", yt[:, :cn], yt2[:, :cn], cpar[:, 1, cb:cb + 1], cpar[:, 2, cb:cb + 1], ALU.mult, ALU.add, [byt2, bcpar], [byt])
                    yield
                    P.tag = "L%d.Cconv" % l
                    act(yt2[:, :cn], yt[:, :cn], AF.Exp, [byt], [byt2], scale=-1.0)
                    yield
                    P.tag = "L%d.Cconv" % l
                    ts("pool", yt2[:, :cn], yt2[:, :cn], 1.0, 1.0, ALU.add, ALU.mult, [byt2], [byt2])
                    yield
                    P.tag = "L%d.Cconv" % l
                    recip(yt2[:, :cn], yt2[:, :cn], [byt2], [byt2])
                    yield
                    P.tag = "L%d.Cconv" % l
                    tt("dve", yt[:, :cn], yt[:, :cn], yt2[:, :cn], ALU.mult, [byt, byt2], [byt])
                    yield
                    P.tag = "L%d.Cconv" % l
                    tt("pool", uo[:, cb, :cn], yt[:, :cn], cgt[:, cb, :cn], ALU.mult, [byt, bcgt], [buo])
                    yield
                for h in range(4):
                    i = h % 2
                    P.tag = "L%d.Cgla" % l
                    P.dma("sp", ggt[i][:, :cn], PXF[GG_ + h * 128:GG_ + (h + 1) * 128, cs_], r=[bPXF], w=[bggt[i]])
                    P.dma("sp", ob[i][:, :cn], OBF[0][h * 128:(h + 1) * 128, cs_], r=[bOBF[0]], w=[bob[i]])
                    P.dma("sp", of_[i][:, :cn], OBF[1][h * 128:(h + 1) * 128, cs_], r=[bOBF[1]], w=[bof[i]])
                    yield
                    P.tag = "L%d.Cgla" % l
                    tt("pool", gsum[:, :cn], ob[i][:, :cn], of_[i][:, :cn], ALU.add, [bob[i], bof[i]], [bgsum])
                    yield
                    P.tag = "L%d.Cgla" % l
                    act(csqb[:, :cn], gsum[:, :cn], AF.Square, [bgsum], [bcsqb])
                    yield
                    P.tag = "L%d.Cgla" % l
                    mm(PS[7][:, :cn], ones_bf[:], csqb[:, :cn], True, True, [bCONST, bcsqb], [bPS[7]])
                    yield
                    P.tag = "L%d.Cgla" % l
                    act(msq[:, :cn], PS[7][:, :cn], AF.Ln, [bPS[7], bCONST], [bmsq], bias=eps_t[:], scale=1.0 / 128)
                    yield
                    P.tag = "L%d.Cgla" % l
                    act(msq[:, :cn], msq[:, :cn], AF.Exp, [bmsq], [bmsq], scale=-0.5)
                    yield
                    P.tag = "L%d.Cgla" % l
                    tt("dve", yt[:, :cn], gsum[:, :cn], msq[:, :cn], ALU.mult, [bgsum, bmsq], [byt])
                    yield
                    P.tag = "L%d.Cgla" % l
                    stt(go[:, h, :cn], yt[:, :cn], gng[:], ggt[i][:, :cn], ALU.mult, ALU.mult, [byt, bcpar, bggt[i]], [bgo])
                    yield

            gA, gB = att_gen(), side_gen()
            nA = 4 * nkt + 1
            nB = 90
            acc = 0.0
            doneB = False
            for _ in gA:
                acc += float(nB) / nA
                while acc >= 1.0 and not doneB:
                    acc -= 1.0
                    try:
                        next(gB)
                    except StopIteration:
                        doneB = True
            if not doneB:
                for _ in gB:
                    pass
            P.tag = "L%d.Cmrg" % l
            for fb in range(8):
                i = fb % 2
                fs = slice(fb * 128, (fb + 1) * 128)
                P.dma("sp", mg[i][:, :, :cn], PXF[M13_:M13_ + 3 * D, cs_].rearrange("(g f p) n -> p g f n", g=3, f=8, p=128)[:, :, fb, :],
                      r=[bPXF], w=[bmg[i]])
                for cb in range(4):
                    mm(PS[0][:, :cn], wco[:, cb, fs], uo[:, cb, :cn], cb == 0, cb == 3, [bwc, buo], [bPS[0]])
                for h in range(4):
                    mm(PS[1][:, :cn], wgo[:, h, fs], go[:, h, :cn], h == 0, h == 3, [bwc, bgo], [bPS[1]])
                for j4 in range(4):
                    mm(PS[2][:, :cn], wao[:, j4, fs], OAT[:, j4, :cn], j4 == 0, j4 == 3, [bwc, bOAT], [bPS[2]])
                tt("dve", mt1[:, :cn], PS[0][:, :cn], mg[i][:, 0, :cn], ALU.mult, [bPS[0], bmg[i]], [bmt1])
                tt("dve", mt2[:, :cn], PS[1][:, :cn], mg[i][:, 1, :cn], ALU.mult, [bPS[1], bmg[i]], [bmt2])
                tt("dve", mt3[:, :cn], PS[2][:, :cn], mg[i][:, 2, :cn], ALU.mult, [bPS[2], bmg[i]], [bmt3])
                tt("pool", mt1[:, :cn], mt1[:, :cn], mt2[:, :cn], ALU.add, [bmt1, bmt2], [bmt1])
                tt("pool", mT[:, fb, :cn], mt1[:, :cn], mt3[:, :cn], ALU.add, [bmt1, bmt3], [bmT])
            P.tag = "L%d.Cfin" % l
            for s4 in range(cn // 128):
                t = c0 // 128 + s4
                xo_, bxo = xo2[s4 % 2], bxo2[s4 % 2]
                P.dma("sp", xr_[:], src_tile(t), r=[bXS], w=[bxr])
                for hf in range(2):
                    pb = 3 + hf
                    for fb in range(8):
                        mm(PS[pb][:], mT[:, fb, s4 * 128:(s4 + 1) * 128], wo[:, fb, hf * 512:(hf + 1) * 512], fb == 0, fb == 7,
                           [bmT, bwc], [bPS[pb]])
                    tt("dve", xo_[:, hf * 512:(hf + 1) * 512], PS[pb][:], gateC[:, hf * 512:(hf + 1) * 512],
                       ALU.mult, [bPS[pb], bgateC], [bxo])
                tt("pool", xo_[:], xo_[:], xr_[:], ALU.add, [bxo, bxr], [bxo])
                if last:
                    xrow = (t - TCT) * 128
                    final_ops.append(P.dma("sp", out_d[xrow:xrow + 128, :], xo_[:], r=[bxo], w=[bOUT]))
                else:
                    P.dma("sp", XS[t * 128:(t + 1) * 128, :], xo_[:], r=[bxo], w=[Buf()])

    P.emit(final_ops)
    return nc, P


_CACHE = {}


def kernel(**inputs):
    T, TC, DEPTH, B = 4096, 256, 4, 8
    x = np.asarray(inputs["x"], np.float32)
    B, T = x.shape[0], x.shape[1]
    TC = inputs["ctx"].shape[1]
    DEPTH = inputs["w_in"].shape[0]
    key = (T, TC, DEPTH)
    if key not in _CACHE:
        _CACHE[key] = build(T, TC, DEPTH)[0]
    nc = _CACHE[key]
    consts = host_consts(T)
    shared = {n: np.ascontiguousarray(np.asarray(inputs[n], np.float32)) for n, _ in WEIGHT_SPECS}
    shared["c_ctx"] = np.ascontiguousarray(np.asarray(inputs["c_ctx"], np.float32).reshape(1, D))
    shared.update(consts)
    in_maps = []
    for b in range(B):
        m = dict(shared)
        m["x"] = np.ascontiguousarray(x[b])
        m["ctx"] = np.ascontiguousarray(np.asarray(inputs["ctx"], np.float32)[b])
        m["c"] = np.ascontiguousarray(np.asarray(inputs["c"], np.float32)[b:b + 1])
        in_maps.append(m)
    res = run_bass_kernel_spmd(nc, in_maps, core_ids=list(range(B)))
    return np.stack([np.asarray(r["out"], np.float32) for r in res.results], axis=0)
```

```python
import contextlib
import numpy as np
import ml_dtypes
import concourse.bass as bass
import concourse.mybir as mybir
from concourse.bass_utils import run_bass_kernel_spmd

F32 = mybir.dt.float32
BF16 = mybir.dt.bfloat16
AF = mybir.ActivationFunctionType
ALU = mybir.AluOpType
AX = mybir.AxisListType

D = 1024
NIN = 7456
EPS = 1e-6


class Buf:
    __slots__ = ("name", "w", "r", "rd")

    def __init__(self, name=""):
        self.name = name
        self.w = None
        self.r = {}
        self.rd = []


class Op:
    __slots__ = ("eng", "fn", "deps", "needs_inc", "val", "dma", "dsem", "dval", "idx", "tag")

    def __init__(self, eng, fn, dma, idx):
        self.eng = eng
        self.fn = fn
        self.dma = dma
        self.deps = []
        self.needs_inc = False
        self.val = None
        self.dsem = None
        self.dval = None
        self.idx = idx


class Prog:
    ENGS = ("pe", "act", "dve", "pool", "sp")
    NDMA = 56

    def __init__(self, nc):
        self.nc = nc
        self.ops = {e: [] for e in self.ENGS}
        self.all = []
        self.dma_since_barrier = []
        self.tag = ""
        self.annotate = False

    def add(self, eng, fn, r=(), w=(), dma=False):
        op = Op(eng, fn, dma, len(self.all))
        op.tag = self.tag
        edeps = {}
        ddeps = {}

        def dep(x):
            if x is None or x is op:
                return
            if x.dma:
                ddeps[x.idx] = x
            else:
                if x.eng == "pe" and eng == "pe" and not dma:
                    return
                o = edeps.get(x.eng)
                if o is None or o.idx < x.idx:
                    edeps[x.eng] = x

        for b in r:
            dep(b.w)
        for b in w:
            dep(b.w)
            for x in b.r.values():
                dep(x)
            for x in b.rd:
                dep(x)
        for b in w:
            b.w = op
            b.r = {}
            b.rd = []
        for b in r:
            if b.w is not op:
                if dma:
                    b.rd.append(op)
                else:
                    b.r[eng] = op
        op.deps = list(edeps.values()) + list(ddeps.values())
        self.ops[eng].append(op)
        self.all.append(op)
        if dma:
            self.dma_since_barrier.append(op)
        return op

    def dma(self, q, out, in_, r=(), w=(), **kw):
        return self.add(q, lambda e: e.dma_start(out=out, in_=in_, **kw), r=r, w=w, dma=True)

    def barrier(self):
        last = {}
        for e in self.ENGS:
            last[e] = None
            for o in reversed(self.ops[e]):
                if not o.dma and o.fn is not None:
                    last[e] = o
                    break
        dmas = list(self.dma_since_barrier)
        self.dma_since_barrier = []
        for e in self.ENGS:
            op = Op(e, None, False, len(self.all))
            for e2 in self.ENGS:
                if e2 != e and last[e2] is not None:
                    op.deps.append(last[e2])
            op.deps.extend(dmas)
            self.ops[e].append(op)
            self.all.append(op)

    def emit(self, final_wait_ops=()):
        nc = self.nc
        fin = Op("sp", None, False, len(self.all))
        fin.deps = list(final_wait_ops)
        self.ops["sp"].append(fin)
        self.all.append(fin)
        sem_last = [None] * self.NDMA
        sem_cnt = [0] * self.NDMA
        k = 0
        for op in self.all:
            if op.dma:
                s = k % self.NDMA
                k += 1
                if sem_last[s] is not None:
                    op.deps.append(sem_last[s])
                sem_last[s] = op
                sem_cnt[s] += 16
                op.dsem = s
                op.dval = sem_cnt[s]
        for op in self.all:
            for d in op.deps:
                d.needs_inc = True
        cnt = {e: 0 for e in self.ENGS}
        for op in self.all:
            if not op.dma and op.fn is not None:
                if op.needs_inc:
                    cnt[op.eng] += 1
                op.val = cnt[op.eng]
        self.stats = {e: [len(self.ops[e]), cnt[e], 0] for e in self.ENGS}
        with contextlib.ExitStack() as st:
            esem = {e: st.enter_context(nc.semaphore("s_" + e)) for e in self.ENGS}
            dsem = [st.enter_context(nc.semaphore("d%d" % i)) for i in range(self.NDMA)]
            block = st.enter_context(nc.Block())

            def run(ename):
                def body(e):
                    waited = {}
                    nwait = 0
                    for op in self.ops[ename]:
                        for d in op.deps:
                            if d.dma:
                                key, v, sem = ("d", d.dsem), d.dval, dsem[d.dsem]
                            else:
                                if d.fn is None:
                                    continue
                                key, v, sem = ("e", d.eng), d.val, esem[d.eng]
                            if waited.get(key, 0) >= v:
                                continue
                            waited[key] = v
                            e.wait_ge(sem, v)
                            nwait += 1
                        if op.fn is None:
                            continue
                        ins = op.fn(e)
                        if self.annotate:
                            ins.annotate(op.tag)
                        if op.dma:
                            ins.then_inc(dsem[op.dsem], 16)
                        elif op.needs_inc:
                            ins.then_inc(esem[ename], 1)
                    self.stats[ename][2] = nwait

                return body

            block.tensor(run("pe"))
            block.scalar(run("act"))
            block.vector(run("dve"))
            block.gpsimd(run("pool"))
            block.sync(run("sp"))


def run_rr(n_items, make_gen, width):
    free = list(range(width))
    active = []
    nxt = 0
    while nxt < n_items or active:
        while nxt < n_items and free:
            slot = free.pop(0)
            active.append((make_gen(nxt, slot), slot))
            nxt += 1
        for item in list(active):
            try:
                next(item[0])
            except StopIteration:
                active.remove(item)
                free.append(item[1])


def host_consts(T):
    c = {}
    c["ident_bf"] = np.eye(128, dtype=np.float32).astype(ml_dtypes.bfloat16)
    c["ident_f"] = np.eye(128, dtype=np.float32)
    tp = np.arange(128)[:, None]
    t = np.arange(128)[None, :]
    mle = (tp <= t).astype(np.float32)
    mge = (tp >= t).astype(np.float32)
    s = -1.0 / 16.0
    masks = np.stack([s * mle, s * (1 - mle), s * mge, s * (1 - mge), mle, mge], axis=1)
    c["masks"] = np.ascontiguousarray(masks.astype(np.float32))
    c["ones_f"] = np.ones((128, 128), np.float32)
    c["neg16"] = np.full((128, 2), s, np.float32)
    n_rows = T // 64
    row = np.repeat(np.arange(n_rows, dtype=np.float32), 64)
    col = np.tile(np.arange(64, dtype=np.float32), n_rows)
    freqs = (np.float32(10000.0) ** (-np.arange(16, dtype=np.float32) / np.float32(16))).astype(np.float32)
    ar = (row[:, None] * freqs).astype(np.float32)
    ac = (col[:, None] * freqs).astype(np.float32)
    cr, sr, cc, sc = np.cos(ar), np.sin(ar), np.cos(ac), np.sin(ac)
    c["rope_cos"] = np.concatenate([cr, cr, cc, cc], axis=1).astype(np.float32)
    c["rope_sin"] = np.concatenate([-sr, sr, -sc, sc], axis=1).astype(np.float32)
    return c


CONST_SPECS = [("ident_bf", [128, 128], BF16), ("ident_f", [128, 128], F32), ("masks", [128, 6, 128], F32),
               ("ones_f", [128, 128], F32), ("neg16", [128, 2], F32)]

WEIGHT_SPECS = [("norm_g", [D]), ("w_mod", [D, 3 * D]), ("b_mod", [3 * D]), ("w_in", [D, NIN]), ("b_in", [NIN]),
                ("conv_dw_w", [31, 512]), ("conv_dw_b", [512]), ("conv_ln_g", [512]), ("conv_ln_b", [512]),
                ("w_conv_out", [512, D]), ("gla_w_gate", [2, 16, 256]), ("gla_b_gate", [2, 256]),
                ("gla_norm_g", [128]), ("w_gla_out", [512, D]), ("q_norm_g", [64]), ("k_norm_g", [64]),
                ("w_attn_out", [512, D]), ("w_out", [D, D])]


class _Stop(Exception):
    pass


def build(T, TC, DEPTH, debug=False, stop=None, annotate=False):
    try:
        return _build(T, TC, DEPTH, debug, stop, annotate)
    except _Stop as e:
        nc, P = e.args
        P.emit([])
        return nc, P


def _build(T, TC, DEPTH, debug=False, stop=None, annotate=False):
    NT = TC + T
    NTILE = NT // 128
    TCT = TC // 128
    chunks = [(0, TC)] + [(TC + 512 * i, 512) for i in range(T // 512)]

    nc = bass.Bass("TRN2", target_bir_lowering=False)
    P = Prog(nc)
    P.annotate = annotate

    def din(name, shape, dt=F32):
        return nc.dram_tensor(name, shape, dt, kind="ExternalInput").ap()

    x_d = din("x", [T, D])
    ctx_d = din("ctx", [TC, D])
    c_d = din("c", [1, D])
    cctx_d = din("c_ctx", [1, D])
    Wd = {n: din(n, [DEPTH] + s) for n, s in WEIGHT_SPECS}
    Cd = {n: din(n, s, dt) for n, s, dt in CONST_SPECS}
    cos_d = din("rope_cos", [T, 64])
    sin_d = din("rope_sin", [T, 64])
    out_d = nc.dram_tensor("out", [T, D], F32, kind="ExternalOutput").ap()

    skind = "ExternalOutput" if debug else "Internal"

    def dscr(name, shape, dt):
        return nc.dram_tensor(name, shape, dt, kind=skind).ap()

    XS = dscr("XS", [NT, D], F32)
    MOD = dscr("MOD", [DEPTH, 2, 3 * D], F32)
    NPX = 5120
    PXF = dscr("PXF", [NPX, NT], BF16)
    QT = dscr("QT", [512, NT], BF16)
    GQK = dscr("GQK", [NT, 512], F32)
    GV = dscr("GV", [NT, 512], BF16)
    GZ = dscr("GZ", [NT, 512], F32)
    OBF = [dscr("OB", [512, NT], F32), dscr("OF", [512, NT], F32)]
    U_, CG_, GG_, AG_, M13_ = 0, 512, 1024, 1536, 2048
    bXS, bMOD, bPXF, bQT, bGQK, bGV, bGZ = [Buf(n) for n in ("XS", "MOD", "PXF", "QT", "GQK", "GV", "GZ")]
    bOBF = [Buf("OB"), Buf("OF")]
    bOUT = Buf("OUT")

    SB_LO, SB_HI = 16512, 229344
    state = {"persist": SB_LO, "top": SB_LO, "n": 0}

    def alloc(name, shape, dt):
        esz = 2 if dt == BF16 else 4
        nbytes = esz
        for s_ in shape[1:]:
            nbytes *= s_
        nbytes = (nbytes + 63) // 64 * 64
        off = state["top"]
        state["top"] += nbytes
        assert state["top"] <= SB_HI, ("SBUF overflow", name, state["top"])
        state["n"] += 1
        return nc.alloc_sbuf_tensor_at("%s_%d" % (name, state["n"]), shape, dt, offset=off)

    def arena_reset():
        state["top"] = state["persist"]

    PSALL = nc.alloc_psum_tensor("psall", [128, 4096], F32)
    PS = [PSALL[:, i * 512:(i + 1) * 512] for i in range(8)]
    bPS = [Buf("ps%d" % i) for i in range(8)]

    def mm(out, lhsT, rhs, start, stop, r, w):
        return P.add("pe", lambda e: e.matmul(out, lhsT=lhsT, rhs=rhs, start=start, stop=stop), r=r, w=w)

    def tr(out, in_, ident, r, w):
        return P.add("pe", lambda e: e.transpose(out=out, in_=in_, identity=ident), r=r, w=w)

    def act(out, in_, func, r, w, bias=None, scale=None, accum=None):
        kw = {}
        if bias is not None:
            kw["bias"] = bias
        if scale is not None:
            kw["scale"] = scale
        if accum is not None:
            kw["accum_out"] = accum
        return P.add("act", lambda e: e.activation(out=out, in_=in_, func=func, **kw), r=r, w=w)

    def tt(eng, out, in0, in1, op, r, w):
        return P.add(eng, lambda e: e.tensor_tensor(out=out, in0=in0, in1=in1, op=op), r=r, w=w)

    def ts(eng, out, in0, s1, s2, op0, op1, r, w):
        return P.add(eng, lambda e: e.tensor_scalar(out=out, in0=in0, scalar1=s1, scalar2=s2, op0=op0, op1=op1), r=r, w=w)

    def stt(out, in0, scalar, in1, op0, op1, r, w):
        return P.add("dve", lambda e: e.scalar_tensor_tensor(out=out, in0=in0, scalar=scalar, in1=in1, op0=op0, op1=op1), r=r, w=w)

    def cp(eng, out, in_, r, w):
        if eng == "act":
            return P.add("act", lambda e: e.copy(out=out, in_=in_), r=r, w=w)
        return P.add(eng, lambda e: e.tensor_copy(out=out, in_=in_), r=r, w=w)

    def memset(eng, ap, v, w):
        return P.add(eng, lambda e: e.memset(ap, v), w=w)

    def recip(out, in_, r, w):
        return P.add("dve", lambda e: e.reciprocal(out=out, in_=in_), r=r, w=w)

    def kview(w2d):
        return w2d.rearrange("(kb p) n -> p kb n", p=128)

    ident_bf = alloc("ident_bf", [128, 128], BF16)
    ident_f = alloc("ident_f", [128, 128], F32)
    masks = alloc("masks", [128, 6, 128], F32)
    ones_f = alloc("ones_f", [128, 128], F32)
    neg16 = alloc("neg16", [128, 2], F32)
    eps_t = alloc("eps", [128, 1], F32)
    ones_bf = alloc("ones_bf", [128, 128], BF16)
    bCONST = Buf("const")
    for n, t_ in (("ident_bf", ident_bf), ("ident_f", ident_f), ("masks", masks), ("ones_f", ones_f), ("neg16", neg16)):
        P.dma("sp", t_[:], Cd[n], w=[bCONST])
    memset("dve", eps_t[:], EPS, [bCONST])
    memset("dve", ones_bf[:], 1.0, [bCONST])
    KT = alloc("KT", [128, NT], BF16)
    Vext = alloc("Vext", [128, NTILE, 2, 65], BF16)
    bKT = [Buf("KT%d" % i) for i in range(NTILE)]
    bV = [Buf("V%d" % i) for i in range(NTILE)]
    memset("pool", Vext[:, :, :, 64:65], 1.0, bV)
    state["persist"] = state["top"]

    P.tag = "0"
    arena_reset()
    cc = alloc("cc", [128, 2, 8], F32)
    sc = alloc("sc", [128, 2, 8], F32)
    bm = alloc("bm", [2, 3 * D], F32)
    ng = alloc("ng", [2, D], F32)
    modsb = alloc("modsb", [2, 3 * D], F32)
    wm = [alloc("wm%d" % i, [128, 8, 512], F32) for i in range(2)]
    bcc, bsc, bbm, bng, bmodsb = Buf(), Buf(), Buf(), Buf(), Buf()
    bwm = [Buf(), Buf()]
    P.dma("sp", cc[:, 0, :], c_d.rearrange("o (kb p) -> p (o kb)", p=128), w=[bcc], allow_slow_non_contiguous=True)
    P.dma("sp", cc[:, 1, :], cctx_d.rearrange("o (kb p) -> p (o kb)", p=128), w=[bcc], allow_slow_non_contiguous=True)
    act(sc[:], cc[:], AF.Silu, [bcc], [bsc])
    k = 0
    for l in range(DEPTH):
        P.dma("sp", bm[:], Wd["b_mod"][l:l + 1, :].partition_broadcast(2), w=[bbm])
        P.dma("sp", ng[:], Wd["norm_g"][l:l + 1, :].partition_broadcast(2), w=[bng])
        for ch in range(6):
            wb_ = wm[k % 2]
            bw_ = bwm[k % 2]
            pb = k % 2
            k += 1
            P.dma("sp", wb_[:], kview(Wd["w_mod"][l][:, ch * 512:(ch + 1) * 512]), w=[bw_])
            for kb in range(8):
                mm(PS[pb][0:2, :], sc[:, :, kb], wb_[:, kb, :], kb == 0, kb == 7, [bsc, bw_], [bPS[pb]])
            tt("dve", modsb[:, ch * 512:(ch + 1) * 512], PS[pb][0:2, :], bm[:, ch * 512:(ch + 1) * 512], ALU.add,
               [bPS[pb], bbm], [bmodsb])
        stt(modsb[:, D:2 * D], modsb[:, D:2 * D], 1.0, ng[:], ALU.add, ALU.mult, [bmodsb, bng], [bmodsb])
        P.dma("sp", MOD[l], modsb[:], r=[bmodsb], w=[bMOD])

    if stop == '0':
        raise _Stop(nc, P)
    final_ops = []
    for l in range(DEPTH):
        need_ctx = l < DEPTH - 1
        last = l == DEPTH - 1
        w_in = Wd["w_in"][l]
        b_in = Wd["b_in"]

        def src_tile(t):
            if l == 0:
                return ctx_d[t * 128:(t + 1) * 128, :] if t < TCT else x_d[(t - TCT) * 128:(t - TCT + 1) * 128, :]
            return XS[t * 128:(t + 1) * 128, :]

        P.tag = "L%d.A" % l
        P.barrier()
        arena_reset()
        hxT = alloc("hxT", [128, 8, NT], BF16)
        mark_hx = state["top"]
        bhx = [Buf("hx%d" % i) for i in range(NTILE)]
        wlr = alloc("wlr", [128, 8, 32], BF16)
        blr = alloc("blr", [32, 1], F32)
        lrT = alloc("lrT", [33, NT], BF16)
        w_tm = alloc("w_tm", [128, 8, 1792], BF16)
        bias_tm = alloc("bias_tm", [128, 1792], F32)
        Gq = alloc("Gq", [128, 64], F32)
        Gk = alloc("Gk", [128, 64], F32)
        wupf = alloc("wupf", [33, 512], F32)
        wup = alloc("wup", [33, 512], BF16)
        bwlr, bblr, blrT, bwtm, bbtm, bG, bwupf, bwup = [Buf() for _ in range(8)]
        P.dma("pool", wlr[:], kview(w_in[:, 3072:3104]), w=[bwlr])
        P.dma("sp", blr[:], b_in[l:l + 1, 3072:3104].rearrange("o n -> n o"), w=[bblr], allow_slow_non_contiguous=True)
        memset("pool", lrT[32:33, :], 1.0, [blrT])
        tm_cols = [(1536, 2048, 0), (2048, 2560, 512), (3104, 3616, 1024), (3616, 3872, 1536)]
        for a, b, o in tm_cols:
            P.dma("pool", w_tm[:, :, o:o + (b - a)], kview(w_in[:, a:b]), w=[bwtm])
            P.dma("sp", bias_tm[:, o:o + (b - a)], b_in[l:l + 1, a:b].partition_broadcast(128), w=[bbtm])
        P.dma("sp", Gq[:], Wd["q_norm_g"][l:l + 1, :].partition_broadcast(128), w=[bG])
        P.dma("sp", Gk[:], Wd["k_norm_g"][l:l + 1, :].partition_broadcast(128), w=[bG])
        memset("dve", wupf[:], 0.0, [bwupf])
        P.dma("sp", wupf[0:16, 0:256], Wd["gla_w_gate"][l, 0], w=[bwupf])
        P.dma("sp", wupf[16:32, 256:512], Wd["gla_w_gate"][l, 1], w=[bwupf])
        P.dma("sp", wupf[32:33, 0:256], Wd["gla_b_gate"][l, 0:1, :], w=[bwupf])
        P.dma("sp", wupf[32:33, 256:512], Wd["gla_b_gate"][l, 1:2, :], w=[bwupf])
        cp("dve", wup[:], wupf[:], [bwupf], [bwup])

        mark_A = state["top"]
        modA = alloc("modA", [128, 2, 2, D], F32)
        bmodA = Buf()
        for s_ in range(2):
            P.dma("sp", modA[:, s_, 0, :], MOD[l, s_:s_ + 1, D:2 * D].partition_broadcast(128), r=[bMOD], w=[bmodA])
            P.dma("sp", modA[:, s_, 1, :], MOD[l, s_:s_ + 1, 0:D].partition_broadcast(128), r=[bMOD], w=[bmodA])
        WA = 4
        xt = [alloc("xt%d" % i, [128, D], F32) for i in range(WA)]
        sqj = [alloc("sqj%d" % i, [128, D], BF16) for i in range(WA)]
        junk = [alloc("junk%d" % i, [128, D], F32) for i in range(WA)]
        hb = [alloc("hb%d" % i, [128, D], BF16) for i in range(WA)]
        ssA = alloc("ssA", [128, WA], F32)
        bxt, bsqj, bjunk, bhb, bss = [[Buf() for _ in range(WA)] for _ in range(5)]
        PTvs = [PS[3 + i].bitcast(BF16).rearrange("p (a b) -> p a b", a=8) for i in range(WA)]

        def genA(t, i):
            s_ = 0 if t >= TCT else 1
            pb = 3 + i
            P.dma("sp", xt[i][:], src_tile(t), r=[bXS], w=[bxt[i]])
            yield
            act(sqj[i][:], xt[i][:], AF.Square, [bxt[i]], [bsqj[i], bss[i]], accum=ssA[:, i:i + 1])
            yield
            act(ssA[:, i:i + 1], ssA[:, i:i + 1], AF.Ln, [bss[i], bCONST], [bss[i]], bias=eps_t[:], scale=1.0 / D)
            yield
            act(ssA[:, i:i + 1], ssA[:, i:i + 1], AF.Exp, [bss[i]], [bss[i]], scale=-0.5)
            yield
            stt(junk[i][:], xt[i][:], ssA[:, i:i + 1], modA[:, s_, 0, :], ALU.mult, ALU.mult, [bxt[i], bss[i], bmodA], [bjunk[i]])
            yield
            tt("dve", hb[i][:], junk[i][:], modA[:, s_, 1, :], ALU.add, [bjunk[i], bmodA], [bhb[i]])
            yield
            for kb in range(8):
                tr(PTvs[i][:, kb, :], hb[i][:, kb * 128:(kb + 1) * 128], ident_bf[:], [bhb[i], bCONST], [bPS[pb]])
            yield
            cp("act", hxT[:, :, t * 128:(t + 1) * 128], PTvs[i][:], [bPS[pb]], [bhx[t]])
            yield

        run_rr(NTILE, genA, WA)
        P.barrier()
        state["top"] = mark_A

        if stop == 'A':
            raise _Stop(nc, P)
        P.tag = "L%d.B0" % l
        for ci, (c0, cn) in enumerate(chunks):
            pb = ci % 2
            tl = list(range(c0 // 128, (c0 + cn) // 128))
            for kb in range(8):
                mm(PS[pb][0:32, :cn], wlr[:, kb, :], hxT[:, kb, c0:c0 + cn], kb == 0, kb == 7,
                   [bwlr] + [bhx[t] for t in tl], [bPS[pb]])
            act(lrT[0:32, c0:c0 + cn], PS[pb][0:32, :cn], AF.Identity, [bPS[pb], bblr], [blrT], bias=blr[:])

        if stop == 'B0':
            raise _Stop(nc, P)
        P.tag = "L%d.B1" % l
        WB = 2
        qk_sb = [alloc("qk_sb%d" % i, [128, 512], F32) for i in range(WB)]
        v_sb = [alloc("v_sb%d" % i, [128, 512], BF16) for i in range(WB)]
        la_e = [alloc("la_e%d" % i, [128, 512], F32) for i in range(WB)]
        la_sb = [alloc("la_sb%d" % i, [128, 512], F32) for i in range(WB)]
        qT_sb = [alloc("qT_sb%d" % i, [128, 4, 128], BF16) for i in range(WB)]
        nq_o = [alloc("nq_o%d" % i, [128, 512], BF16) for i in range(WB)]
        nk_o = [alloc("nk_o%d" % i, [128, 128], BF16) for i in range(WB)]
        cs_t = [alloc("cs_t%d" % i, [128, 2, 64], F32) for i in range(WB)]
        bqk_sb, bv_sb, bla_e, bla_sb, bqT_sb, bnq_o, bnk_o, bcs = [[Buf() for _ in range(WB)] for _ in range(8)]
        nscr = {}
        for i in range(WB):
            for nm, wdt in (("q", 512), ("k", 128)):
                nscr[(nm, i)] = ([alloc("ns%s%d_%d" % (nm, i, z), [128, wdt], F32) for z in range(4)] + [alloc("nss%s%d" % (nm, i), [128, 8], F32)],
                                 [Buf() for _ in range(5)])

        def norm_rope(src, bsrc, nh, G, boff, rope_i, out, bout, scr, perm=False):
            n = nh * 64
            tl_, bl_ = nscr[scr]
            b_, sq_, a_, r_ = [x[:, :n] for x in tl_[:4]]
            ss_ = tl_[4][:, :nh]
            Bb, Bsq, Ba, Br, Bss = bl_

            def v3(ap):
                return ap.rearrange("p (h d) -> p h d", h=nh)

            if perm:
                def vin(ap):
                    return ap.rearrange("p (a j d) -> p a j d", a=2, j=4)
                vout = out.rearrange("p (j a d) -> p a j d", a=2, j=4)
                rsb = ss_.rearrange("p (a j) -> p a j", a=2).unsqueeze(3).broadcast_to([128, 2, 4, 64])
            else:
                vin = v3
                vout = v3(out)
                rsb = ss_.unsqueeze(2).broadcast_to([128, nh, 64])

            tt("dve", b_, src, bias_tm[:, boff:boff + n], ALU.add, [bsrc, bbtm], [Bb])
            yield
            act(sq_, b_, AF.Square, [Bb], [Bsq])
            yield
            P.add("dve", lambda e: e.tensor_reduce(out=ss_, in_=v3(sq_), axis=AX.X, op=ALU.add), r=[Bsq], w=[Bss])
            yield
            act(ss_, ss_, AF.Ln, [Bss, bCONST], [Bss], bias=eps_t[:], scale=1.0 / 64)
            yield
            act(ss_, ss_, AF.Exp, [Bss], [Bss], scale=-0.5)
            yield
            tt("dve", v3(b_), v3(b_), G[:].unsqueeze(1).broadcast_to([128, nh, 64]), ALU.mult, [Bb, bG], [Bb])
            yield
            if rope_i is None:
                tt("dve", vout, vin(b_), rsb, ALU.mult, [Bb, Bss], [bout])
                yield
                return
            tt("dve", v3(b_), v3(b_), ss_.unsqueeze(2).broadcast_to([128, nh, 64]), ALU.mult, [Bb, Bss], [Bb])
            yield
            cs = cs_t[rope_i]
            tt("dve", v3(a_), v3(b_), cs[:, 0, :].unsqueeze(1).broadcast_to([128, nh, 64]), ALU.mult, [Bb, bcs[rope_i]], [Ba])
            yield

            def v5(ap):
                return ap.rearrange("p (h a b c) -> p h a b c", h=nh, a=2, b=2)

            sn5 = cs[:, 1, :].rearrange("p (a b c) -> p a b c", a=2, b=2)
            for hf in range(2):
                tt("dve", v5(r_)[:, :, :, hf, :], v5(b_)[:, :, :, 1 - hf, :],
                   sn5[:, :, hf, :].unsqueeze(1).broadcast_to([128, nh, 2, 16]), ALU.mult, [Bb, bcs[rope_i]], [Br])
                yield
            tt("dve", vout, vin(a_), vin(r_), ALU.add, [Ba, Br], [bout])
            yield

        PT6 = PS[6].bitcast(BF16)

        def genB1(t, i):
            bA, bB, bC, bD = 4 * i, 4 * i + 1, 4 * i + 2, 4 * i + 3
            PTk = PS[bB].bitcast(BF16)
            PTq = PS[bC].bitcast(BF16).rearrange("p (a b) -> p a b", a=8)
            tc_ = slice(t * 128, (t + 1) * 128)
            is_x = t >= TCT
            if is_x:
                xr = (t - TCT) * 128
                P.dma("sp", cs_t[i][:, 0, :], cos_d[xr:xr + 128, :], w=[bcs[i]])
                P.dma("sp", cs_t[i][:, 1, :], sin_d[xr:xr + 128, :], w=[bcs[i]])
            for kb in range(8):
                mm(PS[bA], hxT[:, kb, tc_], w_tm[:, kb, 0:512], kb == 0, kb == 7, [bhx[t], bwtm], [bPS[bA]])
            yield
            tt("dve", qk_sb[i][:], PS[bA], bias_tm[:, 0:512], ALU.add, [bPS[bA], bbtm], [bqk_sb[i]])
            P.dma("sp", GQK[tc_, :], qk_sb[i][:], r=[bqk_sb[i]], w=[bGQK])
            yield
            for kb in range(8):
                mm(PS[bB], hxT[:, kb, tc_], w_tm[:, kb, 512:1024], kb == 0, kb == 7, [bhx[t], bwtm], [bPS[bB]])
            yield
            tt("dve", v_sb[i][:], PS[bB], bias_tm[:, 512:1024], ALU.add, [bPS[bB], bbtm], [bv_sb[i]])
            P.dma("sp", GV[tc_, :], v_sb[i][:], r=[bv_sb[i]], w=[bGV])
            yield
            mm(PS[bA], lrT[0:33, tc_], wup[0:33, :], True, True, [blrT, bwup], [bPS[bA]])
            yield
            act(la_e[i][:], PS[bA], AF.Exp, [bPS[bA]], [bla_e[i]], scale=-1.0)
            yield
            for kb in range(8):
                mm(PS[bD][:, 0:256], hxT[:, kb, tc_], w_tm[:, kb, 1536:1792], kb == 0, kb == 7, [bhx[t], bwtm], [bPS[bD]])
            yield
            tt("dve", Vext[:, t, :, 0:64], PS[bD][:, 128:256].rearrange("p (h d) -> p h d", h=2),
               bias_tm[:, 1664:1792].rearrange("p (h d) -> p h d", h=2), ALU.add, [bPS[bD], bbtm], [bV[t]])
            yield
            act(la_sb[i][:], la_e[i][:], AF.Ln, [bla_e[i]], [bla_sb[i]], bias=1.0)
            P.dma("sp", GZ[tc_, :], la_sb[i][:], r=[bla_sb[i]], w=[bGZ])
            yield
            do_q = is_x or need_ctx
            if do_q:
                for kb in range(8):
                    mm(PS[bC], hxT[:, kb, tc_], w_tm[:, kb, 1024:1536], kb == 0, kb == 7, [bhx[t], bwtm], [bPS[bC]])
                yield
            gk = norm_rope(PS[bD][:, 0:128], bPS[bD], 2, Gk, 1536, i if is_x else None, nk_o[i][:], bnk_o[i], ("k", i))
            gq = norm_rope(PS[bC], bPS[bC], 8, Gq, 1024, i if is_x else None, nq_o[i][:], bnq_o[i], ("q", i), perm=True) if do_q else iter(())
            alive = [gk, gq]
            while alive:
                for g in list(alive):
                    try:
                        next(g)
                    except StopIteration:
                        alive.remove(g)
                yield
            tr(PTk[:, 0:128], nk_o[i][:], ident_bf[:], [bnk_o[i], bCONST], [bPS[bB]])
            yield
            cp("act", KT[:, tc_], PTk[:, 0:128], [bPS[bB]], [bKT[t]])
            yield
            if do_q:
                for j in range(4):
                    tr(PTq[:, j, :], nq_o[i][:, j * 128:(j + 1) * 128], ident_bf[:], [bnq_o[i], bCONST], [bPS[bC]])
                yield
                cp("act", qT_sb[i][:], PTq[:, 0:4, :], [bPS[bC]], [bqT_sb[i]])
                P.dma("sp", QT.rearrange("(j p) n -> p j n", p=128)[:, :, tc_], qT_sb[i][:], r=[bqT_sb[i]], w=[bQT])
                yield

        def genB1_tagged(t, i):
            for _ in genB1(t, i):
                P.tag = "L%d.B1" % l
                yield

        run_rr(NTILE, genB1_tagged, WB)

        if stop == 'B1':
            raise _Stop(nc, P)
        P.tag = "L%d.B2" % l
        P.barrier()
        state["top"] = mark_hx
        NG = 11
        grp_cols = [0, 512, 1024, 2560, 3872] + [4384 + 512 * g for g in range(6)]
        grp_rows = [None, U_, CG_, GG_, AG_] + [M13_ + 512 * g for g in range(6)]
        grp_func = [AF.Identity, AF.Sigmoid, AF.Silu, AF.Silu, AF.Silu] + [AF.Sigmoid] * 6
        bcol = alloc("bcol", [128, NG, 4], F32)
        bbcol = Buf()
        for g in range(NG):
            P.dma("sp", bcol[:, g, :], b_in[l:l + 1, grp_cols[g]:grp_cols[g] + 512].rearrange("o (nb p) -> p (o nb)", p=128),
                  w=[bbcol], allow_slow_non_contiguous=True)
        wst = [alloc("wst%d" % i, [128, 8, 512], BF16) for i in range(3)]
        bwst = [Buf() for _ in range(3)]
        stg = [alloc("stg%d" % i, [128, 512], BF16) for i in range(3)]
        bstg = [Buf() for _ in range(3)]
        tv = [alloc("tv%d" % i, [128, 512], F32) for i in range(2)]
        tg = [alloc("tg%d" % i, [128, 512], F32) for i in range(2)]
        btv, btg = [Buf(), Buf()], [Buf(), Buf()]
        kk = [0, 0]

        def proj(wbuf, bw, mb, c0, cn, pb):
            tl = [bhx[t] for t in range(c0 // 128, (c0 + cn) // 128)]
            for kb in range(8):
                mm(PS[pb][:, :cn], wbuf[:, kb, mb * 128:(mb + 1) * 128], hxT[:, kb, c0:c0 + cn], kb == 0, kb == 7,
                   [bw] + tl, [bPS[pb]])

        P.dma("pool", wst[0][:], kview(w_in[:, 0:512]), w=[bwst[0]])
        P.dma("pool", wst[1][:], kview(w_in[:, 512:1024]), w=[bwst[1]])
        for mb in range(4):
            for (c0, cn) in chunks:
                pa, pg = kk[0] % 8, (kk[0] + 1) % 8
                kk[0] += 2
                j = kk[1] % 2
                s3 = kk[1] % 3
                kk[1] += 1
                proj(wst[0], bwst[0], mb, c0, cn, pa)
                proj(wst[1], bwst[1], mb, c0, cn, pg)
                act(tv[j][:, :cn], PS[pa][:, :cn], AF.Identity, [bPS[pa], bbcol], [btv[j]], bias=bcol[:, 0, mb:mb + 1])
                act(tg[j][:, :cn], PS[pg][:, :cn], AF.Sigmoid, [bPS[pg], bbcol], [btg[j]], bias=bcol[:, 1, mb:mb + 1])
                tt("dve", stg[s3][:, :cn], tv[j][:, :cn], tg[j][:, :cn], ALU.mult, [btv[j], btg[j]], [bstg[s3]])
                P.dma("sp", PXF[U_ + mb * 128:U_ + (mb + 1) * 128, c0:c0 + cn], stg[s3][:, :cn], r=[bstg[s3]], w=[bPXF])
        for g in range(2, NG):
            wi = g % 3
            P.dma("pool", wst[wi][:], kview(w_in[:, grp_cols[g]:grp_cols[g] + 512]), w=[bwst[wi]])
            for mb in range(4):
                for (c0, cn) in chunks:
                    pa = kk[0] % 8
                    kk[0] += 1
                    s3 = kk[1] % 3
                    kk[1] += 1
                    proj(wst[wi], bwst[wi], mb, c0, cn, pa)
                    act(stg[s3][:, :cn], PS[pa][:, :cn], grp_func[g], [bPS[pa], bbcol], [bstg[s3]], bias=bcol[:, g, mb:mb + 1])
                    r0 = grp_rows[g] + mb * 128
                    P.dma("sp", PXF[r0:r0 + 128, c0:c0 + cn], stg[s3][:, :cn], r=[bstg[s3]], w=[bPXF])

        if stop == 'B2':
            raise _Stop(nc, P)
        P.tag = "L%d.G" % l
        P.barrier()
        arena_reset()
        WG = 2
        Sst = alloc("Sst", [128, 2, 128], F32)
        Sbf = [alloc("Sbf%d" % i, [128, 2, 128], BF16) for i in range(2)]
        bS, bSbf = Buf(), [Buf(), Buf()]
        g_la = [alloc("g_la%d" % i, [128, 256], F32) for i in range(WG)]
        g_qk = [alloc("g_qk%d" % i, [128, 512], F32) for i in range(WG)]
        g_v = [alloc("g_v%d" % i, [128, 512], BF16) for i in range(WG)]
        g_e13 = [alloc("g_e13%d" % i, [128, 512], F32) for i in range(WG)]
        g_e2 = [alloc("g_e2%d" % i, [128, 256], F32) for i in range(WG)]
        g_qin = [alloc("g_qin%d" % i, [128, 256], BF16) for i in range(WG)]
        g_kin = [alloc("g_kin%d" % i, [128, 256], BF16) for i in range(WG)]
        g_kst = [alloc("g_kst%d" % i, [128, 2, 2, 128], BF16) for i in range(WG)]
        g_qpad = [alloc("g_qpad%d" % i, [128, 2, 2, 128], BF16) for i in range(WG)]
        g_kT = [alloc("g_kT%d" % i, [128, 2, 128], BF16) for i in range(WG)]
        g_dec = [alloc("g_dec%d" % i, [128, 4], F32) for i in range(WG)]
        g_att = [alloc("g_att%d" % i, [128, 4, 128], BF16) for i in range(WG)]
        g_o = [alloc("g_o%d" % i, [128, 4, 128], F32) for i in range(WG)]
        (bg_la, bg_qk, bg_v, bg_e13, bg_e2, bg_qin, bg_kin, bg_kst, bg_qpad, bg_kT, bg_dec, bg_att, bg_o) = [[Buf() for _ in range(WG)] for _ in range(13)]
        for i in range(WG):
            memset("pool", g_qpad[i][:], 0.0, [bg_qpad[i]])
            memset("pool", g_kst[i][:], 0.0, [bg_kst[i]])

        def prepG(t, i, dr):
            mcum, mrest, matt = masks[:, 2 * dr, :], masks[:, 2 * dr + 1, :], masks[:, 4 + dr, :]
            tc_ = slice(t * 128, (t + 1) * 128)
            pc, pa, pcn = i, 2 + i, 6 + i
            PTc = PS[pc].bitcast(BF16).rearrange("p (a b) -> p a b", a=8)
            P.dma("sp", g_la[i][:], GZ[tc_, dr * 256:(dr + 1) * 256], r=[bGZ], w=[bg_la[i]])
            P.dma("sp", g_qk[i][:], GQK[tc_, :], r=[bGQK], w=[bg_qk[i]])
            P.dma("sp", g_v[i][:], GV[tc_, :], r=[bGV], w=[bg_v[i]])
            yield
            mm(PS[pc][:, 0:256], mcum, g_la[i][:], True, True, [bCONST, bg_la[i]], [bPS[pc]])
            mm(PS[pc][:, 256:512], mrest, g_la[i][:], True, True, [bCONST, bg_la[i]], [bPS[pc]])
            yield
            act(g_e13[i][:], PS[pc], AF.Exp, [bPS[pc]], [bg_e13[i]])
            yield
            act(g_e2[i][:], PS[pc][:, 0:256], AF.Exp, [bPS[pc]], [bg_e2[i]], scale=-1.0)
            yield
            stt(g_qin[i][:], g_qk[i][:, 0:256], 0.125, g_e13[i][:, 0:256], ALU.mult, ALU.mult, [bg_qk[i], bg_e13[i]], [bg_qin[i]])
            yield
            tt("dve", g_kin[i][:], g_qk[i][:, 256:512], g_e2[i][:], ALU.mult, [bg_qk[i], bg_e2[i]], [bg_kin[i]])
            yield
            for hh in range(2):
                tt("dve", g_kst[i][:, :, hh, hh * 64:(hh + 1) * 64],
                   g_qk[i][:, 256:512].rearrange("p (b h d) -> p b h d", b=2, h=2)[:, :, hh, :],
                   g_e13[i][:, 256:512].rearrange("p (b h d) -> p b h d", b=2, h=2)[:, :, hh, :], ALU.mult,
                   [bg_qk[i], bg_e13[i]], [bg_kst[i]])
                yield
            for b2 in range(2):
                tr(PTc[:, b2, :], g_qin[i][:, b2 * 128:(b2 + 1) * 128], ident_bf[:], [bg_qin[i], bCONST], [bPS[pc]])
            for b2 in range(2):
                tr(PTc[:, 2 + b2, :], g_kin[i][:, b2 * 128:(b2 + 1) * 128], ident_bf[:], [bg_kin[i], bCONST], [bPS[pc]])
            for b2 in range(2):
                mm(PS[pc][:, 256 + 2 * b2:256 + 2 * b2 + 2], g_la[i][:, b2 * 128:(b2 + 1) * 128], neg16[:], True, True, [bg_la[i], bCONST], [bPS[pc]])
            yield
            cp("act", g_qpad[i][0:64, :, 0, :], PTc[0:64, 0:2, :], [bPS[pc]], [bg_qpad[i]])
            yield
            cp("act", g_qpad[i][64:128, :, 1, :], PTc[64:128, 0:2, :], [bPS[pc]], [bg_qpad[i]])
            yield
            cp("act", g_kT[i][:], PTc[:, 2:4, :], [bPS[pc]], [bg_kT[i]])
            yield
            act(g_dec[i][:], PS[pc][:, 256:260], AF.Exp, [bPS[pc]], [bg_dec[i]])
            yield
            for h in range(4):
                b2 = h // 2
                mm(PS[pa][:, h * 128:(h + 1) * 128], g_kT[i][:, b2, :], g_qpad[i][:, b2, h % 2, :], True, True,
                   [bg_kT[i], bg_qpad[i]], [bPS[pa]])
            yield
            tt("dve", g_att[i][:], PS[pa].rearrange("p (h t) -> p h t", h=4), matt.unsqueeze(1).broadcast_to([128, 4, 128]),
               ALU.mult, [bPS[pa], bCONST], [bg_att[i]])
            yield
            for b2 in range(2):
                for hh in range(2):
                    h = 2 * b2 + hh
                    mm(PS[pcn][:, b2 * 128:(b2 + 1) * 128], g_kst[i][:, b2, hh, :], g_v[i][:, h * 128:(h + 1) * 128],
                       hh == 0, hh == 1, [bg_kst[i], bg_v[i]], [bPS[pcn]])
            yield

        def stateG(t, i, dr, seq):
            tc_ = slice(t * 128, (t + 1) * 128)
            po_, pcn = 4 + i, 6 + i
            sprev, snew = Sbf[(seq + 1) % 2], Sbf[seq % 2]
            bprev, bnew = bSbf[(seq + 1) % 2], bSbf[seq % 2]
            for h in range(4):
                b2 = h // 2
                mm(PS[po_][:, h * 128:(h + 1) * 128], g_v[i][:, h * 128:(h + 1) * 128], g_att[i][:, h, :], True, False,
                   [bg_v[i], bg_att[i]], [bPS[po_]])
                mm(PS[po_][:, h * 128:(h + 1) * 128], sprev[:, b2, :], g_qpad[i][:, b2, h % 2, :], False, True,
                   [bprev, bg_qpad[i]], [bPS[po_]])
            cp("act", g_o[i][:], PS[po_].rearrange("p (h t) -> p h t", h=4), [bPS[po_]], [bg_o[i]])
            P.dma("sp", OBF[1 - dr].rearrange("(h p) n -> p h n", p=128)[:, :, tc_], g_o[i][:], r=[bg_o[i]], w=[bOBF[1 - dr]])
            for b2 in range(2):
                stt(Sst[:, b2, :], Sst[:, b2, :], g_dec[i][:, 2 * b2:2 * b2 + 1], PS[pcn][:, b2 * 128:(b2 + 1) * 128],
                    ALU.mult, ALU.add, [bS, bg_dec[i], bPS[pcn]], [bS])
            cp("pool", snew[:], Sst[:], [bS], [bnew])

        for dr in (1, 0):
            order = (list(range(TCT)) + list(range(TCT, NTILE))) if dr == 0 else (list(range(TCT - 1, -1, -1)) + list(range(NTILE - 1, TCT - 1, -1)))
            memset("dve", Sst[:], 0.0, [bS])
            memset("pool", Sbf[0][:], 0.0, [bSbf[0]])
            memset("pool", Sbf[1][:], 0.0, [bSbf[1]])
            free = list(range(WG))
            active = []
            done = {}
            nxt = 0
            nstate = 0
            while nstate < len(order):
                while nxt < len(order) and free:
                    slot = free.pop(0)
                    active.append([prepG(order[nxt], slot, dr), slot, nxt])
                    nxt += 1
                for item in list(active):
                    try:
                        next(item[0])
                    except StopIteration:
                        active.remove(item)
                        done[item[2]] = item[1]
                while nstate in done:
                    slot = done.pop(nstate)
                    stateG(order[nstate], slot, dr, nstate)
                    free.append(slot)
                    nstate += 1

        if stop == 'G':
            raise _Stop(nc, P)
        P.tag = "L%d.Cs" % l
        P.barrier()
        arena_reset()
        wco = alloc("wco", [128, 4, D], BF16)
        wgo = alloc("wgo", [128, 4, D], BF16)
        wao = alloc("wao", [64, 8, D], BF16)
        wo = alloc("wo", [128, 8, D], BF16)
        bwc = Buf()
        P.dma("pool", wco[:], kview(Wd["w_conv_out"][l]), w=[bwc])
        P.dma("pool", wgo[:], kview(Wd["w_gla_out"][l]), w=[bwc])
        P.dma("pool", wao[:], Wd["w_attn_out"][l].rearrange("(h p) n -> p h n", p=64), w=[bwc])
        P.dma("pool", wo[:], kview(Wd["w_out"][l]), w=[bwc])
        gateC = alloc("gateC", [128, D], F32)
        bgateC = Buf()
        dwT = alloc("dwT", [32, 512], F32)
        dww = alloc("dww", [128, 4, 32], F32)
        cpar = alloc("cpar", [128, 3, 4], F32)
        gng = alloc("gng", [128, 1], F32)
        diag = alloc("diag", [128, 4, 31, 128], BF16)
        bdwT, bdww, bcpar, bdiag = Buf(), Buf(), Buf(), Buf()
        memset("dve", dwT[:], 0.0, [bdwT])
        P.dma("sp", dwT[0:31, :], Wd["conv_dw_w"][l], w=[bdwT])
        for cb in range(4):
            tr(PS[0][:, cb * 32:cb * 32 + 32], dwT[0:32, cb * 128:(cb + 1) * 128], ident_f[0:32, 0:32], [bdwT, bCONST], [bPS[0]])
        cp("dve", dww[:, :, 0:31], PS[0][:, 0:128].rearrange("p (c j) -> p c j", c=4)[:, :, 0:31], [bPS[0]], [bdww])
        for pi, nm in enumerate(("conv_dw_b", "conv_ln_g", "conv_ln_b")):
            P.dma("sp", cpar[:, pi, :], Wd[nm][l:l + 1, :].rearrange("o (cb p) -> p (o cb)", p=128), w=[bcpar], allow_slow_non_contiguous=True)
        P.dma("sp", gng[:], Wd["gla_norm_g"][l:l + 1, :].rearrange("o n -> n o"), w=[bcpar], allow_slow_non_contiguous=True)
        for cb in range(4):
            tt("dve", diag[:, cb, :, :], ident_f[:].unsqueeze(1).broadcast_to([128, 31, 128]),
               dww[:, cb, 0:31].unsqueeze(2).broadcast_to([128, 31, 128]), ALU.mult, [bCONST, bdww], [bdiag])

        qT = [alloc("c_qT%d" % i, [128, 512], BF16) for i in range(2)]
        bqT = [Buf(), Buf()]
        Ee = [alloc("c_E%d" % i, [128, 2, 512], BF16) for i in range(3)]
        bE = [Buf() for _ in range(3)]
        attg = [alloc("c_attg%d" % i, [64, 2, 512], BF16) for i in range(3)]
        OAT = alloc("c_OAT", [64, 8, 512], BF16)
        rinv = [alloc("c_rinv%d" % i, [65, 512], F32) for i in range(2)]
        og = alloc("c_og", [64, 512], F32)
        battg, bOAT, brinv, bog = [Buf(), Buf(), Buf()], Buf(), [Buf(), Buf()], Buf()
        uh = alloc("c_uh", [128, 4, 512 + 30], BF16)
        cgt = alloc("c_cg", [128, 4, 512], BF16)
        cv = alloc("c_cv", [128, 4, 512], F32)
        csq = alloc("c_csq", [128, 512], F32)
        csqb = alloc("c_csqb", [128, 512], BF16)
        bcsqb = Buf()
        mean = alloc("c_mean", [128, 512], F32)
        msq = alloc("c_msq", [128, 512], F32)
        rstd = alloc("c_rstd", [128, 512], F32)
        yt = alloc("c_yt", [128, 512], F32)
        yt2 = alloc("c_yt2", [128, 512], F32)
        uo = alloc("c_uo", [128, 4, 512], BF16)
        buh, bcgt, bcv, bcsq, bmean, bmsq, brstd, byt, byt2, buo = [Buf() for _ in range(10)]
        ob = [alloc("c_ob0", [128, 512], F32)] * 2
        of_ = [alloc("c_of0", [128, 512], F32)] * 2
        bob, bof = [Buf()] * 2, [Buf()] * 2
        gsum = mean
        ggt = [alloc("c_gg%d" % i, [128, 512], BF16) for i in range(2)]
        go = alloc("c_go", [128, 4, 512], BF16)
        bgsum, bggt, bgo = bmean, [Buf(), Buf()], Buf()
        mg = [alloc("c_mg%d" % i, [128, 3, 512], BF16) for i in range(2)]
        bmg = [Buf(), Buf()]
        mt1, mt2, mt3 = yt, yt2, csq
        bmt1, bmt2, bmt3 = byt, byt2, bcsq
        mT = alloc("c_mT", [128, 8, 512], BF16)
        bmT = Buf()
        xr_ = alloc("c_xr", [128, D], F32)
        xo2 = [alloc("c_xo%d" % i, [128, D], F32) for i in range(2)]
        bxr, bxo2 = Buf(), [Buf(), Buf()]

        for ci, (c0, cn) in enumerate(chunks):
            is_ctx = ci == 0
            if is_ctx and not need_ctx:
                continue
            cs_ = slice(c0, c0 + cn)
            ktiles = list(range(TCT)) if is_ctx else list(range(NTILE))
            if ci <= 1:
                s_ = 1 if is_ctx else 0
                P.dma("sp", gateC[:], MOD[l, s_:s_ + 1, 2 * D:3 * D].partition_broadcast(128), r=[bMOD], w=[bgateC])
            seq0, seq1 = (0, TC) if is_ctx else (TC, NT)
            nkt = len(ktiles)

            def att_gen():
                P_tag = "L%d.Catt" % l
                its = [(j, kt_i, kt) for j in range(4) for kt_i, kt in enumerate(ktiles)]
                n = len(its)
                LA = 1

                def loads(j):
                    qi = j % 2
                    P.dma("sp", qT[qi][:, :cn], QT[j * 128:(j + 1) * 128, cs_], r=[bQT], w=[bqT[qi]])
                    for hh in range(2):
                        r0 = AG_ + (j + 4 * hh) * 64
                        P.dma("sp", attg[j % 3][:, hh, :cn], PXF[r0:r0 + 64, cs_], r=[bPXF], w=[battg[j % 3]])

                for idx in range(n + LA):
                    P.tag = P_tag
                    if idx < n:
                        j, kt_i, kt = its[idx]
                        qi, st_, eb_ = j % 2, idx % 2, idx % 3
                        if kt_i == 0:
                            if j == 0:
                                loads(0)
                            if j + 1 < 4:
                                loads(j + 1)
                        for hh in range(2):
                            p0 = 64 * hh
                            mm(PS[2 * st_ + hh][:, :cn], KT[p0:p0 + 64, kt * 128:(kt + 1) * 128], qT[qi][p0:p0 + 64, :cn], True, True,
                               [bKT[kt], bqT[qi]], [bPS[2 * st_ + hh]])
                        src2 = PSALL[:, st_ * 1024:(st_ + 1) * 1024].rearrange("p (h n) -> p h n", h=2)[:, :, :cn]
                        act(Ee[eb_][:, :, :cn], src2, AF.Exp, [bPS[2 * st_], bPS[2 * st_ + 1]], [bE[eb_]], scale=0.125)
                    if idx >= LA:
                        j, kt_i, kt = its[idx - LA]
                        qi, eb_ = j % 2, (idx - LA) % 3
                        for hh in range(2):
                            mm(PS[4 + hh][0:65, :cn], Vext[:, kt, hh, :], Ee[eb_][:, hh, :cn], kt_i == 0, kt_i == nkt - 1,
                               [bV[kt], bE[eb_]], [bPS[4 + hh]])
                        if kt_i == nkt - 1:
                            for hh in range(2):
                                h8 = j + 4 * hh
                                recip(rinv[hh][64:65, :cn], PS[4 + hh][64:65, :cn], [bPS[4 + hh]], [brinv[hh]])
                                mm(PS[6][0:64, :cn], ones_f[64:65, 0:64], rinv[hh][64:65, :cn], True, True, [bCONST, brinv[hh]], [bPS[6]])
                                tt("dve", og[:, :cn], PS[4 + hh][0:64, :cn], attg[j % 3][:, hh, :cn], ALU.mult, [bPS[4 + hh], battg[j % 3]], [bog])
                                tt("dve", OAT[:, h8, :cn], og[:, :cn], PS[6][0:64, :cn], ALU.mult, [bog, bPS[6]], [bOAT])
                    yield

            def side_gen():
                P.tag = "L%d.Cconv" % l
                lo, hi = max(c0 - 15, seq0), min(c0 + cn + 15, seq1)
                if lo > c0 - 15:
                    memset("pool", uh[:, :, 0:15], 0.0, [buh])
                if hi < c0 + cn + 15:
                    memset("pool", uh[:, :, cn + 15:cn + 30], 0.0, [buh])
                P.dma("sp", uh[:, :, lo - (c0 - 15):hi - (c0 - 15)], PXF[U_:U_ + 512, lo:hi].rearrange("(cb p) n -> p cb n", p=128),
                      r=[bPXF], w=[buh])
                P.dma("sp", cgt[:, :, :cn], PXF[CG_:CG_ + 512, cs_].rearrange("(cb p) n -> p cb n", p=128), r=[bPXF], w=[bcgt])
                yield
                for cb in range(4):
                    P.tag = "L%d.Cconv" % l
                    for j in range(31):
                        mm(PS[7][:, :cn], diag[:, cb, j, :], uh[:, cb, j:j + cn], j == 0, j == 30, [bdiag, buh], [bPS[7]])
                    yield
                    P.tag = "L%d.Cconv" % l
                    act(cv[:, cb, :cn], PS[7][:, :cn], AF.Identity, [bPS[7], bcpar], [bcv], bias=cpar[:, 0, cb:cb + 1])
                    yield
                P.tag = "L%d.Cconv" % l
                for cb in range(4):
                    mm(PS[7][:, :cn], ones_f[:], cv[:, cb, :cn], cb == 0, cb == 3, [bCONST, bcv], [bPS[7]])
                yield
                P.tag = "L%d.Cconv" % l
                act(mean[:, :cn], PS[7][:, :cn], AF.Copy, [bPS[7]], [bmean], scale=1.0 / 512)
                tt("dve", msq[:, :cn], mean[:, :cn], mean[:, :cn], ALU.mult, [bmean], [bmsq])
                yield
                for cb in range(4):
                    P.tag = "L%d.Cconv" % l
                    act(csqb[:, :cn], cv[:, cb, :cn], AF.Square, [bcv], [bcsqb])
                    yield
                    P.tag = "L%d.Cconv" % l
                    mm(PS[7][:, :cn], ones_bf[:], csqb[:, :cn], cb == 0, cb == 3, [bCONST, bcsqb], [bPS[7]])
                    yield
                P.tag = "L%d.Cconv" % l
                stt(rstd[:, :cn], PS[7][:, :cn], 1.0 / 512, msq[:, :cn], ALU.mult, ALU.subtract, [bPS[7], bmsq], [brstd])
                yield
                P.tag = "L%d.Cconv" % l
                act(rstd[:, :cn], rstd[:, :cn], AF.Ln, [brstd, bCONST], [brstd], bias=eps_t[:])
                yield
                P.tag = "L%d.Cconv" % l
                act(rstd[:, :cn], rstd[:, :cn], AF.Exp, [brstd], [brstd], scale=-0.5)
                yield
                for cb in range(4):
                    P.tag = "L%d.Cconv" % l
                    tt("dve", yt[:, :cn], cv[:, cb, :cn], mean[:, :cn], ALU.subtract, [bcv, bmean], [byt])
                    yield
                    P.tag = "L%d.Cconv" % l
                    tt("dve", yt2[:, :cn], yt[:, :cn], rstd[:, :cn], ALU.mult, [byt, brstd], [byt2])
                    yield
                    P.tag = "L%d.Cconv" % l
                    ts("dve", yt[:, :cn], yt2[:, :cn], cpar[:, 1, cb:cb + 1], cpar[:, 2, cb:cb + 1], ALU.mult, ALU.add, [byt2, bcpar], [byt])
                    yield
                    P.tag = "L%d.Cconv" % l
                    act(yt2[:, :cn], yt[:, :cn], AF.Exp, [byt], [byt2], scale=-1.0)
                    yield
                    P.tag = "L%d.Cconv" % l
                    ts("pool", yt2[:, :cn], yt2[:, :cn], 1.0, 1.0, ALU.add, ALU.mult, [byt2], [byt2])
                    yield
                    P.tag = "L%d.Cconv" % l
                    recip(yt2[:, :cn], yt2[:, :cn], [byt2], [byt2])
                    yield
                    P.tag = "L%d.Cconv" % l
                    tt("dve", yt[:, :cn], yt[:, :cn], yt2[:, :cn], ALU.mult, [byt, byt2], [byt])
                    yield
                    P.tag = "L%d.Cconv" % l
                    tt("pool", uo[:, cb, :cn], yt[:, :cn], cgt[:, cb, :cn], ALU.mult, [byt, bcgt], [buo])
                    yield
                for h in range(4):
                    i = h % 2
                    P.tag = "L%d.Cgla" % l
                    P.dma("sp", ggt[i][:, :cn], PXF[GG_ + h * 128:GG_ + (h + 1) * 128, cs_], r=[bPXF], w=[bggt[i]])
                    P.dma("sp", ob[i][:, :cn], OBF[0][h * 128:(h + 1) * 128, cs_], r=[bOBF[0]], w=[bob[i]])
                    P.dma("sp", of_[i][:, :cn], OBF[1][h * 128:(h + 1) * 128, cs_], r=[bOBF[1]], w=[bof[i]])
                    yield
                    P.tag = "L%d.Cgla" % l
                    tt("pool", gsum[:, :cn], ob[i][:, :cn], of_[i][:, :cn], ALU.add, [bob[i], bof[i]], [bgsum])
                    yield
                    P.tag = "L%d.Cgla" % l
                    act(csqb[:, :cn], gsum[:, :cn], AF.Square, [bgsum], [bcsqb])
                    yield
                    P.tag = "L%d.Cgla" % l
                    mm(PS[7][:, :cn], ones_bf[:], csqb[:, :cn], True, True, [bCONST, bcsqb], [bPS[7]])
                    yield
                    P.tag = "L%d.Cgla" % l
                    act(msq[:, :cn], PS[7][:, :cn], AF.Ln, [bPS[7], bCONST], [bmsq], bias=eps_t[:], scale=1.0 / 128)
                    yield
                    P.tag = "L%d.Cgla" % l
                    act(msq[:, :cn], msq[:, :cn], AF.Exp, [bmsq], [bmsq], scale=-0.5)
                    yield
                    P.tag = "L%d.Cgla" % l
                    tt("dve", yt[:, :cn], gsum[:, :cn], msq[:, :cn], ALU.mult, [bgsum, bmsq], [byt])
                    yield
                    P.tag = "L%d.Cgla" % l
                    stt(go[:, h, :cn], yt[:, :cn], gng[:], ggt[i][:, :cn], ALU.mult, ALU.mult, [byt, bcpar, bggt[i]], [bgo])
                    yield

            gA, gB = att_gen(), side_gen()
            nA = 4 * nkt + 1
            nB = 90
            acc = 0.0
            doneB = False
            for _ in gA:
                acc += float(nB) / nA
                while acc >= 1.0 and not doneB:
                    acc -= 1.0
                    try:
                        next(gB)
                    except StopIteration:
                        doneB = True
            if not doneB:
                for _ in gB:
                    pass
            P.tag = "L%d.Cmrg" % l
            for fb in range(8):
                i = fb % 2
                fs = slice(fb * 128, (fb + 1) * 128)
                P.dma("sp", mg[i][:, :, :cn], PXF[M13_:M13_ + 3 * D, cs_].rearrange("(g f p) n -> p g f n", g=3, f=8, p=128)[:, :, fb, :],
                      r=[bPXF], w=[bmg[i]])
                for cb in range(4):
                    mm(PS[0][:, :cn], wco[:, cb, fs], uo[:, cb, :cn], cb == 0, cb == 3, [bwc, buo], [bPS[0]])
                for h in range(4):
                    mm(PS[1][:, :cn], wgo[:, h, fs], go[:, h, :cn], h == 0, h == 3, [bwc, bgo], [bPS[1]])
                for h8 in range(8):
                    mm(PS[2][:, :cn], wao[:, h8, fs], OAT[:, h8, :cn], h8 == 0, h8 == 7, [bwc, bOAT], [bPS[2]])
                tt("dve", mt1[:, :cn], PS[0][:, :cn], mg[i][:, 0, :cn], ALU.mult, [bPS[0], bmg[i]], [bmt1])
                tt("dve", mt2[:, :cn], PS[1][:, :cn], mg[i][:, 1, :cn], ALU.mult, [bPS[1], bmg[i]], [bmt2])
                tt("dve", mt3[:, :cn], PS[2][:, :cn], mg[i][:, 2, :cn], ALU.mult, [bPS[2], bmg[i]], [bmt3])
                tt("pool", mt1[:, :cn], mt1[:, :cn], mt2[:, :cn], ALU.add, [bmt1, bmt2], [bmt1])
                tt("pool", mT[:, fb, :cn], mt1[:, :cn], mt3[:, :cn], ALU.add, [bmt1, bmt3], [bmT])
            P.tag = "L%d.Cfin" % l
            for s4 in range(cn // 128):
                t = c0 // 128 + s4
                xo_, bxo = xo2[s4 % 2], bxo2[s4 % 2]
                P.dma("sp", xr_[:], src_tile(t), r=[bXS], w=[bxr])
                for hf in range(2):
                    pb = 3 + hf
                    for fb in range(8):
                        mm(PS[pb][:], mT[:, fb, s4 * 128:(s4 + 1) * 128], wo[:, fb, hf * 512:(hf + 1) * 512], fb == 0, fb == 7,
                           [bmT, bwc], [bPS[pb]])
                    tt("dve", xo_[:, hf * 512:(hf + 1) * 512], PS[pb][:], gateC[:, hf * 512:(hf + 1) * 512],
                       ALU.mult, [bPS[pb], bgateC], [bxo])
                tt("pool", xo_[:], xo_[:], xr_[:], ALU.add, [bxo, bxr], [bxo])
                if last:
                    xrow = (t - TCT) * 128
                    final_ops.append(P.dma("sp", out_d[xrow:xrow + 128, :], xo_[:], r=[bxo], w=[bOUT]))
                else:
                    P.dma("sp", XS[t * 128:(t + 1) * 128, :], xo_[:], r=[bxo], w=[Buf()])

    P.emit(final_ops)
    return nc, P


_CACHE = {}


def kernel(**inputs):
    T, TC, DEPTH, B = 4096, 256, 4, 8
    x = np.asarray(inputs["x"], np.float32)
    B, T = x.shape[0], x.shape[1]
    TC = inputs["ctx"].shape[1]
    DEPTH = inputs["w_in"].shape[0]
    key = (T, TC, DEPTH)
    if key not in _CACHE:
        _CACHE[key] = build(T, TC, DEPTH)[0]
    nc = _CACHE[key]
    consts = host_consts(T)
    shared = {n: np.ascontiguousarray(np.asarray(inputs[n], np.float32)) for n, _ in WEIGHT_SPECS}
    shared["c_ctx"] = np.ascontiguousarray(np.asarray(inputs["c_ctx"], np.float32).reshape(1, D))
    shared.update(consts)
    in_maps = []
    for b in range(B):
        m = dict(shared)
        m["x"] = np.ascontiguousarray(x[b])
        m["ctx"] = np.ascontiguousarray(np.asarray(inputs["ctx"], np.float32)[b])
        m["c"] = np.ascontiguousarray(np.asarray(inputs["c"], np.float32)[b:b + 1])
        in_maps.append(m)
    res = run_bass_kernel_spmd(nc, in_maps, core_ids=list(range(B)))
    return np.stack([np.asarray(r["out"], np.float32) for r in res.results], axis=0)
```
